# Optimizing a Trainium2 kernel written in Bass

```python
import math
import jax, jax.numpy as jnp
from jax import lax
import numpy as np

D_MODEL = 1024
BATCH = 8
SEQ = 4096
DEPTH = 2

DN_HEADS = 8
DN_HEAD_DIM = 128
DN_WIDTH = DN_HEADS * DN_HEAD_DIM
DN_CONV = 4
DN_CHUNK = 64
DA_HEADS = 12
DA_HEAD_DIM = 64
DA_WIDTH = DA_HEADS * DA_HEAD_DIM
DA_PATTERNS = ((128, 1), (512, 4), (2048, 16))
DA_BLOCK = 128
ALIBI_MAX_EXP = 8.0
D_FF = 2816
MACARON_WEIGHT = 0.5
NORM_EPS = 1e-6
N_ADA = 9
IN_SPLITS = (3 * DN_WIDTH, DN_WIDTH, DN_HEADS, DN_HEADS,
             DA_WIDTH, DA_WIDTH, DA_WIDTH, D_MODEL, D_MODEL)
IN_COLS = 3 * DN_WIDTH + DN_WIDTH + 2 * DN_HEADS + 3 * DA_WIDTH + 2 * D_MODEL

kernel_name = "hybrid_deltanet_dilated_attn_macaron_adaln"


def rmsnorm(x, g):
    xf = x.astype(jnp.float32)
    y = xf * lax.rsqrt(jnp.mean(xf * xf, axis=-1, keepdims=True) + NORM_EPS)
    return (y * g.astype(jnp.float32)).astype(x.dtype)


def l2norm(x):
    xf = x.astype(jnp.float32)
    return xf * lax.rsqrt(jnp.sum(xf * xf, axis=-1, keepdims=True) + NORM_EPS)


def modulate(x, shift, scale):
    return x * (1.0 + scale[:, None, :]) + shift[:, None, :]


def swiglu(x, w_gate, w_up, w_down):
    return (jax.nn.silu(x @ w_gate) * (x @ w_up)) @ w_down


def causal_depthwise_conv(x, w):
    K = w.shape[0]
    S = x.shape[1]
    xp = jnp.pad(x, ((0, 0), (K - 1, 0), (0, 0)))
    y = xp[:, 0:S] * w[0]
    for j in range(1, K):
        y = y + xp[:, j:j + S] * w[j]
    return y


def gated_delta_rule(q, k, v, g, beta):
    B, S, H, dk = q.shape
    dv = v.shape[-1]
    C = DN_CHUNK
    N = S // C
    f32 = jnp.float32

    def chunks(t):
        t = t.astype(f32).reshape((B, N, C, H) + t.shape[3:])
        return t.transpose((1, 0, 3, 2) + tuple(range(4, t.ndim)))

    qc, kc, vc = chunks(q), chunks(k), chunks(v)
    gc = jnp.cumsum(chunks(g), axis=-1)
    bc = chunks(beta)
    kb = kc * bc[..., None]
    vb = vc * bc[..., None]
    incl = jnp.tril(jnp.ones((C, C), dtype=bool))
    strict = jnp.tril(jnp.ones((C, C), dtype=bool), -1)
    decay = jnp.exp(jnp.where(incl, gc[..., :, None] - gc[..., None, :], -jnp.inf))
    m = jnp.where(strict, jnp.einsum('nbhid,nbhjd->nbhij', kb, kc) * decay, 0.0)
    a = m + jnp.eye(C, dtype=f32)
    u_c = lax.linalg.triangular_solve(a, vb, left_side=True, lower=True, unit_diagonal=True)
    w_c = lax.linalg.triangular_solve(a, kb * jnp.exp(gc)[..., None], left_side=True,
                                      lower=True, unit_diagonal=True)
    qk = jnp.einsum('nbhid,nbhjd->nbhij', qc, kc) * decay

    def step(state, xs):
        q_i, k_i, u_i, w_i, g_i, qk_i = xs
        v_new = u_i - jnp.einsum('bhcd,bhde->bhce', w_i, state)
        o_i = (jnp.einsum('bhcd,bhde->bhce', q_i * jnp.exp(g_i)[..., None], state)
               + jnp.einsum('bhij,bhje->bhie', qk_i, v_new))
        g_last = g_i[..., -1]
        state = (state * jnp.exp(g_last)[..., None, None]
                 + jnp.einsum('bhcd,bhce->bhde',
                              k_i * jnp.exp(g_last[..., None] - g_i)[..., None], v_new))
        return state, o_i

    state0 = jnp.zeros((B, H, dk, dv), f32)
    _, o = lax.scan(step, state0, (qc, kc, u_c, w_c, gc, qk))
    return o.transpose(1, 0, 3, 2, 4).reshape(B, S, H, dv)


def dilated_window_branch(q, k, v, slopes, window, dilation):
    B, S, H, dh = q.shape
    r = dilation
    n = S // r
    span = window // r
    nb = -(-n // DA_BLOCK)
    n_pad = nb * DA_BLOCK
    z = B * r

    def to_sub(t):
        t = t.reshape(B, n, r, H, dh).transpose(0, 2, 1, 3, 4).reshape(z, n, H, dh)
        return jnp.pad(t, ((0, 0), (0, n_pad - n), (0, 0), (0, 0)))

    def band(t):
        tp = jnp.pad(t, ((0, 0), (DA_BLOCK, 0), (0, 0), (0, 0)))
        prev = tp[:, :n_pad].reshape(z, nb, DA_BLOCK, H, dh)
        cur = t.reshape(z, nb, DA_BLOCK, H, dh)
        return jnp.concatenate([prev, cur], axis=2)

    qs, ks, vs = to_sub(q), to_sub(k), to_sub(v)
    qb = qs.reshape(z, nb, DA_BLOCK, H, dh)
    kb, vb = band(ks), band(vs)
    s = jnp.einsum('znqhd,znkhd->znhqk', qb, kb).astype(jnp.float32) * (dh ** -0.5)
    qi = jnp.arange(DA_BLOCK)[:, None]
    ki = jnp.arange(2 * DA_BLOCK)[None, :]
    dist = qi + DA_BLOCK - ki
    key_pos = jnp.arange(nb)[:, None, None] * DA_BLOCK + ki[None] - DA_BLOCK
    valid = (dist[None] >= 0) & (dist[None] <= span) & (key_pos >= 0)
    bias = -slopes[:, None, None] * (dist * r).astype(jnp.float32)[None]
    s = jnp.where(valid[None, :, None], s + bias[None, None], -jnp.inf)
    mx = jnp.max(s, axis=-1, keepdims=True)
    p = jnp.exp(s - mx)
    l = jnp.sum(p, axis=-1, keepdims=True)
    o = jnp.einsum('znhqk,znkhd->znqhd', (p / l).astype(v.dtype), vb).astype(jnp.float32)
    lse = (mx + jnp.log(l))[..., 0]
    o = o.reshape(z, n_pad, H, dh)[:, :n].reshape(B, r, n, H, dh)
    o = o.transpose(0, 2, 1, 3, 4).reshape(B, S, H, dh)
    lse = lse.transpose(0, 1, 3, 2).reshape(z, n_pad, H)[:, :n].reshape(B, r, n, H)
    lse = lse.transpose(0, 2, 1, 3).reshape(B, S, H)
    return o, lse


def hybrid_mixer(u, w_in, conv_w, a_log, dt_bias, dn_norm, w_a, w_b, w_o):
    B, S, _ = u.shape
    proj = u @ w_in
    idx = np.cumsum(IN_SPLITS)[:-1].tolist()
    dn_qkv, dn_z, dn_b, dn_a, da_q, da_k, da_v, gate_a, gate_b = jnp.split(proj, idx, axis=-1)

    qkv = jax.nn.silu(causal_depthwise_conv(dn_qkv, conv_w))
    q, k, v = jnp.split(qkv, 3, axis=-1)
    q = l2norm(q.reshape(B, S, DN_HEADS, DN_HEAD_DIM)) * (DN_HEAD_DIM ** -0.5)
    k = l2norm(k.reshape(B, S, DN_HEADS, DN_HEAD_DIM))
    v = v.reshape(B, S, DN_HEADS, DN_HEAD_DIM)
    beta = jax.nn.sigmoid(dn_b.astype(jnp.float32))
    g = -jnp.exp(a_log.astype(jnp.float32)) * jax.nn.softplus(
        dn_a.astype(jnp.float32) + dt_bias.astype(jnp.float32))
    o_a = gated_delta_rule(q, k, v, g, beta).astype(u.dtype)
    o_a = rmsnorm(o_a, dn_norm) * jax.nn.silu(dn_z.reshape(B, S, DN_HEADS, DN_HEAD_DIM))
    y_a = o_a.reshape(B, S, DN_WIDTH) @ w_a

    qd = da_q.reshape(B, S, DA_HEADS, DA_HEAD_DIM)
    kd = da_k.reshape(B, S, DA_HEADS, DA_HEAD_DIM)
    vd = da_v.reshape(B, S, DA_HEADS, DA_HEAD_DIM)
    slopes = 2.0 ** (-ALIBI_MAX_EXP * jnp.arange(1, DA_HEADS + 1, dtype=jnp.float32) / DA_HEADS)
    outs, lses = [], []
    for window, dilation in DA_PATTERNS:
        o_p, lse_p = dilated_window_branch(qd, kd, vd, slopes, window, dilation)
        outs.append(o_p)
        lses.append(lse_p)
    wts = jax.nn.softmax(jnp.stack(lses, axis=0), axis=0)
    o_b = jnp.sum(wts[..., None] * jnp.stack(outs, axis=0), axis=0).astype(u.dtype)
    y_b = o_b.reshape(B, S, DA_WIDTH) @ w_b

    merged = jax.nn.sigmoid(gate_a) * y_a + jax.nn.sigmoid(gate_b) * y_b
    return merged @ w_o


def setup_inputs(seed: int = 0) -> dict:
    key = jax.random.key(seed)
    ks = jax.random.split(key, 24)
    f32 = jnp.float32
    D = D_MODEL

    def nrm(k, shape, scale):
        return jax.random.normal(k, shape, f32) * scale

    x = nrm(ks[0], (BATCH, SEQ, D), 1.0)
    c = nrm(ks[1], (BATCH, D), 1.0)
    ada_w = nrm(ks[2], (DEPTH, D, N_ADA * D), 0.5 * D ** -0.5)
    ada_b = nrm(ks[3], (DEPTH, N_ADA * D), 0.02)
    ln_ffn1 = 1.0 + nrm(ks[4], (DEPTH, D), 0.02)
    ln_mix = 1.0 + nrm(ks[5], (DEPTH, D), 0.02)
    ln_ffn2 = 1.0 + nrm(ks[6], (DEPTH, D), 0.02)
    ffn1_wg = nrm(ks[7], (DEPTH, D, D_FF), D ** -0.5)
    ffn1_wu = nrm(ks[8], (DEPTH, D, D_FF), D ** -0.5)
    ffn1_wd = nrm(ks[9], (DEPTH, D_FF, D), D_FF ** -0.5)
    w_in = nrm(ks[10], (DEPTH, D, IN_COLS), D ** -0.5)
    conv_w = nrm(ks[11], (DEPTH, DN_CONV, 3 * DN_WIDTH), DN_CONV ** -0.5)
    a_log = jnp.log(jax.random.uniform(ks[12], (DEPTH, DN_HEADS), f32, 1.0, 16.0))
    dt = jnp.exp(jax.random.uniform(ks[13], (DEPTH, DN_HEADS), f32,
                                    math.log(1e-3), math.log(1e-1)))
    dt_bias = dt + jnp.log(-jnp.expm1(-dt))
    dn_norm = 1.0 + nrm(ks[14], (DEPTH, DN_HEAD_DIM), 0.02)
    w_a = nrm(ks[15], (DEPTH, DN_WIDTH, D), DN_WIDTH ** -0.5)
    w_b = nrm(ks[16], (DEPTH, DA_WIDTH, D), DA_WIDTH ** -0.5)
    w_o = nrm(ks[17], (DEPTH, D, D), D ** -0.5)
    ffn2_wg = nrm(ks[18], (DEPTH, D, D_FF), D ** -0.5)
    ffn2_wu = nrm(ks[19], (DEPTH, D, D_FF), D ** -0.5)
    ffn2_wd = nrm(ks[20], (DEPTH, D_FF, D), D_FF ** -0.5)
    final_norm = 1.0 + nrm(ks[21], (D,), 0.02)
    return {"x": x, "c": c, "ada_w": ada_w, "ada_b": ada_b,
            "ln_ffn1": ln_ffn1, "ln_mix": ln_mix, "ln_ffn2": ln_ffn2,
            "ffn1_wg": ffn1_wg, "ffn1_wu": ffn1_wu, "ffn1_wd": ffn1_wd,
            "w_in": w_in, "conv_w": conv_w, "a_log": a_log, "dt_bias": dt_bias,
            "dn_norm": dn_norm, "w_a": w_a, "w_b": w_b, "w_o": w_o,
            "ffn2_wg": ffn2_wg, "ffn2_wu": ffn2_wu, "ffn2_wd": ffn2_wd,
            "final_norm": final_norm}


def reference(x, c, ada_w, ada_b, ln_ffn1, ln_mix, ln_ffn2, ffn1_wg, ffn1_wu, ffn1_wd,
              w_in, conv_w, a_log, dt_bias, dn_norm, w_a, w_b, w_o,
              ffn2_wg, ffn2_wu, ffn2_wd, final_norm):
    h = x
    c_act = jax.nn.silu(c)
    for l in range(DEPTH):
        mod = c_act @ ada_w[l] + ada_b[l]
        (sh1, sc1, gt1, sh2, sc2, gt2, sh3, sc3, gt3) = jnp.split(mod, N_ADA, axis=-1)
        f = swiglu(modulate(rmsnorm(h, ln_ffn1[l]), sh1, sc1), ffn1_wg[l], ffn1_wu[l], ffn1_wd[l])
        h = h + MACARON_WEIGHT * gt1[:, None, :] * f
        u = modulate(rmsnorm(h, ln_mix[l]), sh2, sc2)
        m = hybrid_mixer(u, w_in[l], conv_w[l], a_log[l], dt_bias[l], dn_norm[l],
                         w_a[l], w_b[l], w_o[l])
        h = h + gt2[:, None, :] * m
        f = swiglu(modulate(rmsnorm(h, ln_ffn2[l]), sh3, sc3), ffn2_wg[l], ffn2_wu[l], ffn2_wd[l])
        h = h + MACARON_WEIGHT * gt3[:, None, :] * f
    return rmsnorm(h, final_norm)
```

```python
import numpy as np
import ml_dtypes
import concourse.bass as bass
import concourse.mybir as mybir
from concourse.bass_utils import run_bass_kernel_spmd
from contextlib import ExitStack

F32 = mybir.dt.float32
BF16 = mybir.dt.bfloat16
AF = mybir.ActivationFunctionType
ALU = mybir.AluOpType

D = 1024
S = 4096
KC = 8
DFF = 2816
NFF = 22
DEPTH = 2
INC = 8464
O_Z, O_B, O_A, O_DQ, O_DK, O_DV, O_GA, O_GB = 3072, 4096, 4104, 4112, 4880, 5648, 6416, 7440
EPS = 1e-6
NEG = -1.0e9
BIGD = 1.0e5
SLOPES = [2.0 ** (-8.0 * (h + 1) / 12.0) for h in range(12)]


ENGS = ("pe", "act", "dve", "pool", "sp")
ENGOBJ = {"pe": "tensor", "act": "scalar", "dve": "vector", "pool": "gpsimd", "sp": "sync"}
EPOCH = 12000
NDMASEM = 24


class Sems:
    def __init__(self, nc, stack):
        self.nc, self.stack = nc, stack
        self.sems = {}
        self.cnt = {e: 0 for e in ENGS}
        self.ndma = 0
        self.lastdma = {}

    def get(self, key):
        if key not in self.sems:
            self.sems[key] = self.stack.enter_context(self.nc.semaphore("s_%s_%s" % key))
        return self.sems[key]


class Prog:
    def __init__(self, nc, sems):
        self.nc, self.S = nc, sems
        self.ops = []
        self.last_w = {}
        self.readers = {}

    def op(self, eng, fn, r=(), w=(), dma=False):
        i = len(self.ops)
        deps = set()
        for b in r:
            lw = self.last_w.get(b)
            if lw is not None:
                deps.add(lw)
        for b in w:
            lw = self.last_w.get(b)
            if lw is not None:
                deps.add(lw)
            for x in self.readers.get(b, ()):
                deps.add(x)
        deps.discard(i)
        for b in r:
            self.readers.setdefault(b, []).append(i)
        for b in w:
            self.last_w[b] = i
            self.readers[b] = []
        self.ops.append(dict(eng=eng, fn=fn, deps=deps, dma=dma))
        return i

    def emit(self):
        nc, S, ops = self.nc, self.S, self.ops
        need = [False] * len(ops)
        for i, o in enumerate(ops):
            if o["dma"]:
                need[i] = True
            for d in o["deps"]:
                od = ops[d]
                if od["dma"] or o["dma"] or od["eng"] != o["eng"]:
                    need[d] = True
        for i, o in enumerate(ops):
            if o["dma"]:
                k = S.ndma % NDMASEM
                o["sem"] = S.get(("dma", k))
                o["val"] = 16 * (S.ndma // NDMASEM + 1)
                o["prev"] = S.lastdma.get(k)
                S.lastdma[k] = (o["sem"], o["val"])
                S.ndma += 1
            elif need[i]:
                e = o["eng"]
                o["sem"] = S.get((e, S.cnt[e] // EPOCH))
                o["val"] = S.cnt[e] % EPOCH + 1
                S.cnt[e] += 1
        per = {e: [] for e in ENGS}
        for i, o in enumerate(ops):
            per[o["eng"]].append(i)
        final_dma = list(S.lastdma.values())
        with nc.Block() as block:
            for e in ENGS:
                lst = per[e]
                if not lst and e != "sp":
                    continue

                def body(eng, lst=lst, e=e):
                    waited = {}

                    def wait(sem, val):
                        k = id(sem)
                        if waited.get(k, 0) >= val:
                            return
                        eng.wait_ge(sem, val)
                        waited[k] = val

                    for i in lst:
                        o = ops[i]
                        for d in sorted(o["deps"]):
                            od = ops[d]
                            if od["dma"] or o["dma"] or od["eng"] != e:
                                wait(od["sem"], od["val"])
                        if o["dma"] and o["prev"] is not None:
                            wait(*o["prev"])
                        ins = o["fn"](eng)
                        if o["dma"]:
                            ins.then_inc(o["sem"], 16)
                        elif need[i]:
                            ins.then_inc(o["sem"], 1)
                    if e == "sp":
                        for sem, val in final_dma:
                            wait(sem, val)

                getattr(block, ENGOBJ[e])(body)


class Ctx:
    pass


def _mm(P, out, lhsT, rhs, start, stop, r, w):
    P.op("pe", lambda e: e.matmul(out, lhsT=lhsT, rhs=rhs, start=start, stop=stop), r=r, w=w)


def build(nc, debug=False, phases=None):
    g = Ctx()
    g.nc = nc
    dk = "ExternalOutput" if debug else "Internal"

    def din(name, shape, dt=F32):
        return nc.dram_tensor(name, shape, dt, kind="ExternalInput").ap()

    g.xT = din("xT", [D, S])
    g.c_col = din("c_col", [128, KC])
    g.ada_w = din("ada_w", [DEPTH, D, 9 * D])
    g.ada_bT = din("ada_bT", [128, DEPTH * 72])
    g.lnT = din("lnT", [128, DEPTH * 24 + 8])
    g.conv_wT = din("conv_wT", [128, DEPTH * 96])
    g.dnnT = din("dnnT", [128, DEPTH])
    g.alog = din("alog", [8, DEPTH])
    g.dtb = din("dtb", [8, DEPTH])
    g.w = {}
    for nm, shp in (("ffn1_wg", [DEPTH, D, DFF]), ("ffn1_wu", [DEPTH, D, DFF]), ("ffn1_wd", [DEPTH, DFF, D]),
                    ("ffn2_wg", [DEPTH, D, DFF]), ("ffn2_wu", [DEPTH, D, DFF]), ("ffn2_wd", [DEPTH, DFF, D]),
                    ("w_in", [DEPTH, D, INC]), ("w_a", [DEPTH, D, D]), ("w_b", [DEPTH, 768, D]), ("w_o", [DEPTH, D, D])):
        g.w[nm] = din(nm, shp)
    g.cf = din("cf", [128, 2176])
    g.cb = din("cb", [128, 512])
    g.outT = nc.dram_tensor("outT", [D, S], F32, kind="ExternalOutput").ap()
    g.hT = nc.dram_tensor("hT", [D, S], F32, kind=dk).ap()
    g.uT = nc.dram_tensor("uT", [D, S], BF16, kind=dk).ap()
    g.oaT = nc.dram_tensor("oaT", [D, S], BF16, kind=dk).ap()
    g.obT = nc.dram_tensor("obT", [768, S], BF16, kind=dk).ap()
    if debug:
        g.dbg = nc.dram_tensor("dbg", [128, 4096], F32, kind="ExternalOutput").ap()

    with ExitStack() as gst:
        g.sems = Sems(nc, gst)

        def gsb(name, shape, dt):
            return gst.enter_context(nc.sbuf_tensor(name, shape, dt))

        g.cf_sb = gsb("cf_sb", [128, 2176], F32)
        g.cb_sb = gsb("cb_sb", [128, 512], BF16)
        g.identf = g.cf_sb[:, 0:128]
        g.negmask = g.cf_sb[:, 128:256]
        g.strict01 = g.cf_sb[:, 256:384]
        g.D2 = g.cf_sb[:, 384:640]
        g.sel = g.cf_sb[0:8, 640:1664]
        g.resetm = g.cf_sb[0:8, 1664:2176]
        g.identb = g.cb_sb[:, 0:128]
        g.onesb = g.cb_sb[:, 128:256]
        g.headsel = g.cb_sb[:, 256:512]
        g.ccol = gsb("ccol", [128, KC], F32)
        g.cact = gsb("cact", [128, KC], BF16)
        g.adab = gsb("adab", [128, DEPTH * 72], F32)
        g.ln = gsb("ln", [128, DEPTH * 24 + 8], F32)
        g.convw = gsb("convw", [128, DEPTH * 96], F32)
        g.dnn = gsb("dnn", [128, DEPTH], F32)
        g.alog_sb = gsb("alog_sb", [8, DEPTH], F32)
        g.dtb_sb = gsb("dtb_sb", [8, DEPTH], F32)
        g.nA = gsb("nA", [8, DEPTH], F32)
        g.modT = gsb("modT", [128, 72], F32)
        g.gm = gsb("gm", [128, 24], F32)
        g.gate = gsb("gate", [128, 24], F32)
        g.zero8 = gsb("zero8", [128, 8], F32)

        plist = []
        plist.append(("setup", lambda: phase_setup(g)))
        for l in range(DEPTH):
            plist.append(("mod%d" % l, lambda l=l: phase_mod(g, l)))
            plist.append(("ffn1_%d" % l, lambda l=l: phase_ffn(g, l, 0, g.xT if l == 0 else g.hT)))
            plist.append(("u%d" % l, lambda l=l: phase_u(g, l)))
            plist.append(("dn%d" % l, lambda l=l: phase_dn(g, l)))
            plist.append(("da%d" % l, lambda l=l: phase_da(g, l)))
            plist.append(("out%d" % l, lambda l=l: phase_out(g, l)))
            plist.append(("ffn2_%d" % l, lambda l=l: phase_ffn(g, l, 2, g.hT)))
        plist.append(("final", lambda: phase_final(g)))
        for name, fn in plist:
            if phases is None or name in phases:
                fn()
    return nc


def new_phase(g):
    st = ExitStack()
    P = Prog(g.nc, g.sems)
    g.pid = getattr(g, "pid", 0) + 1
    pid = g.pid

    def sb(name, shape, dt):
        return st.enter_context(g.nc.sbuf_tensor("p%d_%s" % (pid, name), shape, dt))

    def ps(name, shape=(128, 512), dt=F32):
        return st.enter_context(g.nc.psum_tensor("p%d_%s" % (pid, name), list(shape), dt))

    return st, P, sb, ps


def phase_setup(g):
    st, P, sb, ps = new_phase(g)
    with st:
        P.op("sp", lambda e: e.dma_start(out=g.cf_sb[:], in_=g.cf), w=["cf"], dma=True)
        P.op("pool", lambda e: e.dma_start(out=g.cb_sb[:], in_=g.cb), w=["cb"], dma=True)
        for nm, dst, src in (("ccol", g.ccol, g.c_col), ("adab", g.adab, g.ada_bT), ("ln", g.ln, g.lnT),
                             ("convw", g.convw, g.conv_wT), ("dnn", g.dnn, g.dnnT),
                             ("alog", g.alog_sb, g.alog), ("dtb", g.dtb_sb, g.dtb)):
            P.op("sp", lambda e, dst=dst, src=src: e.dma_start(out=dst[:], in_=src), w=[nm], dma=True)
        P.op("act", lambda e: e.activation(out=g.cact[:], in_=g.ccol[:], func=AF.Silu), r=["ccol"], w=["cact"])
        P.op("act", lambda e: e.activation(out=g.nA[:], in_=g.alog_sb[:], func=AF.Exp), r=["alog"], w=["nA"])
        P.op("dve", lambda e: e.tensor_scalar(out=g.nA[:], in0=g.nA[:], scalar1=-1.0, scalar2=None, op0=ALU.mult), r=["nA"], w=["nA"])
        P.op("dve", lambda e: e.memset(g.zero8[:], 0.0), w=["zero8"])
        P.emit()


def phase_mod(g, l):
    st, P, sb, ps = new_phase(g)
    NB = 8
    BW = 9 * D // NB
    with st:
        wA = [sb("wA%d" % i, [128, KC, BW], BF16) for i in range(2)]
        pm = ps("pm", (128, 72))
        src = g.ada_w[l].rearrange("(kc p) n -> p kc n", p=128)
        for blk in range(NB):
            buf = wA[blk % 2]
            for kc in range(KC):
                P.op("pool", lambda e, buf=buf, kc=kc, blk=blk: e.dma_start(out=buf[:, kc, :], in_=src[:, kc, blk * BW:(blk + 1) * BW]),
                     w=[("wA", blk % 2, kc)], dma=True)
            for j in range(BW // 128):
                col = blk * (BW // 128) + j
                for kc in range(KC):
                    _mm(P, pm[:, col:col + 1], buf[:, kc, j * 128:(j + 1) * 128], g.cact[:, kc:kc + 1], kc == 0, kc == KC - 1,
                        r=[("wA", blk % 2, kc), "cact"], w=["pm"])
        P.op("dve", lambda e: e.tensor_tensor(out=g.modT[:], in0=pm[:], in1=g.adab[:, l * 72:(l + 1) * 72], op=ALU.add),
             r=["pm"], w=["modT"])
        for v in range(3):
            sc = g.modT[:, (3 * v + 1) * 8:(3 * v + 2) * 8]
            gt = g.modT[:, (3 * v + 2) * 8:(3 * v + 3) * 8]
            lnv = g.ln[:, l * 24 + v * 8: l * 24 + v * 8 + 8]
            P.op("dve", lambda e, v=v, sc=sc, lnv=lnv: e.scalar_tensor_tensor(out=g.gm[:, v * 8:(v + 1) * 8], in0=sc, scalar=1.0, in1=lnv,
                                                                             op0=ALU.add, op1=ALU.mult), r=["modT"], w=["gm"])
            P.op("dve", lambda e, v=v, gt=gt: e.tensor_scalar(out=g.gate[:, v * 8:(v + 1) * 8], in0=gt, scalar1=(1.0 if v == 1 else 0.5),
                                                               scalar2=None, op0=ALU.mult), r=["modT"], w=["gate"])
        P.emit()


def load_w(P, dst, src, nk, tag, cols=None):
    v = src.rearrange("(kc p) n -> p kc n", p=128)
    for kc in range(nk):
        s = v[:, kc, :] if cols is None else v[:, kc, cols[0]:cols[1]]
        P.op("pool", lambda e, kc=kc, s=s: e.dma_start(out=dst[:, kc, :], in_=s), w=[(tag, kc)], dma=True)


def phase_ffn(g, l, v, h_src):
    st, P, sb, ps = new_phase(g)
    T = 256
    NT = S // T
    pre = "ffn1" if v == 0 else "ffn2"
    with st:
        wg = sb("wg", [128, KC, DFF], BF16)
        wu = sb("wu", [128, KC, DFF], BF16)
        wd = sb("wd", [128, NFF, D], BF16)
        hin = [sb("hin%d" % i, [128, KC, T], F32) for i in range(2)]
        sq = sb("sq", [128, KC, T], BF16)
        u = sb("u", [128, KC, T], BF16)
        sr = sb("sr", [128, T], F32)
        rinv = sb("rinv", [128, T], F32)
        tmp = [sb("tmp%d" % i, [128, T], F32) for i in range(2)]
        sg = [sb("sg%d" % i, [128, T], F32) for i in range(2)]
        aT = sb("aT", [128, NFF, T], BF16)
        ssum = ps("ssum")
        pgu = [ps("pgu%d" % i) for i in range(2)]
        pd = [ps("pd%d" % i) for i in range(2)]
        load_w(P, wg, g.w[pre + "_wg"][l], KC, "wg")
        load_w(P, wu, g.w[pre + "_wu"][l], KC, "wu")
        load_w(P, wd, g.w[pre + "_wd"][l], NFF, "wd")
        gm_cols = g.gm[:, v * 8:(v + 1) * 8]
        sh_cols = g.modT[:, (3 * v) * 8:(3 * v + 1) * 8]
        gate_cols = g.gate[:, v * 8:(v + 1) * 8]
        hsv = h_src.rearrange("(kc p) t -> p kc t", p=128)
        hdv = g.hT.rearrange("(kc p) t -> p kc t", p=128)
        import os
        NT = int(os.environ.get("FFN_NT", NT))
        for t in range(NT):
            hb = hin[t % 2]
            tag = "f%d" % (t % 2)
            P.op("sp", lambda e, hb=hb, t=t: e.dma_start(out=hb[:], in_=hsv[:, :, t * T:(t + 1) * T]), w=[tag + "h"], dma=True)
            emit_norm(g, P, hb, T, sq, ssum, sr, rinv, tmp, gm_cols, sh_cols, u, tag)
            for j in range(NFF):
                pb = pgu[j % 2]
                for kc in range(KC):
                    _mm(P, pb[:, 0:T], wg[:, kc, j * 128:(j + 1) * 128], u[:, kc, :], kc == 0, kc == KC - 1,
                        r=[("wg", kc), tag + "u"], w=[("pgu", j % 2)])
                for kc in range(KC):
                    _mm(P, pb[:, T:2 * T], wu[:, kc, j * 128:(j + 1) * 128], u[:, kc, :], kc == 0, kc == KC - 1,
                        r=[("wu", kc), tag + "u"], w=[("pgu", j % 2)])
                sgb = sg[j % 2]
                P.op("act", lambda e, pb=pb, sgb=sgb: e.activation(out=sgb[:], in_=pb[:, 0:T], func=AF.Silu),
                     w=[("pgu", j % 2), ("sg", j % 2)])
                P.op("dve", lambda e, pb=pb, sgb=sgb, j=j: e.tensor_tensor(out=aT[:, j, :], in0=pb[:, T:2 * T], in1=sgb[:], op=ALU.mult),
                     r=[("sg", j % 2)], w=[("pgu", j % 2), ("aT", j)])
            for m in range(KC):
                pb = pd[m % 2]
                for j in range(NFF):
                    _mm(P, pb[:, 0:T], wd[:, j, m * 128:(m + 1) * 128], aT[:, j, :], j == 0, j == NFF - 1,
                        r=[("wd", j), ("aT", j)], w=[("pd", m % 2)])
                P.op("dve", lambda e, pb=pb, m=m, hb=hb: e.scalar_tensor_tensor(out=hb[:, m, :], in0=pb[:, 0:T], scalar=gate_cols[:, m:m + 1],
                                                                               in1=hb[:, m, :], op0=ALU.mult, op1=ALU.add),
                     w=[("pd", m % 2), tag + "h"])
            P.op("sp", lambda e, hb=hb, t=t: e.dma_start(out=hdv[:, :, t * T:(t + 1) * T], in_=hb[:]), r=[tag + "h"], w=["hT_dram"], dma=True)
        P.emit()


def phase_u(g, l):
    st, P, sb, ps = new_phase(g)
    T = 512
    NT = S // T
    with st:
        hin = [sb("hin%d" % i, [128, KC, T], F32) for i in range(2)]
        sq = sb("sq", [128, KC, T], BF16)
        u = [sb("u%d" % i, [128, KC, T], BF16) for i in range(2)]
        sr = sb("sr", [128, T], F32)
        rinv = sb("rinv", [128, T], F32)
        tmp = [sb("tmp%d" % i, [128, T], F32) for i in range(2)]
        ssum = ps("ssum")
        gm_cols = g.gm[:, 8:16]
        sh_cols = g.modT[:, 24:32]
        hsv = g.hT.rearrange("(kc p) t -> p kc t", p=128)
        udv = g.uT.rearrange("(kc p) t -> p kc t", p=128)
        for t in range(NT):
            hb = hin[t % 2]
            ub = u[t % 2]
            tag = "n%d" % (t % 2)
            P.op("sp", lambda e, hb=hb, t=t: e.dma_start(out=hb[:], in_=hsv[:, :, t * T:(t + 1) * T]), w=[tag + "h"], dma=True)
            emit_norm(g, P, hb, T, sq, ssum, sr, rinv, tmp, gm_cols, sh_cols, ub, tag)
            P.op("sp", lambda e, ub=ub, t=t: e.dma_start(out=udv[:, :, t * T:(t + 1) * T], in_=ub[:]), r=[tag + "u"], dma=True)
        P.emit()


def phase_final(g):
    st, P, sb, ps = new_phase(g)
    T = 512
    NT = S // T
    with st:
        hin = [sb("hin%d" % i, [128, KC, T], F32) for i in range(2)]
        sq = sb("sq", [128, KC, T], BF16)
        o = [sb("o%d" % i, [128, KC, T], F32) for i in range(2)]
        sr = sb("sr", [128, T], F32)
        rinv = sb("rinv", [128, T], F32)
        ssum = ps("ssum")
        gm_cols = g.ln[:, DEPTH * 24:DEPTH * 24 + 8]
        hsv = g.hT.rearrange("(kc p) t -> p kc t", p=128)
        odv = g.outT.rearrange("(kc p) t -> p kc t", p=128)
        for t in range(NT):
            hb = hin[t % 2]
            ob = o[t % 2]
            tag = "n%d" % (t % 2)
            P.op("sp", lambda e, hb=hb, t=t: e.dma_start(out=hb[:], in_=hsv[:, :, t * T:(t + 1) * T]), w=[tag + "h"], dma=True)
            emit_norm(g, P, hb, T, sq, ssum, sr, rinv, None, gm_cols, None, ob, tag, out_f32=True)
            P.op("sp", lambda e, ob=ob, t=t: e.dma_start(out=odv[:, :, t * T:(t + 1) * T], in_=ob[:]), r=[tag + "u"], dma=True)
        P.emit()


def emit_norm(g, P, hin, T, sq, ssum, sr, rinv, tmp, gm_cols, sh_cols, out_tile, tag, out_f32=False):
    hflat = hin[:].rearrange("p a b -> p (a b)")
    sqflat = sq[:].rearrange("p a b -> p (a b)")
    P.op("act", lambda e: e.activation(out=sqflat, in_=hflat, func=AF.Square), r=[tag + "h"], w=["n_sq"])
    for kc in range(KC):
        _mm(P, ssum[:, 0:T], g.onesb, sq[:, kc, :], kc == 0, kc == KC - 1, r=["n_sq"], w=["n_ssum"])
    P.op("act", lambda e: e.activation(out=sr[:, 0:T], in_=ssum[:, 0:T], func=AF.Sqrt, scale=1.0 / D, bias=EPS),
         w=["n_ssum", "n_sr"])
    P.op("dve", lambda e: e.reciprocal(out=rinv[:, 0:T], in_=sr[:, 0:T]), r=["n_sr"], w=["n_rinv"])
    for kc in range(KC):
        if out_f32:
            P.op("dve", lambda e, kc=kc: e.scalar_tensor_tensor(out=out_tile[:, kc, :], in0=hin[:, kc, :], scalar=gm_cols[:, kc:kc + 1],
                                                               in1=rinv[:, 0:T], op0=ALU.mult, op1=ALU.mult),
                 r=[tag + "h", "n_rinv"], w=[tag + "u"])
        else:
            tb = tmp[kc % 2]
            P.op("dve", lambda e, kc=kc, tb=tb: e.scalar_tensor_tensor(out=tb[:, 0:T], in0=hin[:, kc, :], scalar=gm_cols[:, kc:kc + 1],
                                                                      in1=rinv[:, 0:T], op0=ALU.mult, op1=ALU.mult),
                 r=[tag + "h", "n_rinv"], w=[("n_tmp", kc % 2)])
            P.op("act", lambda e, kc=kc, tb=tb: e.activation(out=out_tile[:, kc, :], in_=tb[:, 0:T], func=AF.Identity,
                                                             bias=sh_cols[:, kc:kc + 1]),
                 r=[("n_tmp", kc % 2)], w=[tag + "u"])


AX = mybir.AxisListType


def phase_out(g, l):
    st, P, sb, ps = new_phase(g)
    T = 256
    NT = S // T
    with st:
        wa = sb("wa", [128, 8, D], BF16)
        wb = sb("wb", [128, 6, D], BF16)
        wo = sb("wo", [128, 8, D], BF16)
        wga = sb("wga", [128, 8, D], BF16)
        wgb = sb("wgb", [128, 8, D], BF16)
        load_w(P, wa, g.w["w_a"][l], 8, "wa")
        load_w(P, wb, g.w["w_b"][l], 6, "wb")
        load_w(P, wo, g.w["w_o"][l], 8, "wo")
        load_w(P, wga, g.w["w_in"][l], 8, "wga", cols=(O_GA, O_GA + D))
        load_w(P, wgb, g.w["w_in"][l], 8, "wgb", cols=(O_GB, O_GB + D))
        ut = [sb("ut%d" % i, [128, 8, T], BF16) for i in range(2)]
        oa = [sb("oa%d" % i, [128, 8, T], BF16) for i in range(2)]
        ob = [sb("ob%d" % i, [128, 6, T], BF16) for i in range(2)]
        hb_ = [sb("hb%d" % i, [128, 8, T], F32) for i in range(2)]
        mg = sb("mg", [128, 8, T], BF16)
        sgab = [sb("sgab%d" % i, [128, 2 * T], F32) for i in range(2)]
        t12 = [sb("t12%d" % i, [128, 2 * T], F32) for i in range(2)]
        bA = [ps("bA%d" % i) for i in range(2)]
        bB = [ps("bB%d" % i) for i in range(2)]
        bC = [ps("bC%d" % i) for i in range(2)]
        gate_cols = g.gate[:, 8:16]
        uv = g.uT.rearrange("(kc p) t -> p kc t", p=128)
        oav = g.oaT.rearrange("(kc p) t -> p kc t", p=128)
        obv = g.obT.rearrange("(kc p) t -> p kc t", p=128)
        hv = g.hT.rearrange("(kc p) t -> p kc t", p=128)
        for t in range(NT):
            q = t % 2
            sl = slice(t * T, (t + 1) * T)
            P.op("sp", lambda e, q=q, sl=sl: e.dma_start(out=ut[q][:], in_=uv[:, :, sl]), w=[("ut", q)], dma=True)
            P.op("sp", lambda e, q=q, sl=sl: e.dma_start(out=oa[q][:], in_=oav[:, :, sl]), w=[("oa", q)], dma=True)
            P.op("sp", lambda e, q=q, sl=sl: e.dma_start(out=ob[q][:], in_=obv[:, :, sl]), w=[("ob", q)], dma=True)
            P.op("sp", lambda e, q=q, sl=sl: e.dma_start(out=hb_[q][:], in_=hv[:, :, sl]), w=[("hb", q)], dma=True)
            for m in range(8):
                mi = m % 2
                ms = slice(m * 128, (m + 1) * 128)
                for c in range(8):
                    _mm(P, bA[mi][:, 0:T], wa[:, c, ms], oa[q][:, c, :], c == 0, c == 7, r=[("wa", c), ("oa", q)], w=[("bA", mi)])
                for c in range(6):
                    _mm(P, bA[mi][:, T:2 * T], wb[:, c, ms], ob[q][:, c, :], c == 0, c == 5, r=[("wb", c), ("ob", q)], w=[("bA", mi)])
                for c in range(8):
                    _mm(P, bB[mi][:, 0:T], wga[:, c, ms], ut[q][:, c, :], c == 0, c == 7, r=[("wga", c), ("ut", q)], w=[("bB", mi)])
                for c in range(8):
                    _mm(P, bB[mi][:, T:2 * T], wgb[:, c, ms], ut[q][:, c, :], c == 0, c == 7, r=[("wgb", c), ("ut", q)], w=[("bB", mi)])
                P.op("act", lambda e, mi=mi: e.activation(out=sgab[mi][:], in_=bB[mi][:, :], func=AF.Sigmoid), w=[("bB", mi), ("sgab", mi)])
                P.op("dve", lambda e, mi=mi: e.tensor_tensor(out=t12[mi][:], in0=bA[mi][:, :], in1=sgab[mi][:], op=ALU.mult),
                     r=[("sgab", mi)], w=[("bA", mi), ("t12", mi)])
                P.op("pool", lambda e, mi=mi, m=m: e.tensor_tensor(out=mg[:, m, :], in0=t12[mi][:, 0:T], in1=t12[mi][:, T:2 * T], op=ALU.add),
                     r=[("t12", mi)], w=[("mg", m)])
            for m in range(8):
                mi = m % 2
                ms = slice(m * 128, (m + 1) * 128)
                for c in range(8):
                    _mm(P, bC[mi][:, 0:T], wo[:, c, ms], mg[:, c, :], c == 0, c == 7, r=[("wo", c), ("mg", c)], w=[("bC", mi)])
                P.op("dve", lambda e, mi=mi, m=m, q=q: e.scalar_tensor_tensor(out=hb_[q][:, m, :], in0=bC[mi][:, 0:T], scalar=gate_cols[:, m:m + 1],
                                                                             in1=hb_[q][:, m, :], op0=ALU.mult, op1=ALU.add),
                     w=[("bC", mi), ("hb", q)])
            P.op("sp", lambda e, q=q, sl=sl: e.dma_start(out=hv[:, :, sl], in_=hb_[q][:]), r=[("hb", q)], w=["hT_dram"], dma=True)
        P.emit()


def phase_da(g, l):
    st, P, sb, ps = new_phase(g)
    import os
    NG = int(os.environ.get("DA_NG", 6))
    with st:
        uT = sb("uT", [128, KC, S], BF16)
        udv = g.uT.rearrange("(kc p) t -> p kc t", p=128)
        for kc in range(KC):
            P.op("sp", lambda e, kc=kc: e.dma_start(out=uT[:, kc, :], in_=udv[:, kc, :]), w=[("uT", kc)], dma=True)
        wq = [sb("wq%d" % i, [128, KC, 128], BF16) for i in range(2)]
        wk = [sb("wk%d" % i, [128, KC, 128], BF16) for i in range(2)]
        wv = [sb("wv%d" % i, [128, KC, 128], BF16) for i in range(2)]
        QT = sb("QT", [128, S], BF16)
        KT = sb("KT", [128, S], BF16)
        acc = sb("acc", [128, 2, S], F32)
        sqt = sb("sqt", [128, 512], BF16)
        mx = sb("mx", [128, 4], F32)
        tm = sb("tm", [128, 1], F32)
        prod = sb("prod", [128, 2], F32)
        negm = sb("negm", [128, 2], F32)
        Vaug = [sb("Vaug%d" % i, [128, 2, 128], BF16) for i in range(2)]
        sbt = [sb("sbt%d" % i, [128, 2, 256], F32) for i in range(2)]
        PT = [sb("PT%d" % i, [128, 2, 256], BF16) for i in range(2)]
        rl = sb("rl", [128, S], F32)
        rl2 = sb("rl2", [128, S], F32)
        ob = sb("ob", [128, S], BF16)
        B = [ps("B%d" % i) for i in range(8)]
        pqk, pss = B[0], B[1]
        pv = B[0]
        pst = [[B[2], B[3]], [B[4], B[5]]]
        ppv = [B[6], B[7]]
        for i in range(2):
            P.op("pool", lambda e, i=i: e.memset(Vaug[i][:], 1.0), w=[("V", i)])
        vi = 0
        bi = 0
        for gi in range(NG):
            gq = gi % 2
            w_in = g.w["w_in"][l]
            load_w(P, wq[gq], w_in, KC, ("wq", gq), cols=(O_DQ + gi * 128, O_DQ + (gi + 1) * 128))
            load_w(P, wk[gq], w_in, KC, ("wk", gq), cols=(O_DK + gi * 128, O_DK + (gi + 1) * 128))
            load_w(P, wv[gq], w_in, KC, ("wv", gq), cols=(O_DV + gi * 128, O_DV + (gi + 1) * 128))
            P.op("dve", lambda e: e.memset(mx[:], 0.0), w=["mx"])
            for t in range(8):
                sl = slice(t * 512, (t + 1) * 512)
                for (W, wn, dst, dn, mc) in ((wq[gq], "wq", QT, "QT", 0), (wk[gq], "wk", KT, "KT", 2)):
                    for kc in range(KC):
                        _mm(P, pqk[:, :], W[:, kc, :], uT[:, kc, sl], kc == 0, kc == KC - 1, r=[((wn, gq), kc), ("uT", kc)], w=["B0"])
                    P.op("act", lambda e, dst=dst, sl=sl: e.activation(out=dst[:, sl], in_=pqk[:, :], func=AF.Copy), w=["B0", dn])
                    P.op("act", lambda e: e.activation(out=sqt[:], in_=pqk[:, :], func=AF.Square), w=["B0", "sqt"])
                    for hh in range(2):
                        _mm(P, pss[:, :], g.headsel[:, hh * 128:(hh + 1) * 128], sqt[:], True, True, r=["sqt"], w=["B1"])
                        P.op("dve", lambda e: e.reduce_max(out=tm[:, 0:1], in_=pss[:, :], axis=AX.X), w=["B1", "tm"])
                        P.op("dve", lambda e, c=mc + hh: e.tensor_tensor(out=mx[:, c:c + 1], in0=mx[:, c:c + 1], in1=tm[:, 0:1], op=ALU.max),
                             r=["tm"], w=["mx"])
            P.op("dve", lambda e: e.tensor_tensor(out=prod[:], in0=mx[:, 0:2], in1=mx[:, 2:4], op=ALU.mult), r=["mx"], w=["prod"])
            P.op("act", lambda e: e.activation(out=prod[:], in_=prod[:], func=AF.Sqrt), w=["prod"])
            P.op("dve", lambda e: e.tensor_scalar(out=negm[:], in0=prod[:], scalar1=-0.125, scalar2=None, op0=ALU.mult), r=["prod"], w=["negm"])
            for pi, r_ in enumerate((1, 4, 16)):
                nblk = 32 // r_
                for p_ in range(r_):
                    for b in range(nblk):
                        def tok(bb, p_=p_, r_=r_):
                            s0 = p_ + r_ * 128 * bb
                            return slice(s0, s0 + r_ * 127 + 1, r_)
                        cur = vi % 2
                        prv = (vi - 1) % 2
                        vi += 1
                        bq = bi % 2
                        bi += 1
                        tb = tok(b)
                        for kc in range(KC):
                            _mm(P, pv[:, 0:128], uT[:, kc, tb], wv[gq][:, kc, :], kc == 0, kc == KC - 1,
                                r=[("uT", kc), (("wv", gq), kc)], w=["B0"])
                        P.op("act", lambda e, cur=cur: e.activation(out=Vaug[cur][:, :, 64:128],
                                                                  in_=pv[:, 0:128].rearrange("p (h d) -> p h d", h=2), func=AF.Copy),
                             w=["B0", ("V", cur)])
                        lo = 128 if b == 0 else 0
                        for hh in range(2):
                            rows = slice(64 * hh, 64 * hh + 64)
                            pb = pst[bq][hh]
                            btok = "B%d" % (2 + 2 * bq + hh)
                            if b > 0:
                                _mm(P, pb[:, 0:128], KT[rows, tok(b - 1)], QT[rows, tb], True, True, r=["KT", "QT"], w=[btok])
                            _mm(P, pb[:, 128:256], KT[rows, tb], QT[rows, tb], True, True, r=["KT", "QT"], w=[btok])
                            cc = -8.0 * SLOPES[gi * 2 + hh] * r_
                            P.op("dve", lambda e, pb=pb, hh=hh, bq=bq, lo=lo, cc=cc: e.scalar_tensor_tensor(
                                out=sbt[bq][:, hh, lo:256], in0=g.D2[:, lo:256], scalar=cc, in1=pb[:, lo:256], op0=ALU.mult, op1=ALU.add),
                                w=[btok, ("sbt", bq, hh)])
                            P.op("act", lambda e, hh=hh, bq=bq, lo=lo: e.activation(out=PT[bq][:, hh, lo:256], in_=sbt[bq][:, hh, lo:256],
                                                                                 func=AF.Exp, scale=0.125, bias=negm[:, hh:hh + 1]),
                                 r=[("sbt", bq, hh), "negm"], w=[("PT", bq, hh)])
                        pp = ppv[bq]
                        ptok = "B%d" % (6 + bq)
                        for hh in range(2):
                            if b > 0:
                                _mm(P, pp[:, hh * 128:(hh + 1) * 128], Vaug[prv][:, hh, :], PT[bq][:, hh, 0:128], True, False,
                                    r=[("V", prv), ("PT", bq, hh)], w=[ptok])
                            _mm(P, pp[:, hh * 128:(hh + 1) * 128], Vaug[cur][:, hh, :], PT[bq][:, hh, 128:256], b == 0, True,
                                r=[("V", cur), ("PT", bq, hh)], w=[ptok])
                        ppv3 = pp[:, 0:256].rearrange("p (h q) -> p h q", h=2)
                        if pi == 0:
                            P.op("dve", lambda e, ppv3=ppv3, tb=tb: e.tensor_copy(out=acc[:, :, tb], in_=ppv3), w=[ptok, "acc"])
                        else:
                            P.op("dve", lambda e, ppv3=ppv3, tb=tb: e.tensor_tensor(out=acc[:, :, tb], in0=ppv3, in1=acc[:, :, tb], op=ALU.add),
                                 w=[ptok, "acc"])
            for hh in range(2):
                P.op("dve", lambda e, hh=hh: e.reciprocal(out=rl[0:64, :], in_=acc[0:64, hh, :]), r=["acc"], w=["rl"])
                P.op("act", lambda e: e.activation(out=rl2[64:128, :], in_=rl[0:64, :], func=AF.Copy), r=["rl"], w=["rl2"])
                P.op("pool", lambda e, hh=hh: e.tensor_tensor(out=ob[64:128, :], in0=acc[64:128, hh, :], in1=rl2[64:128, :], op=ALU.mult),
                     r=["acc", "rl2"], w=["ob"])
                r0 = (2 * gi + hh) * 64
                P.op("sp", lambda e, r0=r0: e.dma_start(out=g.obT[r0:r0 + 64, :], in_=ob[64:128, :]), r=["ob"], dma=True)
        P.emit()


def phase_dn(g, l):
    st, P, sb, ps = new_phase(g)
    import os
    HG = 4
    NPASS = int(os.environ.get("DN_NPASS", 2))
    NT = int(os.environ.get("DN_NT", 8))
    T = 512
    with st:
        wqkv = sb("wqkv", [128, KC, 3 * HG * 128], BF16)
        wz = sb("wz", [128, KC, HG * 128], BF16)
        wba = sb("wba", [128, KC, 16], BF16)
        ut = [sb("ut%d" % i, [128, KC, T], BF16) for i in range(2)]
        betaT = sb("betaT", [8, T], F32)
        g1 = sb("g1", [8, T], F32)
        g2 = sb("g2", [8, T], F32)
        g3 = sb("g3", [8, T], F32)
        gcT = sb("gcT", [8, T], F32)
        tk = sb("tk", [128, 4, 16], F32)
        egc = sb("egc", [128, 4, 8], F32)
        negc = sb("negc", [128, 4, 8], F32)
        bege = sb("bege", [128, 4, 8], F32)
        glb = sb("glb", [128, 4, 8], F32)
        dl = sb("dl", [128, 4, 8], F32)
        edl = sb("edl", [128, 4, 8], F32)
        egl = [sb("egl%d" % i, [128, 4, HG], F32) for i in range(2)]
        halo = sb("halo", [128, 3 * HG, 3], F32)
        xpre = [sb("xpre%d" % i, [128, T + 3], F32) for i in range(2)]
        yb = [sb("yb%d" % i, [128, T], F32) for i in range(2)]
        sbf = [sb("sbf%d" % i, [128, T], F32) for i in range(2)]
        sqh = sb("sqh", [128, T], BF16)
        ctmp = sb("ctmp", [128, T], F32)
        srn = sb("srn", [128, T], F32)
        rinvn = sb("rinvn", [128, T], F32)
        qT = [sb("qT%d" % i, [128, T], BF16) for i in range(2)]
        kT = [sb("kT%d" % i, [128, T], BF16) for i in range(2)]
        vT = [sb("vT%d" % i, [128, T], BF16) for i in range(2)]
        egcb = sb("egcb", [128, T], F32)
        tE = sb("tE", [128, 4, 128], F32)
        E4 = sb("E4", [128, 4, 128], F32)
        BBs = sb("BBs", [128, 4, 128], F32)
        EBs = sb("EBs", [128, 4, 128], F32)
        A = [sb("A%d" % i, [128, 4, 128], F32) for i in range(2)]
        Bm = [sb("Bm%d" % i, [128, 4, 128], F32) for i in range(2)]
        R = sb("R", [128, 4, 128], F32)
        kbg = sb("kbg", [128, 4, 128], F32)
        vb = sb("vb", [128, 4, 128], F32)
        wT4 = [sb("wT4%d" % i, [128, HG, T], BF16) for i in range(2)]
        u4 = [sb("u4%d" % i, [128, HG, 4, 128], F32) for i in range(2)]
        qgT = [sb("qgT%d" % i, [128, HG, T], BF16) for i in range(2)]
        qkT4 = [sb("qkT4%d" % i, [128, HG, T], BF16) for i in range(2)]
        kdec = [sb("kdec%d" % i, [128, HG, 4, 128], BF16) for i in range(2)]
        Sf = sb("Sf", [128, HG, 128], F32)
        Sb = sb("Sb", [128, HG, 128], BF16)
        vnew = [sb("vnew%d" % i, [128, HG, 128], BF16) for i in range(2)]
        oraw = sb("oraw", [128, HG, T], F32)
        sqo = sb("sqo", [128, T], BF16)
        sro = sb("sro", [128, T], F32)
        rinvo = sb("rinvo", [128, T], F32)
        sz = sb("sz", [128, T], F32)
        on = sb("on", [128, T], F32)
        oab = [sb("oab%d" % i, [128, T], BF16) for i in range(2)]
        PA = ps("PA")
        PB = ps("PB")
        PC = [ps("PC%d" % i) for i in range(2)]
        PTt = ps("PTt", (128, 1024), BF16)
        PSW = ps("PSW")
        PSO = ps("PSO")
        PSD = ps("PSD")
        pcn = [0]

        def pc():
            i = pcn[0] % 2
            pcn[0] += 1
            return PC[i], ("PC", i)

        def c4(ap):
            return ap.rearrange("p (c i) -> p c i", c=4)

        def bc4(ap2):
            return ap2.unsqueeze(1).to_broadcast([128, 4, 128])

        def colbc(ap_c):
            return ap_c.unsqueeze(2).to_broadcast([128, 4, 128])

        w_in = g.w["w_in"][l]
        udv = g.uT.rearrange("(kc p) t -> p kc t", p=128)
        nAcol = g.nA[:, l:l + 1]
        dtbcol = g.dtb_sb[:, l:l + 1]
        dnncol = g.dnn[:, l:l + 1]

        def gates(t):
            q = t % 2
            P.op("sp", lambda e: e.dma_start(out=ut[q][:], in_=udv[:, :, t * T:(t + 1) * T]), w=[("ut", q)], dma=True)
            for kc in range(KC):
                _mm(P, PA[0:8, :], wba[:, kc, 0:8], ut[q][:, kc, :], kc == 0, kc == KC - 1, r=["wba", ("ut", q)], w=["PA"])
            P.op("act", lambda e: e.activation(out=betaT[:], in_=PA[0:8, :], func=AF.Sigmoid), w=["PA", "betaT"])
            for kc in range(KC):
                _mm(P, PA[0:8, :], wba[:, kc, 8:16], ut[q][:, kc, :], kc == 0, kc == KC - 1, r=["wba", ("ut", q)], w=["PA"])
            P.op("dve", lambda e: e.tensor_scalar(out=g1[:], in0=PA[0:8, :], scalar1=dtbcol, scalar2=None, op0=ALU.add), w=["PA", "g1"])
            P.op("dve", lambda e: e.tensor_scalar(out=g2[:], in0=g1[:], scalar1=-1.0, scalar2=None, op0=ALU.mult), r=["g1"], w=["g2"])
            P.op("dve", lambda e: e.tensor_tensor(out=g2[:], in0=g2[:], in1=g1[:], op=ALU.max), r=["g1"], w=["g2"])
            P.op("act", lambda e: e.activation(out=g2[:], in_=g2[:], func=AF.Exp, scale=-1.0), w=["g2"])
            P.op("act", lambda e: e.activation(out=g2[:], in_=g2[:], func=AF.Ln, bias=1.0), w=["g2"])
            P.op("dve", lambda e: e.tensor_scalar(out=g1[:], in0=g1[:], scalar1=0.0, scalar2=None, op0=ALU.max), w=["g1"])
            P.op("dve", lambda e: e.tensor_tensor(out=g1[:], in0=g1[:], in1=g2[:], op=ALU.add), r=["g2"], w=["g1"])
            P.op("dve", lambda e: e.tensor_scalar(out=g3[:], in0=g1[:], scalar1=nAcol, scalar2=None, op0=ALU.mult), r=["g1"], w=["g3"])
            P.op("dve", lambda e: e.tensor_tensor_scan(out=gcT[:], data0=g.resetm, data1=g3[:], initial=0.0, op0=ALU.mult, op1=ALU.add),
                 r=["g3"], w=["gcT"])
            for c in range(4):
                cs = slice(c * 128, (c + 1) * 128)
                _mm(P, PB[:, c * 16:c * 16 + 8], gcT[0:8, cs], g.identf[0:8, 0:8], True, True, r=["gcT"], w=["PB"])
                _mm(P, PB[:, c * 16 + 8:c * 16 + 16], betaT[0:8, cs], g.identf[0:8, 0:8], True, True, r=["betaT"], w=["PB"])
            tkf = tk[:].rearrange("p c k -> p (c k)")
            P.op("act", lambda e: e.activation(out=tkf, in_=PB[:, 0:64], func=AF.Copy), w=["PB", "tk"])
            P.op("act", lambda e: e.activation(out=egc[:], in_=tk[:, :, 0:8], func=AF.Exp), r=["tk"], w=["egc"])
            P.op("dve", lambda e: e.tensor_scalar(out=negc[:], in0=tk[:, :, 0:8], scalar1=-1.0, scalar2=None, op0=ALU.mult), r=["tk"], w=["negc"])
            P.op("dve", lambda e: e.tensor_tensor(out=bege[:], in0=tk[:, :, 8:16], in1=egc[:], op=ALU.mult), r=["tk", "egc"], w=["bege"])

        def pre(t, hp, hl):
            h = hp * HG + hl
            q = t % 2
            ws = hl % 2
            for part in range(3):
                fcl = part * HG + hl
                fcg = part * 8 + h
                xi = part % 2
                xp = xpre[xi]
                for kc in range(KC):
                    _mm(P, PA[:, :], wqkv[:, kc, fcl * 128:(fcl + 1) * 128], ut[q][:, kc, :], kc == 0, kc == KC - 1,
                        r=[("wqkv", kc), ("ut", q)], w=["PA"])
                P.op("pool", lambda e, xp=xp, fcl=fcl: e.tensor_copy(out=xp[:, 0:3], in_=halo[:, fcl, :]), r=[("halo", fcl)], w=[("xpre", xi)])
                P.op("act", lambda e, xp=xp: e.activation(out=xp[:, 3:T + 3], in_=PA[:, :], func=AF.Copy), w=["PA", ("xpre", xi)])
                P.op("pool", lambda e, xp=xp, fcl=fcl: e.tensor_copy(out=halo[:, fcl, :], in_=xp[:, T:T + 3]), r=[("xpre", xi)], w=[("halo", fcl)])
                y = yb[xi]
                cwb = l * 96 + fcg * 4
                if part == 0:
                    P.op("dve", lambda e, xp=xp, y=y, cwb=cwb: e.tensor_scalar(out=y[:], in0=xp[:, 0:T], scalar1=g.convw[:, cwb:cwb + 1], scalar2=None,
                                                                           op0=ALU.mult), r=[("xpre", xi)], w=[("y", xi)])
                    for j in range(1, 4):
                        P.op("dve", lambda e, xp=xp, y=y, cwb=cwb, j=j: e.scalar_tensor_tensor(out=y[:], in0=xp[:, j:j + T],
                                                                                          scalar=g.convw[:, cwb + j:cwb + j + 1],
                                                                                          in1=y[:], op0=ALU.mult, op1=ALU.add),
                             r=[("xpre", xi)], w=[("y", xi)])
                else:
                    P.op("pool", lambda e, xp=xp, y=y, cwb=cwb: e.tensor_scalar(out=y[:], in0=xp[:, 0:T], scalar1=g.convw[:, cwb:cwb + 1], scalar2=None,
                                                                            op0=ALU.mult), r=[("xpre", xi)], w=[("y", xi)])
                    for j in range(1, 4):
                        P.op("pool", lambda e, xp=xp, cwb=cwb, j=j: e.tensor_scalar(out=ctmp[:], in0=xp[:, j:j + T],
                                                                                scalar1=g.convw[:, cwb + j:cwb + j + 1], scalar2=None,
                                                                                op0=ALU.mult), r=[("xpre", xi)], w=["ctmp"])
                        P.op("pool", lambda e, y=y: e.tensor_tensor(out=y[:], in0=y[:], in1=ctmp[:], op=ALU.add), r=["ctmp"], w=[("y", xi)])
                if part == 2:
                    P.op("act", lambda e, y=y: e.activation(out=vT[ws][:], in_=y[:], func=AF.Silu), r=[("y", xi)], w=[("vT", ws)])
                else:
                    s_ = sbf[xi]
                    dst = qT[ws] if part == 0 else kT[ws]
                    dn_ = ("qT", ws) if part == 0 else ("kT", ws)
                    P.op("act", lambda e, y=y, s_=s_: e.activation(out=s_[:], in_=y[:], func=AF.Silu), r=[("y", xi)], w=[("sbf", xi)])
                    P.op("act", lambda e, s_=s_: e.activation(out=sqh[:], in_=s_[:], func=AF.Square), r=[("sbf", xi)], w=["sqh"])
                    _mm(P, PB[:, :], g.onesb, sqh[:], True, True, r=["sqh"], w=["PB"])
                    P.op("act", lambda e: e.activation(out=srn[:], in_=PB[:, :], func=AF.Sqrt, bias=EPS), w=["PB", "srn"])
                    P.op("dve", lambda e: e.reciprocal(out=rinvn[:], in_=srn[:]), r=["srn"], w=["rinvn"])
                    scl = (128.0 ** -0.5) if part == 0 else 1.0
                    P.op("dve", lambda e, s_=s_, dst=dst, scl=scl: e.scalar_tensor_tensor(out=dst[:], in0=s_[:], scalar=scl, in1=rinvn[:],
                                                                                         op0=ALU.mult, op1=ALU.mult),
                         r=[("sbf", xi), "rinvn"], w=[dn_])
            selh = g.sel[:, h * 128:(h + 1) * 128]
            _mm(P, PB[:, :], selh, gcT[:], True, True, r=["gcT"], w=["PB"])
            P.op("act", lambda e: e.activation(out=glb[:, :, h], in_=PB[:, 127:512:128], func=AF.Copy), w=["PB", ("glb", h)])
            P.op("act", lambda e: e.activation(out=egcb[:], in_=PB[:, :], func=AF.Exp), w=["PB", "egcb"])
            P.op("dve", lambda e: e.tensor_tensor(out=tE[:], in0=c4(PB[:, :]), in1=bc4(g.negmask), op=ALU.add), w=["PB", "tE"])
            P.op("pool", lambda e: e.tensor_tensor(out=qgT[q][:, hl, :], in0=qT[ws][:], in1=egcb[:], op=ALU.mult),
                 r=[("qT", ws), "egcb"], w=[("qg", q, hl)])
            P.op("dve", lambda e: e.tensor_tensor(out=dl[:, :, h], in0=glb[:, :, h], in1=tk[:, :, h], op=ALU.subtract),
                 r=[("glb", h), "tk"], w=[("dl", h)])
            P.op("act", lambda e: e.activation(out=edl[:, :, h], in_=dl[:, :, h], func=AF.Exp), r=[("dl", h)], w=[("edl", h)])
            P.op("act", lambda e: e.activation(out=egl[q][:, :, hl], in_=glb[:, :, h], func=AF.Exp), r=[("glb", h)], w=[("egl", q, hl)])
            for c in range(4):
                P.op("act", lambda e, c=c: e.activation(out=E4[:, c, :], in_=tE[:, c, :], func=AF.Exp, bias=negc[:, c, h:h + 1]),
                     r=["tE", "negc"], w=["E4"])
            _mm(P, PB[:, :], selh, betaT[:], True, True, r=["betaT"], w=["PB"])
            P.op("dve", lambda e: e.tensor_tensor(out=BBs[:], in0=c4(PB[:, :]), in1=bc4(g.strict01), op=ALU.mult), w=["PB", "BBs"])
            P.op("pool", lambda e: e.tensor_tensor(out=EBs[:], in0=E4[:], in1=BBs[:], op=ALU.mult), r=["E4", "BBs"], w=["EBs"])
            for c in range(4):
                cs = slice(c * 128, (c + 1) * 128)
                P.op("pe", lambda e, cs=cs: e.transpose(out=PTt[:, cs], in_=kT[ws][:, cs], identity=g.identb), r=[("kT", ws)], w=["PT"])
            for c in range(4):
                cs = slice(c * 128, (c + 1) * 128)
                P.op("pe", lambda e, cs=cs, c=c: e.transpose(out=PTt[:, 512 + c * 128:512 + (c + 1) * 128], in_=vT[ws][:, cs], identity=g.identb),
                     r=[("vT", ws)], w=["PT"])
            P.op("dve", lambda e: e.tensor_tensor(out=kbg[:], in0=c4(PTt[:, 0:512]), in1=colbc(bege[:, :, h]), op=ALU.mult),
                 r=["bege"], w=["PT", "kbg"])
            P.op("dve", lambda e: e.tensor_tensor(out=kdec[q][:, hl, :, :], in0=c4(PTt[:, 0:512]), in1=colbc(edl[:, :, h]), op=ALU.mult),
                 r=[("edl", h)], w=["PT", ("kdec", q, hl)])
            P.op("dve", lambda e: e.tensor_tensor(out=vb[:], in0=c4(PTt[:, 512:1024]), in1=colbc(tk[:, :, 8 + h]), op=ALU.mult),
                 r=["tk"], w=["PT", "vb"])
            pkk, tkk = pc()
            for c in range(4):
                cs = slice(c * 128, (c + 1) * 128)
                _mm(P, pkk[:, cs], kT[ws][:, cs], kT[ws][:, cs], True, True, r=[("kT", ws)], w=[tkk])
            pqk, tqk = pc()
            for c in range(4):
                cs = slice(c * 128, (c + 1) * 128)
                _mm(P, pqk[:, cs], kT[ws][:, cs], qT[ws][:, cs], True, True, r=[("kT", ws), ("qT", ws)], w=[tqk])
            P.op("dve", lambda e: e.tensor_tensor(out=A[0][:], in0=c4(pkk[:, :]), in1=EBs[:], op=ALU.mult), r=["EBs"], w=[tkk, ("A", 0)])
            P.op("dve", lambda e: e.tensor_tensor(out=c4(qkT4[q][:, hl, :]), in0=c4(pqk[:, :]), in1=E4[:], op=ALU.mult),
                 r=["E4"], w=[tqk, ("qk", q, hl)])
            pt0, tt0 = pc()
            for c in range(4):
                cs = slice(c * 128, (c + 1) * 128)
                _mm(P, pt0[:, cs], A[0][:, c, :], g.identf, True, True, r=[("A", 0)], w=[tt0])
            P.op("act", lambda e, pt0=pt0: e.activation(out=Bm[0][:], in_=c4(pt0[:, :]), func=AF.Copy), w=[tt0, ("B", 0)])
            P.op("dve", lambda e: e.scalar_tensor_tensor(out=R[:], in0=A[0][:], scalar=-1.0, in1=bc4(g.identf), op0=ALU.mult, op1=ALU.add),
                 r=[("A", 0)], w=["R"])
            for k in range(1, 7):
                ap_, bp_ = (k - 1) % 2, (k - 1) % 2
                an_, bn_ = k % 2, k % 2
                if k <= 5:
                    px, tx = pc()
                    for c in range(4):
                        cs = slice(c * 128, (c + 1) * 128)
                        _mm(P, px[:, cs], Bm[bp_][:, c, :], A[ap_][:, c, :], True, True, r=[("B", bp_), ("A", ap_)], w=[tx])
                py, ty = pc()
                for c in range(4):
                    cs = slice(c * 128, (c + 1) * 128)
                    _mm(P, py[:, cs], A[ap_][:, c, :], Bm[bp_][:, c, :], True, True, r=[("B", bp_), ("A", ap_)], w=[ty])
                if k <= 5:
                    P.op("act", lambda e, px=px, an_=an_: e.activation(out=A[an_][:], in_=c4(px[:, :]), func=AF.Copy), w=[tx, ("A", an_)])
                P.op("act", lambda e, py=py, bn_=bn_: e.activation(out=Bm[bn_][:], in_=c4(py[:, :]), func=AF.Copy), w=[ty, ("B", bn_)])
                pz, tz = pc()
                for c in range(4):
                    cs = slice(c * 128, (c + 1) * 128)
                    _mm(P, pz[:, cs], Bm[bn_][:, c, :], R[:, c, :], True, True, r=[("B", bn_), "R"], w=[tz])
                P.op("dve", lambda e, pz=pz: e.tensor_tensor(out=R[:], in0=c4(pz[:, :]), in1=R[:], op=ALU.add), w=[tz, "R"])
            pw, tw = pc()
            for c in range(4):
                cs = slice(c * 128, (c + 1) * 128)
                _mm(P, pw[:, cs], kbg[:, c, :], R[:, c, :], True, True, r=["kbg", "R"], w=[tw])
            P.op("act", lambda e, pw=pw: e.activation(out=wT4[q][:, hl, :], in_=pw[:, :], func=AF.Copy), w=[tw, ("wT", q, hl)])
            pu, tu = pc()
            for c in range(4):
                cs = slice(c * 128, (c + 1) * 128)
                _mm(P, pu[:, cs], R[:, c, :], vb[:, c, :], True, True, r=["vb", "R"], w=[tu])
            P.op("act", lambda e, pu=pu: e.activation(out=u4[q][:, hl, :, :], in_=c4(pu[:, :]), func=AF.Copy), w=[tu, ("u4", q, hl)])

        def scan_step(t, hp, c):
            q = t % 2
            vq = c % 2
            cs = slice(c * 128, (c + 1) * 128)
            for hl in range(HG):
                _mm(P, PSW[:, hl * 128:(hl + 1) * 128], wT4[q][:, hl, cs], Sb[:, hl, :], True, True, r=[("wT", q, hl), "Sb"], w=["PSW"])
            P.op("dve", lambda e: e.tensor_tensor(out=vnew[vq][:], in0=u4[q][:, :, c, :], in1=PSW[:, :].rearrange("p (h e) -> p h e", h=HG),
                                                  op=ALU.subtract),
                 r=[("u4", q, hl) for hl in range(HG)], w=["PSW", ("vnew", vq)])
            for hl in range(HG):
                _mm(P, PSO[:, hl * 128:(hl + 1) * 128], Sb[:, hl, :], qgT[q][:, hl, cs], True, False, r=[("qg", q, hl), "Sb"], w=["PSO"])
                _mm(P, PSO[:, hl * 128:(hl + 1) * 128], vnew[vq][:, hl, :], qkT4[q][:, hl, cs], False, True,
                    r=[("qk", q, hl), ("vnew", vq)], w=["PSO"])
            for hl in range(HG):
                _mm(P, PSD[:, hl * 128:(hl + 1) * 128], kdec[q][:, hl, c, :], vnew[vq][:, hl, :], True, True,
                    r=[("kdec", q, hl), ("vnew", vq)], w=["PSD"])
            for hl in range(HG):
                P.op("dve", lambda e, hl=hl: e.scalar_tensor_tensor(out=Sf[:, hl, :], in0=Sf[:, hl, :], scalar=egl[q][:, c, hl:hl + 1],
                                                                   in1=PSD[:, hl * 128:(hl + 1) * 128], op0=ALU.mult, op1=ALU.add),
                     r=[("egl", q, hl)], w=["PSD", "Sf"])
            P.op("pool", lambda e: e.tensor_copy(out=Sb[:], in_=Sf[:]), r=["Sf"], w=["Sb"])
            P.op("act", lambda e: e.activation(out=oraw[:, :, cs], in_=PSO[:, :].rearrange("p (h i) -> p h i", h=HG), func=AF.Copy),
                 w=["PSO", "oraw"])

        def post(t, hp, hl):
            h = hp * HG + hl
            q = t % 2
            P.op("act", lambda e: e.activation(out=sqo[:], in_=oraw[:, hl, :], func=AF.Square), r=["oraw"], w=["sqo"])
            _mm(P, PB[:, :], g.onesb, sqo[:], True, True, r=["sqo"], w=["PB"])
            P.op("act", lambda e: e.activation(out=sro[:], in_=PB[:, :], func=AF.Sqrt, scale=1.0 / 128.0, bias=EPS), w=["PB", "sro"])
            P.op("dve", lambda e: e.reciprocal(out=rinvo[:], in_=sro[:]), r=["sro"], w=["rinvo"])
            for kc in range(KC):
                _mm(P, PA[:, :], wz[:, kc, hl * 128:(hl + 1) * 128], ut[q][:, kc, :], kc == 0, kc == KC - 1, r=[("wz", kc), ("ut", q)], w=["PA"])
            P.op("act", lambda e: e.activation(out=sz[:], in_=PA[:, :], func=AF.Silu), w=["PA", "sz"])
            P.op("pool", lambda e: e.tensor_tensor(out=on[:], in0=oraw[:, hl, :], in1=rinvo[:], op=ALU.mult), r=["oraw", "rinvo"], w=["on"])
            ob_ = oab[hl % 2]
            P.op("dve", lambda e: e.scalar_tensor_tensor(out=ob_[:], in0=on[:], scalar=dnncol, in1=sz[:], op0=ALU.mult, op1=ALU.mult),
                 r=["on", "sz"], w=[("oab", hl % 2)])
            P.op("sp", lambda e: e.dma_start(out=g.oaT[h * 128:(h + 1) * 128, t * T:(t + 1) * T], in_=ob_[:]), r=[("oab", hl % 2)], dma=True)

        load_w(P, wba, w_in, KC, "wba_", cols=(O_B, O_B + 16))
        P.op("pool", lambda e: e.memset(g1[:], 0.0), r=[("wba_", kc) for kc in range(KC)], w=["wba"])
        for hp in range(NPASS):
            v3 = w_in.rearrange("(kc p) n -> p kc n", p=128)
            for part in range(3):
                for kc in range(KC):
                    c0 = part * 1024 + hp * HG * 128
                    P.op("pool", lambda e, part=part, kc=kc, c0=c0: e.dma_start(out=wqkv[:, kc, part * HG * 128:(part + 1) * HG * 128],
                                                                              in_=v3[:, kc, c0:c0 + HG * 128]), w=[("wqkv", kc)], dma=True)
            load_w(P, wz, w_in, KC, "wz", cols=(O_Z + hp * HG * 128, O_Z + (hp + 1) * HG * 128))
            P.op("pool", lambda e: e.memset(halo[:], 0.0), w=[("halo", i) for i in range(3 * HG)])
            P.op("pool", lambda e: e.memset(Sf[:], 0.0), w=["Sf"])
            P.op("pool", lambda e: e.memset(Sb[:], 0.0), w=["Sb"])
            gates(0)
            for hl in range(HG):
                pre(0, hp, hl)
            for t in range(NT):
                if t + 1 < NT:
                    gates(t + 1)
                for c in range(4):
                    scan_step(t, hp, c)
                    if t + 1 < NT:
                        pre(t + 1, hp, c)
                for hl in range(HG):
                    post(t, hp, hl)
        P.emit()


def host_consts():
    cf = np.zeros((128, 2176), np.float32)
    j = np.arange(128)[:, None]
    i = np.arange(128)[None, :]
    cf[:, 0:128] = np.eye(128, dtype=np.float32)
    cf[:, 128:256] = np.where(j <= i, 0.0, NEG)
    cf[:, 256:384] = (j < i).astype(np.float32)
    k = j
    q = i
    prev = np.where(q <= k, 128.0 + q - k, BIGD)
    cur = np.where(q >= k, (q - k) * 1.0, BIGD)
    cf[:, 384:512] = prev
    cf[:, 512:640] = cur
    for h in range(8):
        cf[h, 640 + h * 128: 640 + (h + 1) * 128] = 1.0
    rm = np.ones((512,), np.float32)
    rm[0::128] = 0.0
    cf[0:8, 1664:2176] = rm[None, :]
    cb = np.zeros((128, 512), np.float32)
    cb[:, 0:128] = np.eye(128, dtype=np.float32)
    cb[:, 128:256] = 1.0
    cb[0:64, 256:384] = 1.0
    cb[64:128, 384:512] = 1.0
    return cf, cb


def make_in_maps(inputs, ncores=8):
    f = lambda a: np.ascontiguousarray(np.asarray(a, dtype=np.float32))
    cf, cb = host_consts()
    shared = {}
    shared["ada_w"] = f(inputs["ada_w"])
    shared["ada_bT"] = f(np.asarray(inputs["ada_b"]).reshape(DEPTH, 72, 128).transpose(2, 0, 1).reshape(128, DEPTH * 72))
    ln = np.stack([np.asarray(inputs["ln_ffn1"]), np.asarray(inputs["ln_mix"]), np.asarray(inputs["ln_ffn2"])], axis=1)
    lnT = ln.reshape(DEPTH, 3, 8, 128).transpose(3, 0, 1, 2).reshape(128, DEPTH * 24)
    fn = np.asarray(inputs["final_norm"]).reshape(8, 128).T
    shared["lnT"] = f(np.concatenate([lnT, fn], axis=1))
    cw = np.asarray(inputs["conv_w"])
    shared["conv_wT"] = f(cw.reshape(DEPTH, 4, 24, 128).transpose(3, 0, 2, 1).reshape(128, DEPTH * 96))
    shared["dnnT"] = f(np.asarray(inputs["dn_norm"]).T)
    shared["alog"] = f(np.asarray(inputs["a_log"]).T)
    shared["dtb"] = f(np.asarray(inputs["dt_bias"]).T)
    for nm in ("ffn1_wg", "ffn1_wu", "ffn1_wd", "ffn2_wg", "ffn2_wu", "ffn2_wd", "w_in", "w_a", "w_b", "w_o"):
        shared[nm] = f(inputs[nm])
    shared["cf"] = cf
    shared["cb"] = cb
    x = np.asarray(inputs["x"], dtype=np.float32)
    c = np.asarray(inputs["c"], dtype=np.float32)
    maps = []
    for b in range(ncores):
        m = dict(shared)
        m["xT"] = np.ascontiguousarray(x[b].T)
        m["c_col"] = np.ascontiguousarray(c[b].reshape(8, 128).T)
        maps.append(m)
    return maps


def kernel(**inputs):
    nc = bass.Bass("TRN2", target_bir_lowering=False)
    build(nc)
    maps = make_in_maps(inputs, 8)
    res = run_bass_kernel_spmd(nc, maps, core_ids=list(range(8)))
    out = np.stack([np.ascontiguousarray(r["outT"].T) for r in res.results], axis=0)
    return out.astype(np.float32)
```

```python
import numpy as np
import ml_dtypes
import concourse.bass as bass
import concourse.mybir as mybir
from concourse.bass_utils import run_bass_kernel_spmd
from contextlib import ExitStack

F32 = mybir.dt.float32
BF16 = mybir.dt.bfloat16
AF = mybir.ActivationFunctionType
ALU = mybir.AluOpType

D = 1024
S = 4096
KC = 8
DFF = 2816
NFF = 22
DEPTH = 2
INC = 8464
O_Z, O_B, O_A, O_DQ, O_DK, O_DV, O_GA, O_GB = 3072, 4096, 4104, 4112, 4880, 5648, 6416, 7440
EPS = 1e-6
NEG = -1.0e9
BIGD = 1.0e5
SLOPES = [2.0 ** (-8.0 * (h + 1) / 12.0) for h in range(12)]


ENGS = ("pe", "act", "dve", "pool", "sp")
ENGOBJ = {"pe": "tensor", "act": "scalar", "dve": "vector", "pool": "gpsimd", "sp": "sync"}
EPOCH = 12000
NDMASEM = 24


class Sems:
    def __init__(self, nc, stack):
        self.nc, self.stack = nc, stack
        self.sems = {}
        self.cnt = {e: 0 for e in ENGS}
        self.ndma = 0
        self.lastdma = {}

    def get(self, key):
        if key not in self.sems:
            self.sems[key] = self.stack.enter_context(self.nc.semaphore("s_%s_%s" % key))
        return self.sems[key]


class Prog:
    def __init__(self, nc, sems):
        self.nc, self.S = nc, sems
        self.ops = []
        self.last_w = {}
        self.readers = {}

    def op(self, eng, fn, r=(), w=(), dma=False):
        i = len(self.ops)
        deps = set()
        for b in r:
            lw = self.last_w.get(b)
            if lw is not None:
                deps.add(lw)
        for b in w:
            lw = self.last_w.get(b)
            if lw is not None:
                deps.add(lw)
            for x in self.readers.get(b, ()):
                deps.add(x)
        deps.discard(i)
        for b in r:
            self.readers.setdefault(b, []).append(i)
        for b in w:
            self.last_w[b] = i
            self.readers[b] = []
        self.ops.append(dict(eng=eng, fn=fn, deps=deps, dma=dma))
        return i

    def emit(self):
        nc, S, ops = self.nc, self.S, self.ops
        need = [False] * len(ops)
        for i, o in enumerate(ops):
            if o["dma"]:
                need[i] = True
            for d in o["deps"]:
                od = ops[d]
                if od["dma"] or o["dma"] or od["eng"] != o["eng"]:
                    need[d] = True
        for i, o in enumerate(ops):
            if o["dma"]:
                k = S.ndma % NDMASEM
                o["sem"] = S.get(("dma", k))
                o["val"] = 16 * (S.ndma // NDMASEM + 1)
                o["prev"] = S.lastdma.get(k)
                S.lastdma[k] = (o["sem"], o["val"])
                S.ndma += 1
            elif need[i]:
                e = o["eng"]
                o["sem"] = S.get((e, S.cnt[e] // EPOCH))
                o["val"] = S.cnt[e] % EPOCH + 1
                S.cnt[e] += 1
        per = {e: [] for e in ENGS}
        for i, o in enumerate(ops):
            per[o["eng"]].append(i)
        final_dma = list(S.lastdma.values())
        with nc.Block() as block:
            for e in ENGS:
                lst = per[e]
                if not lst and e != "sp":
                    continue

                def body(eng, lst=lst, e=e):
                    waited = {}

                    def wait(sem, val):
                        k = id(sem)
                        if waited.get(k, 0) >= val:
                            return
                        eng.wait_ge(sem, val)
                        waited[k] = val

                    for i in lst:
                        o = ops[i]
                        for d in sorted(o["deps"]):
                            od = ops[d]
                            if od["dma"] or o["dma"] or od["eng"] != e:
                                wait(od["sem"], od["val"])
                        if o["dma"] and o["prev"] is not None:
                            wait(*o["prev"])
                        ins = o["fn"](eng)
                        if o["dma"]:
                            ins.then_inc(o["sem"], 16)
                        elif need[i]:
                            ins.then_inc(o["sem"], 1)
                    if e == "sp":
                        for sem, val in final_dma:
                            wait(sem, val)

                getattr(block, ENGOBJ[e])(body)


class Ctx:
    pass


def _mm(P, out, lhsT, rhs, start, stop, r, w):
    P.op("pe", lambda e: e.matmul(out, lhsT=lhsT, rhs=rhs, start=start, stop=stop), r=r, w=w)


F32R = mybir.dt.float32r


def _mmr(P, out, lhsT, rhs, start, stop, r, w):
    import os
    P.op("pe", lambda e: e.matmul(out, lhsT=lhsT, rhs=rhs, start=start, stop=stop), r=r, w=w)


def build(nc, debug=False, phases=None):
    g = Ctx()
    g.nc = nc
    dk = "ExternalOutput" if debug else "Internal"

    def din(name, shape, dt=F32):
        return nc.dram_tensor(name, shape, dt, kind="ExternalInput").ap()

    g.xT = din("xT", [D, S])
    g.c_col = din("c_col", [128, KC])
    g.ada_w = din("ada_w", [DEPTH, D, 9 * D])
    g.ada_bT = din("ada_bT", [128, DEPTH * 72])
    g.lnT = din("lnT", [128, DEPTH * 24 + 8])
    g.conv_wT = din("conv_wT", [128, DEPTH * 96])
    g.dnnT = din("dnnT", [128, DEPTH])
    g.alog = din("alog", [8, DEPTH])
    g.dtb = din("dtb", [8, DEPTH])
    g.w = {}
    for nm, shp in (("ffn1_wg", [DEPTH, D, DFF]), ("ffn1_wu", [DEPTH, D, DFF]), ("ffn1_wd", [DEPTH, DFF, D]),
                    ("ffn2_wg", [DEPTH, D, DFF]), ("ffn2_wu", [DEPTH, D, DFF]), ("ffn2_wd", [DEPTH, DFF, D]),
                    ("w_in", [DEPTH, D, INC]), ("w_a", [DEPTH, D, D]), ("w_b", [DEPTH, 768, D]), ("w_o", [DEPTH, D, D])):
        g.w[nm] = din(nm, shp)
    g.cf = din("cf", [128, 2176])
    g.cb = din("cb", [128, 512])
    g.outT = nc.dram_tensor("outT", [D, S], F32, kind="ExternalOutput").ap()
    g.hT = nc.dram_tensor("hT", [D, S], F32, kind=dk).ap()
    g.uT = nc.dram_tensor("uT", [D, S], BF16, kind=dk).ap()
    g.oaT = nc.dram_tensor("oaT", [D, S], BF16, kind=dk).ap()
    g.obT = nc.dram_tensor("obT", [768, S], BF16, kind=dk).ap()
    if debug:
        g.dbg = nc.dram_tensor("dbg", [128, 4096], F32, kind="ExternalOutput").ap()

    with ExitStack() as gst:
        g.sems = Sems(nc, gst)

        def gsb(name, shape, dt):
            return gst.enter_context(nc.sbuf_tensor(name, shape, dt))

        g.cf_sb = gsb("cf_sb", [128, 2176], F32)
        g.cb_sb = gsb("cb_sb", [128, 512], BF16)
        g.identf = g.cf_sb[:, 0:128]
        g.negmask = g.cf_sb[:, 128:256]
        g.strict01 = g.cf_sb[:, 256:384]
        g.D2 = g.cf_sb[:, 384:640]
        g.sel = g.cf_sb[0:8, 640:1664]
        g.resetm = g.cf_sb[0:8, 1664:2176]
        g.identb = g.cb_sb[:, 0:128]
        g.onesb = g.cb_sb[:, 128:256]
        g.headsel = g.cb_sb[:, 256:512]
        g.ccol = gsb("ccol", [128, KC], F32)
        g.cact = gsb("cact", [128, KC], BF16)
        g.adab = gsb("adab", [128, DEPTH * 72], F32)
        g.ln = gsb("ln", [128, DEPTH * 24 + 8], F32)
        g.convw = gsb("convw", [128, DEPTH * 96], F32)
        g.dnn = gsb("dnn", [128, DEPTH], F32)
        g.alog_sb = gsb("alog_sb", [8, DEPTH], F32)
        g.dtb_sb = gsb("dtb_sb", [8, DEPTH], F32)
        g.nA = gsb("nA", [8, DEPTH], F32)
        g.modT = gsb("modT", [128, 72], F32)
        g.gm = gsb("gm", [128, 24], F32)
        g.gate = gsb("gate", [128, 24], F32)
        g.zero8 = gsb("zero8", [128, 8], F32)

        plist = []
        plist.append(("setup", lambda: phase_setup(g)))
        for l in range(DEPTH):
            plist.append(("mod%d" % l, lambda l=l: phase_mod(g, l)))
            plist.append(("ffn1_%d" % l, lambda l=l: phase_ffn(g, l, 0, g.xT if l == 0 else g.hT)))
            plist.append(("u%d" % l, lambda l=l: phase_u(g, l)))
            plist.append(("dn%d" % l, lambda l=l: phase_dn(g, l)))
            plist.append(("da%d" % l, lambda l=l: phase_da(g, l)))
            plist.append(("out%d" % l, lambda l=l: phase_out(g, l)))
            plist.append(("ffn2_%d" % l, lambda l=l: phase_ffn(g, l, 2, g.hT)))
        plist.append(("final", lambda: phase_final(g)))
        for name, fn in plist:
            if phases is None or name in phases:
                fn()
    return nc


def new_phase(g):
    st = ExitStack()
    P = Prog(g.nc, g.sems)
    g.pid = getattr(g, "pid", 0) + 1
    pid = g.pid

    def sb(name, shape, dt):
        return st.enter_context(g.nc.sbuf_tensor("p%d_%s" % (pid, name), shape, dt))

    def ps(name, shape=(128, 512), dt=F32):
        return st.enter_context(g.nc.psum_tensor("p%d_%s" % (pid, name), list(shape), dt))

    return st, P, sb, ps


def phase_setup(g):
    st, P, sb, ps = new_phase(g)
    with st:
        P.op("sp", lambda e: e.dma_start(out=g.cf_sb[:], in_=g.cf), w=["cf"], dma=True)
        P.op("pool", lambda e: e.dma_start(out=g.cb_sb[:], in_=g.cb), w=["cb"], dma=True)
        for nm, dst, src in (("ccol", g.ccol, g.c_col), ("adab", g.adab, g.ada_bT), ("ln", g.ln, g.lnT),
                             ("convw", g.convw, g.conv_wT), ("dnn", g.dnn, g.dnnT),
                             ("alog", g.alog_sb, g.alog), ("dtb", g.dtb_sb, g.dtb)):
            P.op("sp", lambda e, dst=dst, src=src: e.dma_start(out=dst[:], in_=src), w=[nm], dma=True)
        P.op("act", lambda e: e.activation(out=g.cact[:], in_=g.ccol[:], func=AF.Silu), r=["ccol"], w=["cact"])
        P.op("act", lambda e: e.activation(out=g.nA[:], in_=g.alog_sb[:], func=AF.Exp), r=["alog"], w=["nA"])
        P.op("dve", lambda e: e.tensor_scalar(out=g.nA[:], in0=g.nA[:], scalar1=-1.0, scalar2=None, op0=ALU.mult), r=["nA"], w=["nA"])
        P.op("dve", lambda e: e.memset(g.zero8[:], 0.0), w=["zero8"])
        P.emit()


def phase_mod(g, l):
    st, P, sb, ps = new_phase(g)
    NB = 8
    BW = 9 * D // NB
    with st:
        wA = [sb("wA%d" % i, [128, KC, BW], BF16) for i in range(2)]
        pm = ps("pm", (128, 72))
        src = g.ada_w[l].rearrange("(kc p) n -> p kc n", p=128)
        for blk in range(NB):
            buf = wA[blk % 2]
            for kc in range(KC):
                P.op("pool", lambda e, buf=buf, kc=kc, blk=blk: e.dma_start(out=buf[:, kc, :], in_=src[:, kc, blk * BW:(blk + 1) * BW]),
                     w=[("wA", blk % 2, kc)], dma=True)
            for j in range(BW // 128):
                col = blk * (BW // 128) + j
                for kc in range(KC):
                    _mm(P, pm[:, col:col + 1], buf[:, kc, j * 128:(j + 1) * 128], g.cact[:, kc:kc + 1], kc == 0, kc == KC - 1,
                        r=[("wA", blk % 2, kc), "cact"], w=["pm"])
        P.op("dve", lambda e: e.tensor_tensor(out=g.modT[:], in0=pm[:], in1=g.adab[:, l * 72:(l + 1) * 72], op=ALU.add),
             r=["pm"], w=["modT"])
        for v in range(3):
            sc = g.modT[:, (3 * v + 1) * 8:(3 * v + 2) * 8]
            gt = g.modT[:, (3 * v + 2) * 8:(3 * v + 3) * 8]
            lnv = g.ln[:, l * 24 + v * 8: l * 24 + v * 8 + 8]
            P.op("dve", lambda e, v=v, sc=sc, lnv=lnv: e.scalar_tensor_tensor(out=g.gm[:, v * 8:(v + 1) * 8], in0=sc, scalar=1.0, in1=lnv,
                                                                             op0=ALU.add, op1=ALU.mult), r=["modT"], w=["gm"])
            P.op("dve", lambda e, v=v, gt=gt: e.tensor_scalar(out=g.gate[:, v * 8:(v + 1) * 8], in0=gt, scalar1=(1.0 if v == 1 else 0.5),
                                                               scalar2=None, op0=ALU.mult), r=["modT"], w=["gate"])
        P.emit()


def load_w(P, dst, src, nk, tag, cols=None):
    v = src.rearrange("(kc p) n -> p kc n", p=128)
    for kc in range(nk):
        s = v[:, kc, :] if cols is None else v[:, kc, cols[0]:cols[1]]
        P.op("pool", lambda e, kc=kc, s=s: e.dma_start(out=dst[:, kc, :], in_=s), w=[(tag, kc)], dma=True)


def phase_ffn(g, l, v, h_src):
    st, P, sb, ps = new_phase(g)
    T = 256
    NT = S // T
    pre = "ffn1" if v == 0 else "ffn2"
    with st:
        wg = sb("wg", [128, KC, DFF], BF16)
        wu = sb("wu", [128, KC, DFF], BF16)
        wd = sb("wd", [128, NFF, D], BF16)
        hin = [sb("hin%d" % i, [128, KC, T], F32) for i in range(2)]
        sq = sb("sq", [128, KC, T], BF16)
        u = sb("u", [128, KC, T], BF16)
        sr = sb("sr", [128, T], F32)
        rinv = sb("rinv", [128, T], F32)
        tmp = [sb("tmp%d" % i, [128, T], F32) for i in range(2)]
        sg = [sb("sg%d" % i, [128, T], F32) for i in range(2)]
        aT = sb("aT", [128, NFF, T], BF16)
        ssum = ps("ssum")
        pgu = [ps("pgu%d" % i) for i in range(2)]
        pd = [ps("pd%d" % i) for i in range(2)]
        load_w(P, wg, g.w[pre + "_wg"][l], KC, "wg")
        load_w(P, wu, g.w[pre + "_wu"][l], KC, "wu")
        load_w(P, wd, g.w[pre + "_wd"][l], NFF, "wd")
        gm_cols = g.gm[:, v * 8:(v + 1) * 8]
        sh_cols = g.modT[:, (3 * v) * 8:(3 * v + 1) * 8]
        gate_cols = g.gate[:, v * 8:(v + 1) * 8]
        hsv = h_src.rearrange("(kc p) t -> p kc t", p=128)
        hdv = g.hT.rearrange("(kc p) t -> p kc t", p=128)
        import os
        NT = int(os.environ.get("FFN_NT", NT))
        for t in range(NT):
            hb = hin[t % 2]
            tag = "f%d" % (t % 2)
            P.op("sp", lambda e, hb=hb, t=t: e.dma_start(out=hb[:], in_=hsv[:, :, t * T:(t + 1) * T]), w=[tag + "h"], dma=True)
            emit_norm(g, P, hb, T, sq, ssum, sr, rinv, tmp, gm_cols, sh_cols, u, tag)
            for j in range(NFF):
                pb = pgu[j % 2]
                for kc in range(KC):
                    _mm(P, pb[:, 0:T], wg[:, kc, j * 128:(j + 1) * 128], u[:, kc, :], kc == 0, kc == KC - 1,
                        r=[("wg", kc), tag + "u"], w=[("pgu", j % 2)])
                for kc in range(KC):
                    _mm(P, pb[:, T:2 * T], wu[:, kc, j * 128:(j + 1) * 128], u[:, kc, :], kc == 0, kc == KC - 1,
                        r=[("wu", kc), tag + "u"], w=[("pgu", j % 2)])
                sgb = sg[j % 2]
                P.op("act", lambda e, pb=pb, sgb=sgb: e.activation(out=sgb[:], in_=pb[:, 0:T], func=AF.Silu),
                     w=[("pgu", j % 2), ("sg", j % 2)])
                P.op("dve", lambda e, pb=pb, sgb=sgb, j=j: e.tensor_tensor(out=aT[:, j, :], in0=pb[:, T:2 * T], in1=sgb[:], op=ALU.mult),
                     r=[("sg", j % 2)], w=[("pgu", j % 2), ("aT", j)])
            for m in range(KC):
                pb = pd[m % 2]
                for j in range(NFF):
                    _mm(P, pb[:, 0:T], wd[:, j, m * 128:(m + 1) * 128], aT[:, j, :], j == 0, j == NFF - 1,
                        r=[("wd", j), ("aT", j)], w=[("pd", m % 2)])
                P.op("dve", lambda e, pb=pb, m=m, hb=hb: e.scalar_tensor_tensor(out=hb[:, m, :], in0=pb[:, 0:T], scalar=gate_cols[:, m:m + 1],
                                                                               in1=hb[:, m, :], op0=ALU.mult, op1=ALU.add),
                     w=[("pd", m % 2), tag + "h"])
            P.op("sp", lambda e, hb=hb, t=t: e.dma_start(out=hdv[:, :, t * T:(t + 1) * T], in_=hb[:]), r=[tag + "h"], w=["hT_dram"], dma=True)
        P.emit()


def phase_u(g, l):
    st, P, sb, ps = new_phase(g)
    T = 512
    NT = S // T
    with st:
        hin = [sb("hin%d" % i, [128, KC, T], F32) for i in range(2)]
        sq = sb("sq", [128, KC, T], BF16)
        u = [sb("u%d" % i, [128, KC, T], BF16) for i in range(2)]
        sr = sb("sr", [128, T], F32)
        rinv = sb("rinv", [128, T], F32)
        tmp = [sb("tmp%d" % i, [128, T], F32) for i in range(2)]
        ssum = ps("ssum")
        gm_cols = g.gm[:, 8:16]
        sh_cols = g.modT[:, 24:32]
        hsv = g.hT.rearrange("(kc p) t -> p kc t", p=128)
        udv = g.uT.rearrange("(kc p) t -> p kc t", p=128)
        for t in range(NT):
            hb = hin[t % 2]
            ub = u[t % 2]
            tag = "n%d" % (t % 2)
            P.op("sp", lambda e, hb=hb, t=t: e.dma_start(out=hb[:], in_=hsv[:, :, t * T:(t + 1) * T]), w=[tag + "h"], dma=True)
            emit_norm(g, P, hb, T, sq, ssum, sr, rinv, tmp, gm_cols, sh_cols, ub, tag)
            P.op("sp", lambda e, ub=ub, t=t: e.dma_start(out=udv[:, :, t * T:(t + 1) * T], in_=ub[:]), r=[tag + "u"], dma=True)
        P.emit()


def phase_final(g):
    st, P, sb, ps = new_phase(g)
    T = 512
    NT = S // T
    with st:
        hin = [sb("hin%d" % i, [128, KC, T], F32) for i in range(2)]
        sq = sb("sq", [128, KC, T], BF16)
        o = [sb("o%d" % i, [128, KC, T], F32) for i in range(2)]
        sr = sb("sr", [128, T], F32)
        rinv = sb("rinv", [128, T], F32)
        ssum = ps("ssum")
        gm_cols = g.ln[:, DEPTH * 24:DEPTH * 24 + 8]
        hsv = g.hT.rearrange("(kc p) t -> p kc t", p=128)
        odv = g.outT.rearrange("(kc p) t -> p kc t", p=128)
        for t in range(NT):
            hb = hin[t % 2]
            ob = o[t % 2]
            tag = "n%d" % (t % 2)
            P.op("sp", lambda e, hb=hb, t=t: e.dma_start(out=hb[:], in_=hsv[:, :, t * T:(t + 1) * T]), w=[tag + "h"], dma=True)
            emit_norm(g, P, hb, T, sq, ssum, sr, rinv, None, gm_cols, None, ob, tag, out_f32=True)
            P.op("sp", lambda e, ob=ob, t=t: e.dma_start(out=odv[:, :, t * T:(t + 1) * T], in_=ob[:]), r=[tag + "u"], dma=True)
        P.emit()


def emit_norm(g, P, hin, T, sq, ssum, sr, rinv, tmp, gm_cols, sh_cols, out_tile, tag, out_f32=False):
    hflat = hin[:].rearrange("p a b -> p (a b)")
    sqflat = sq[:].rearrange("p a b -> p (a b)")
    P.op("act", lambda e: e.activation(out=sqflat, in_=hflat, func=AF.Square), r=[tag + "h"], w=["n_sq"])
    for kc in range(KC):
        _mm(P, ssum[:, 0:T], g.onesb, sq[:, kc, :], kc == 0, kc == KC - 1, r=["n_sq"], w=["n_ssum"])
    P.op("act", lambda e: e.activation(out=sr[:, 0:T], in_=ssum[:, 0:T], func=AF.Sqrt, scale=1.0 / D, bias=EPS),
         w=["n_ssum", "n_sr"])
    P.op("dve", lambda e: e.reciprocal(out=rinv[:, 0:T], in_=sr[:, 0:T]), r=["n_sr"], w=["n_rinv"])
    for kc in range(KC):
        if out_f32:
            P.op("dve", lambda e, kc=kc: e.scalar_tensor_tensor(out=out_tile[:, kc, :], in0=hin[:, kc, :], scalar=gm_cols[:, kc:kc + 1],
                                                               in1=rinv[:, 0:T], op0=ALU.mult, op1=ALU.mult),
                 r=[tag + "h", "n_rinv"], w=[tag + "u"])
        else:
            tb = tmp[kc % 2]
            P.op("dve", lambda e, kc=kc, tb=tb: e.scalar_tensor_tensor(out=tb[:, 0:T], in0=hin[:, kc, :], scalar=gm_cols[:, kc:kc + 1],
                                                                      in1=rinv[:, 0:T], op0=ALU.mult, op1=ALU.mult),
                 r=[tag + "h", "n_rinv"], w=[("n_tmp", kc % 2)])
            P.op("act", lambda e, kc=kc, tb=tb: e.activation(out=out_tile[:, kc, :], in_=tb[:, 0:T], func=AF.Identity,
                                                             bias=sh_cols[:, kc:kc + 1]),
                 r=[("n_tmp", kc % 2)], w=[tag + "u"])


AX = mybir.AxisListType


def phase_out(g, l):
    st, P, sb, ps = new_phase(g)
    T = 256
    NT = S // T
    with st:
        wa = sb("wa", [128, 8, D], BF16)
        wb = sb("wb", [128, 6, D], BF16)
        wo = sb("wo", [128, 8, D], BF16)
        wga = sb("wga", [128, 8, D], BF16)
        wgb = sb("wgb", [128, 8, D], BF16)
        load_w(P, wa, g.w["w_a"][l], 8, "wa")
        load_w(P, wb, g.w["w_b"][l], 6, "wb")
        load_w(P, wo, g.w["w_o"][l], 8, "wo")
        load_w(P, wga, g.w["w_in"][l], 8, "wga", cols=(O_GA, O_GA + D))
        load_w(P, wgb, g.w["w_in"][l], 8, "wgb", cols=(O_GB, O_GB + D))
        ut = [sb("ut%d" % i, [128, 8, T], BF16) for i in range(2)]
        oa = [sb("oa%d" % i, [128, 8, T], BF16) for i in range(2)]
        ob = [sb("ob%d" % i, [128, 6, T], BF16) for i in range(2)]
        hb_ = [sb("hb%d" % i, [128, 8, T], F32) for i in range(2)]
        mg = sb("mg", [128, 8, T], BF16)
        sgab = [sb("sgab%d" % i, [128, 2 * T], F32) for i in range(2)]
        t12 = [sb("t12%d" % i, [128, 2 * T], F32) for i in range(2)]
        bA = [ps("bA%d" % i) for i in range(2)]
        bB = [ps("bB%d" % i) for i in range(2)]
        bC = [ps("bC%d" % i) for i in range(2)]
        gate_cols = g.gate[:, 8:16]
        uv = g.uT.rearrange("(kc p) t -> p kc t", p=128)
        oav = g.oaT.rearrange("(kc p) t -> p kc t", p=128)
        obv = g.obT.rearrange("(kc p) t -> p kc t", p=128)
        hv = g.hT.rearrange("(kc p) t -> p kc t", p=128)
        for t in range(NT):
            q = t % 2
            sl = slice(t * T, (t + 1) * T)
            P.op("sp", lambda e, q=q, sl=sl: e.dma_start(out=ut[q][:], in_=uv[:, :, sl]), w=[("ut", q)], dma=True)
            P.op("sp", lambda e, q=q, sl=sl: e.dma_start(out=oa[q][:], in_=oav[:, :, sl]), w=[("oa", q)], dma=True)
            P.op("sp", lambda e, q=q, sl=sl: e.dma_start(out=ob[q][:], in_=obv[:, :, sl]), w=[("ob", q)], dma=True)
            P.op("sp", lambda e, q=q, sl=sl: e.dma_start(out=hb_[q][:], in_=hv[:, :, sl]), w=[("hb", q)], dma=True)
            for m in range(8):
                mi = m % 2
                ms = slice(m * 128, (m + 1) * 128)
                for c in range(8):
                    _mm(P, bA[mi][:, 0:T], wa[:, c, ms], oa[q][:, c, :], c == 0, c == 7, r=[("wa", c), ("oa", q)], w=[("bA", mi)])
                for c in range(6):
                    _mm(P, bA[mi][:, T:2 * T], wb[:, c, ms], ob[q][:, c, :], c == 0, c == 5, r=[("wb", c), ("ob", q)], w=[("bA", mi)])
                for c in range(8):
                    _mm(P, bB[mi][:, 0:T], wga[:, c, ms], ut[q][:, c, :], c == 0, c == 7, r=[("wga", c), ("ut", q)], w=[("bB", mi)])
                for c in range(8):
                    _mm(P, bB[mi][:, T:2 * T], wgb[:, c, ms], ut[q][:, c, :], c == 0, c == 7, r=[("wgb", c), ("ut", q)], w=[("bB", mi)])
                P.op("act", lambda e, mi=mi: e.activation(out=sgab[mi][:], in_=bB[mi][:, :], func=AF.Sigmoid), w=[("bB", mi), ("sgab", mi)])
                P.op("dve", lambda e, mi=mi: e.tensor_tensor(out=t12[mi][:], in0=bA[mi][:, :], in1=sgab[mi][:], op=ALU.mult),
                     r=[("sgab", mi)], w=[("bA", mi), ("t12", mi)])
                P.op("pool", lambda e, mi=mi, m=m: e.tensor_tensor(out=mg[:, m, :], in0=t12[mi][:, 0:T], in1=t12[mi][:, T:2 * T], op=ALU.add),
                     r=[("t12", mi)], w=[("mg", m)])
            for m in range(8):
                mi = m % 2
                ms = slice(m * 128, (m + 1) * 128)
                for c in range(8):
                    _mm(P, bC[mi][:, 0:T], wo[:, c, ms], mg[:, c, :], c == 0, c == 7, r=[("wo", c), ("mg", c)], w=[("bC", mi)])
                P.op("dve", lambda e, mi=mi, m=m, q=q: e.scalar_tensor_tensor(out=hb_[q][:, m, :], in0=bC[mi][:, 0:T], scalar=gate_cols[:, m:m + 1],
                                                                             in1=hb_[q][:, m, :], op0=ALU.mult, op1=ALU.add),
                     w=[("bC", mi), ("hb", q)])
            P.op("sp", lambda e, q=q, sl=sl: e.dma_start(out=hv[:, :, sl], in_=hb_[q][:]), r=[("hb", q)], w=["hT_dram"], dma=True)
        P.emit()


def phase_da(g, l):
    st, P, sb, ps = new_phase(g)
    import os
    NG = int(os.environ.get("DA_NG", 6))
    with st:
        uT = sb("uT", [128, KC, S], BF16)
        udv = g.uT.rearrange("(kc p) t -> p kc t", p=128)
        for kc in range(KC):
            P.op("sp", lambda e, kc=kc: e.dma_start(out=uT[:, kc, :], in_=udv[:, kc, :]), w=[("uT", kc)], dma=True)
        wq = [sb("wq%d" % i, [128, KC, 128], BF16) for i in range(2)]
        wk = [sb("wk%d" % i, [128, KC, 128], BF16) for i in range(2)]
        wv = [sb("wv%d" % i, [128, KC, 128], BF16) for i in range(2)]
        QT = sb("QT", [128, S], BF16)
        KT = sb("KT", [128, S], BF16)
        acc = sb("acc", [128, 2, S], F32)
        sqt = sb("sqt", [128, 512], BF16)
        mx = sb("mx", [128, 4], F32)
        tm = sb("tm", [128, 1], F32)
        prod = sb("prod", [128, 2], F32)
        negm = sb("negm", [128, 2], F32)
        Vaug = [sb("Vaug%d" % i, [128, 2, 128], BF16) for i in range(2)]
        sbt = [sb("sbt%d" % i, [128, 2, 256], F32) for i in range(2)]
        PT = [sb("PT%d" % i, [128, 2, 256], BF16) for i in range(2)]
        rl = sb("rl", [128, S], F32)
        rl2 = sb("rl2", [128, S], F32)
        ob = sb("ob", [128, S], BF16)
        B = [ps("B%d" % i) for i in range(8)]
        pqk, pss = B[0], B[1]
        pv = B[0]
        pst = [[B[2], B[3]], [B[4], B[5]]]
        ppv = [B[6], B[7]]
        for i in range(2):
            P.op("pool", lambda e, i=i: e.memset(Vaug[i][:], 1.0), w=[("V", i)])
        vi = 0
        bi = 0
        for gi in range(NG):
            gq = gi % 2
            w_in = g.w["w_in"][l]
            load_w(P, wq[gq], w_in, KC, ("wq", gq), cols=(O_DQ + gi * 128, O_DQ + (gi + 1) * 128))
            load_w(P, wk[gq], w_in, KC, ("wk", gq), cols=(O_DK + gi * 128, O_DK + (gi + 1) * 128))
            load_w(P, wv[gq], w_in, KC, ("wv", gq), cols=(O_DV + gi * 128, O_DV + (gi + 1) * 128))
            P.op("dve", lambda e: e.memset(mx[:], 0.0), w=["mx"])
            for t in range(8):
                sl = slice(t * 512, (t + 1) * 512)
                for (W, wn, dst, dn, mc) in ((wq[gq], "wq", QT, "QT", 0), (wk[gq], "wk", KT, "KT", 2)):
                    for kc in range(KC):
                        _mm(P, pqk[:, :], W[:, kc, :], uT[:, kc, sl], kc == 0, kc == KC - 1, r=[((wn, gq), kc), ("uT", kc)], w=["B0"])
                    P.op("act", lambda e, dst=dst, sl=sl: e.activation(out=dst[:, sl], in_=pqk[:, :], func=AF.Copy), w=["B0", dn])
                    P.op("act", lambda e: e.activation(out=sqt[:], in_=pqk[:, :], func=AF.Square), w=["B0", "sqt"])
                    for hh in range(2):
                        _mm(P, pss[:, :], g.headsel[:, hh * 128:(hh + 1) * 128], sqt[:], True, True, r=["sqt"], w=["B1"])
                        P.op("dve", lambda e: e.reduce_max(out=tm[:, 0:1], in_=pss[:, :], axis=AX.X), w=["B1", "tm"])
                        P.op("dve", lambda e, c=mc + hh: e.tensor_tensor(out=mx[:, c:c + 1], in0=mx[:, c:c + 1], in1=tm[:, 0:1], op=ALU.max),
                             r=["tm"], w=["mx"])
            P.op("dve", lambda e: e.tensor_tensor(out=prod[:], in0=mx[:, 0:2], in1=mx[:, 2:4], op=ALU.mult), r=["mx"], w=["prod"])
            P.op("act", lambda e: e.activation(out=prod[:], in_=prod[:], func=AF.Sqrt), w=["prod"])
            P.op("dve", lambda e: e.tensor_scalar(out=negm[:], in0=prod[:], scalar1=-0.125, scalar2=None, op0=ALU.mult), r=["prod"], w=["negm"])
            for pi, r_ in enumerate((1, 4, 16)):
                nblk = 32 // r_
                for p_ in range(r_):
                    for b in range(nblk):
                        def tok(bb, p_=p_, r_=r_):
                            s0 = p_ + r_ * 128 * bb
                            return slice(s0, s0 + r_ * 127 + 1, r_)
                        cur = vi % 2
                        prv = (vi - 1) % 2
                        vi += 1
                        bq = bi % 2
                        bi += 1
                        tb = tok(b)
                        for kc in range(KC):
                            _mm(P, pv[:, 0:128], uT[:, kc, tb], wv[gq][:, kc, :], kc == 0, kc == KC - 1,
                                r=[("uT", kc), (("wv", gq), kc)], w=["B0"])
                        P.op("act", lambda e, cur=cur: e.activation(out=Vaug[cur][:, :, 64:128],
                                                                  in_=pv[:, 0:128].rearrange("p (h d) -> p h d", h=2), func=AF.Copy),
                             w=["B0", ("V", cur)])
                        lo = 128 if b == 0 else 0
                        for hh in range(2):
                            rows = slice(64 * hh, 64 * hh + 64)
                            pb = pst[bq][hh]
                            btok = "B%d" % (2 + 2 * bq + hh)
                            if b > 0:
                                _mm(P, pb[:, 0:128], KT[rows, tok(b - 1)], QT[rows, tb], True, True, r=["KT", "QT"], w=[btok])
                            _mm(P, pb[:, 128:256], KT[rows, tb], QT[rows, tb], True, True, r=["KT", "QT"], w=[btok])
                            cc = -8.0 * SLOPES[gi * 2 + hh] * r_
                            P.op("dve", lambda e, pb=pb, hh=hh, bq=bq, lo=lo, cc=cc: e.scalar_tensor_tensor(
                                out=sbt[bq][:, hh, lo:256], in0=g.D2[:, lo:256], scalar=cc, in1=pb[:, lo:256], op0=ALU.mult, op1=ALU.add),
                                w=[btok, ("sbt", bq, hh)])
                            P.op("act", lambda e, hh=hh, bq=bq, lo=lo: e.activation(out=PT[bq][:, hh, lo:256], in_=sbt[bq][:, hh, lo:256],
                                                                                 func=AF.Exp, scale=0.125, bias=negm[:, hh:hh + 1]),
                                 r=[("sbt", bq, hh), "negm"], w=[("PT", bq, hh)])
                        pp = ppv[bq]
                        ptok = "B%d" % (6 + bq)
                        for hh in range(2):
                            if b > 0:
                                _mm(P, pp[:, hh * 128:(hh + 1) * 128], Vaug[prv][:, hh, :], PT[bq][:, hh, 0:128], True, False,
                                    r=[("V", prv), ("PT", bq, hh)], w=[ptok])
                            _mm(P, pp[:, hh * 128:(hh + 1) * 128], Vaug[cur][:, hh, :], PT[bq][:, hh, 128:256], b == 0, True,
                                r=[("V", cur), ("PT", bq, hh)], w=[ptok])
                        ppv3 = pp[:, 0:256].rearrange("p (h q) -> p h q", h=2)
                        if pi == 0:
                            P.op("dve", lambda e, ppv3=ppv3, tb=tb: e.tensor_copy(out=acc[:, :, tb], in_=ppv3), w=[ptok, "acc"])
                        else:
                            P.op("dve", lambda e, ppv3=ppv3, tb=tb: e.tensor_tensor(out=acc[:, :, tb], in0=ppv3, in1=acc[:, :, tb], op=ALU.add),
                                 w=[ptok, "acc"])
            for hh in range(2):
                P.op("dve", lambda e, hh=hh: e.reciprocal(out=rl[0:64, :], in_=acc[0:64, hh, :]), r=["acc"], w=["rl"])
                P.op("act", lambda e: e.activation(out=rl2[64:128, :], in_=rl[0:64, :], func=AF.Copy), r=["rl"], w=["rl2"])
                P.op("pool", lambda e, hh=hh: e.tensor_tensor(out=ob[64:128, :], in0=acc[64:128, hh, :], in1=rl2[64:128, :], op=ALU.mult),
                     r=["acc", "rl2"], w=["ob"])
                r0 = (2 * gi + hh) * 64
                P.op("sp", lambda e, r0=r0: e.dma_start(out=g.obT[r0:r0 + 64, :], in_=ob[64:128, :]), r=["ob"], dma=True)
        P.emit()


def phase_dn(g, l):
    st, P, sb, ps = new_phase(g)
    import os
    HG = 4
    NPASS = int(os.environ.get("DN_NPASS", 2))
    NT = int(os.environ.get("DN_NT", 8))
    T = 512
    with st:
        wqkv = sb("wqkv", [128, KC, 3 * HG * 128], BF16)
        wz = sb("wz", [128, KC, HG * 128], BF16)
        wba = sb("wba", [128, KC, 16], BF16)
        ut = [sb("ut%d" % i, [128, KC, T], BF16) for i in range(2)]
        betaT = sb("betaT", [8, T], F32)
        g1 = sb("g1", [8, T], F32)
        g2 = sb("g2", [8, T], F32)
        g3 = sb("g3", [8, T], F32)
        gcT = sb("gcT", [8, T], F32)
        tk = sb("tk", [128, 4, 16], F32)
        egc = sb("egc", [128, 4, 8], F32)
        negc = sb("negc", [128, 4, 8], F32)
        bege = sb("bege", [128, 4, 8], F32)
        glb = sb("glb", [128, 4, 8], F32)
        dl = sb("dl", [128, 4, 8], F32)
        edl = sb("edl", [128, 4, 8], F32)
        egl = [sb("egl%d" % i, [128, 4, HG], F32) for i in range(2)]
        halo = sb("halo", [128, 3 * HG, 3], F32)
        xpre = [sb("xpre%d" % i, [128, T + 3], F32) for i in range(2)]
        yb = [sb("yb%d" % i, [128, T], F32) for i in range(2)]
        sbf = [sb("sbf%d" % i, [128, T], F32) for i in range(2)]
        sqh = sb("sqh", [128, T], BF16)
        ctmp = sb("ctmp", [128, T], F32)
        srn = sb("srn", [128, T], F32)
        rinvn = sb("rinvn", [128, T], F32)
        qT = [sb("qT%d" % i, [128, T], BF16) for i in range(2)]
        kT = [sb("kT%d" % i, [128, T], BF16) for i in range(2)]
        vT = [sb("vT%d" % i, [128, T], BF16) for i in range(2)]
        egcb = sb("egcb", [128, T], F32)
        tE = sb("tE", [128, 4, 128], F32)
        E4 = sb("E4", [128, 4, 128], F32)
        BBs = sb("BBs", [128, 4, 128], F32)
        EBs = sb("EBs", [128, 4, 128], F32)
        import os as _os
        DT_T = F32R if _os.environ.get("USE_F32R") else F32
        A = [sb("A%d" % i, [128, 4, 128], DT_T) for i in range(2)]
        Bm = [sb("Bm%d" % i, [128, 4, 128], DT_T) for i in range(2)]
        R = sb("R", [128, 4, 128], DT_T)
        kbg = sb("kbg", [128, 4, 128], DT_T)
        vb = sb("vb", [128, 4, 128], DT_T)
        identr = sb("identr", [128, 128], DT_T)
        P.op("act", lambda e: e.activation(out=identr[:], in_=g.identf, func=AF.Copy), w=["identr"])
        wT4 = [sb("wT4%d" % i, [128, HG, T], BF16) for i in range(2)]
        u4 = [sb("u4%d" % i, [128, HG, 4, 128], F32) for i in range(2)]
        qgT = [sb("qgT%d" % i, [128, HG, T], BF16) for i in range(2)]
        qkT4 = [sb("qkT4%d" % i, [128, HG, T], BF16) for i in range(2)]
        kdec = [sb("kdec%d" % i, [128, HG, 4, 128], BF16) for i in range(2)]
        Sf = sb("Sf", [128, HG, 128], F32)
        Sb = sb("Sb", [128, HG, 128], BF16)
        vnew = [sb("vnew%d" % i, [128, HG, 128], BF16) for i in range(2)]
        oraw = sb("oraw", [128, HG, T], F32)
        sqo = sb("sqo", [128, T], BF16)
        sro = sb("sro", [128, T], F32)
        rinvo = sb("rinvo", [128, T], F32)
        sz = sb("sz", [128, T], F32)
        on = sb("on", [128, T], F32)
        oab = [sb("oab%d" % i, [128, T], BF16) for i in range(2)]
        PA = ps("PA")
        PB = ps("PB")
        PC = [ps("PC%d" % i) for i in range(2)]
        PTt = ps("PTt", (128, 1024), BF16)
        PSW = ps("PSW")
        PSO = ps("PSO")
        PSD = ps("PSD")
        pcn = [0]

        def pc():
            i = pcn[0] % 2
            pcn[0] += 1
            return PC[i], ("PC", i)

        def c4(ap):
            return ap.rearrange("p (c i) -> p c i", c=4)

        def bc4(ap2):
            return ap2.unsqueeze(1).to_broadcast([128, 4, 128])

        def colbc(ap_c):
            return ap_c.unsqueeze(2).to_broadcast([128, 4, 128])

        w_in = g.w["w_in"][l]
        udv = g.uT.rearrange("(kc p) t -> p kc t", p=128)
        nAcol = g.nA[:, l:l + 1]
        dtbcol = g.dtb_sb[:, l:l + 1]
        dnncol = g.dnn[:, l:l + 1]

        def gates(t):
            q = t % 2
            P.op("sp", lambda e: e.dma_start(out=ut[q][:], in_=udv[:, :, t * T:(t + 1) * T]), w=[("ut", q)], dma=True)
            for kc in range(KC):
                _mm(P, PA[0:8, :], wba[:, kc, 0:8], ut[q][:, kc, :], kc == 0, kc == KC - 1, r=["wba", ("ut", q)], w=["PA"])
            P.op("act", lambda e: e.activation(out=betaT[:], in_=PA[0:8, :], func=AF.Sigmoid), w=["PA", "betaT"])
            for kc in range(KC):
                _mm(P, PA[0:8, :], wba[:, kc, 8:16], ut[q][:, kc, :], kc == 0, kc == KC - 1, r=["wba", ("ut", q)], w=["PA"])
            P.op("dve", lambda e: e.tensor_scalar(out=g1[:], in0=PA[0:8, :], scalar1=dtbcol, scalar2=None, op0=ALU.add), w=["PA", "g1"])
            P.op("dve", lambda e: e.tensor_scalar(out=g2[:], in0=g1[:], scalar1=-1.0, scalar2=None, op0=ALU.mult), r=["g1"], w=["g2"])
            P.op("dve", lambda e: e.tensor_tensor(out=g2[:], in0=g2[:], in1=g1[:], op=ALU.max), r=["g1"], w=["g2"])
            P.op("act", lambda e: e.activation(out=g2[:], in_=g2[:], func=AF.Exp, scale=-1.0), w=["g2"])
            P.op("act", lambda e: e.activation(out=g2[:], in_=g2[:], func=AF.Ln, bias=1.0), w=["g2"])
            P.op("dve", lambda e: e.tensor_scalar(out=g1[:], in0=g1[:], scalar1=0.0, scalar2=None, op0=ALU.max), w=["g1"])
            P.op("dve", lambda e: e.tensor_tensor(out=g1[:], in0=g1[:], in1=g2[:], op=ALU.add), r=["g2"], w=["g1"])
            P.op("dve", lambda e: e.tensor_scalar(out=g3[:], in0=g1[:], scalar1=nAcol, scalar2=None, op0=ALU.mult), r=["g1"], w=["g3"])
            P.op("dve", lambda e: e.tensor_tensor_scan(out=gcT[:], data0=g.resetm, data1=g3[:], initial=0.0, op0=ALU.mult, op1=ALU.add),
                 r=["g3"], w=["gcT"])
            for c in range(4):
                cs = slice(c * 128, (c + 1) * 128)
                _mm(P, PB[:, c * 16:c * 16 + 8], gcT[0:8, cs], g.identf[0:8, 0:8], True, True, r=["gcT"], w=["PB"])
                _mm(P, PB[:, c * 16 + 8:c * 16 + 16], betaT[0:8, cs], g.identf[0:8, 0:8], True, True, r=["betaT"], w=["PB"])
            tkf = tk[:].rearrange("p c k -> p (c k)")
            P.op("act", lambda e: e.activation(out=tkf, in_=PB[:, 0:64], func=AF.Copy), w=["PB", "tk"])
            P.op("act", lambda e: e.activation(out=egc[:], in_=tk[:, :, 0:8], func=AF.Exp), r=["tk"], w=["egc"])
            P.op("dve", lambda e: e.tensor_scalar(out=negc[:], in0=tk[:, :, 0:8], scalar1=-1.0, scalar2=None, op0=ALU.mult), r=["tk"], w=["negc"])
            P.op("dve", lambda e: e.tensor_tensor(out=bege[:], in0=tk[:, :, 8:16], in1=egc[:], op=ALU.mult), r=["tk", "egc"], w=["bege"])

        def pre(t, hp, hl):
            h = hp * HG + hl
            q = t % 2
            ws = hl % 2
            for part in range(3):
                fcl = part * HG + hl
                fcg = part * 8 + h
                xi = part % 2
                xp = xpre[xi]
                for kc in range(KC):
                    _mm(P, PA[:, :], wqkv[:, kc, fcl * 128:(fcl + 1) * 128], ut[q][:, kc, :], kc == 0, kc == KC - 1,
                        r=[("wqkv", kc), ("ut", q)], w=["PA"])
                P.op("pool", lambda e, xp=xp, fcl=fcl: e.tensor_copy(out=xp[:, 0:3], in_=halo[:, fcl, :]), r=[("halo", fcl)], w=[("xpre", xi)])
                P.op("act", lambda e, xp=xp: e.activation(out=xp[:, 3:T + 3], in_=PA[:, :], func=AF.Copy), w=["PA", ("xpre", xi)])
                P.op("pool", lambda e, xp=xp, fcl=fcl: e.tensor_copy(out=halo[:, fcl, :], in_=xp[:, T:T + 3]), r=[("xpre", xi)], w=[("halo", fcl)])
                y = yb[xi]
                cwb = l * 96 + fcg * 4
                P.op("dve", lambda e, xp=xp, y=y, cwb=cwb: e.tensor_scalar(out=y[:], in0=xp[:, 0:T], scalar1=g.convw[:, cwb:cwb + 1], scalar2=None,
                                                                       op0=ALU.mult), r=[("xpre", xi)], w=[("y", xi)])
                for j in range(1, 4):
                    P.op("dve", lambda e, xp=xp, y=y, cwb=cwb, j=j: e.scalar_tensor_tensor(out=y[:], in0=xp[:, j:j + T],
                                                                                      scalar=g.convw[:, cwb + j:cwb + j + 1],
                                                                                      in1=y[:], op0=ALU.mult, op1=ALU.add),
                         r=[("xpre", xi)], w=[("y", xi)])
                if part == 2:
                    P.op("act", lambda e, y=y: e.activation(out=vT[ws][:], in_=y[:], func=AF.Silu), r=[("y", xi)], w=[("vT", ws)])
                else:
                    s_ = sbf[xi]
                    dst = qT[ws] if part == 0 else kT[ws]
                    dn_ = ("qT", ws) if part == 0 else ("kT", ws)
                    P.op("act", lambda e, y=y, s_=s_: e.activation(out=s_[:], in_=y[:], func=AF.Silu), r=[("y", xi)], w=[("sbf", xi)])
                    P.op("act", lambda e, s_=s_: e.activation(out=sqh[:], in_=s_[:], func=AF.Square), r=[("sbf", xi)], w=["sqh"])
                    _mm(P, PB[:, :], g.onesb, sqh[:], True, True, r=["sqh"], w=["PB"])
                    scl = 128.0 if part == 0 else 1.0
                    P.op("act", lambda e, scl=scl: e.activation(out=srn[:], in_=PB[:, :], func=AF.Ln, scale=scl, bias=EPS * scl), w=["PB", "srn"])
                    P.op("act", lambda e: e.activation(out=rinvn[:], in_=srn[:], func=AF.Exp, scale=-0.5), r=["srn"], w=["rinvn"])
                    P.op("dve", lambda e, s_=s_, dst=dst: e.tensor_tensor(out=dst[:], in0=s_[:], in1=rinvn[:], op=ALU.mult),
                         r=[("sbf", xi), "rinvn"], w=[dn_])
            selh = g.sel[:, h * 128:(h + 1) * 128]
            _mm(P, PB[:, :], selh, gcT[:], True, True, r=["gcT"], w=["PB"])
            P.op("act", lambda e: e.activation(out=glb[:, :, h], in_=PB[:, 127:512:128], func=AF.Copy), w=["PB", ("glb", h)])
            P.op("act", lambda e: e.activation(out=egcb[:], in_=PB[:, :], func=AF.Exp), w=["PB", "egcb"])
            P.op("dve", lambda e: e.tensor_tensor(out=tE[:], in0=c4(PB[:, :]), in1=bc4(g.negmask), op=ALU.add), w=["PB", "tE"])
            P.op("pool", lambda e: e.tensor_tensor(out=qgT[q][:, hl, :], in0=qT[ws][:], in1=egcb[:], op=ALU.mult),
                 r=[("qT", ws), "egcb"], w=[("qg", q, hl)])
            P.op("dve", lambda e: e.tensor_tensor(out=dl[:, :, h], in0=glb[:, :, h], in1=tk[:, :, h], op=ALU.subtract),
                 r=[("glb", h), "tk"], w=[("dl", h)])
            P.op("act", lambda e: e.activation(out=edl[:, :, h], in_=dl[:, :, h], func=AF.Exp), r=[("dl", h)], w=[("edl", h)])
            P.op("act", lambda e: e.activation(out=egl[q][:, :, hl], in_=glb[:, :, h], func=AF.Exp), r=[("glb", h)], w=[("egl", q, hl)])
            for c in range(4):
                P.op("act", lambda e, c=c: e.activation(out=E4[:, c, :], in_=tE[:, c, :], func=AF.Exp, bias=negc[:, c, h:h + 1]),
                     r=["tE", "negc"], w=["E4"])
            _mm(P, PB[:, :], selh, betaT[:], True, True, r=["betaT"], w=["PB"])
            P.op("dve", lambda e: e.tensor_tensor(out=BBs[:], in0=c4(PB[:, :]), in1=bc4(g.strict01), op=ALU.mult), w=["PB", "BBs"])
            P.op("pool", lambda e: e.tensor_tensor(out=EBs[:], in0=E4[:], in1=BBs[:], op=ALU.mult), r=["E4", "BBs"], w=["EBs"])
            for c in range(4):
                cs = slice(c * 128, (c + 1) * 128)
                P.op("pe", lambda e, cs=cs: e.transpose(out=PTt[:, cs], in_=kT[ws][:, cs], identity=g.identb), r=[("kT", ws)], w=["PT"])
            for c in range(4):
                cs = slice(c * 128, (c + 1) * 128)
                P.op("pe", lambda e, cs=cs, c=c: e.transpose(out=PTt[:, 512 + c * 128:512 + (c + 1) * 128], in_=vT[ws][:, cs], identity=g.identb),
                     r=[("vT", ws)], w=["PT"])
            P.op("dve", lambda e: e.tensor_tensor(out=kbg[:], in0=c4(PTt[:, 0:512]), in1=colbc(bege[:, :, h]), op=ALU.mult),
                 r=["bege"], w=["PT", "kbg"])
            P.op("dve", lambda e: e.tensor_tensor(out=kdec[q][:, hl, :, :], in0=c4(PTt[:, 0:512]), in1=colbc(edl[:, :, h]), op=ALU.mult),
                 r=[("edl", h)], w=["PT", ("kdec", q, hl)])
            P.op("dve", lambda e: e.tensor_tensor(out=vb[:], in0=c4(PTt[:, 512:1024]), in1=colbc(tk[:, :, 8 + h]), op=ALU.mult),
                 r=["tk"], w=["PT", "vb"])
            pkk, tkk = pc()
            for c in range(4):
                cs = slice(c * 128, (c + 1) * 128)
                _mm(P, pkk[:, cs], kT[ws][:, cs], kT[ws][:, cs], True, True, r=[("kT", ws)], w=[tkk])
            pqk, tqk = pc()
            for c in range(4):
                cs = slice(c * 128, (c + 1) * 128)
                _mm(P, pqk[:, cs], kT[ws][:, cs], qT[ws][:, cs], True, True, r=[("kT", ws), ("qT", ws)], w=[tqk])
            P.op("dve", lambda e: e.tensor_tensor(out=A[0][:], in0=c4(pkk[:, :]), in1=EBs[:], op=ALU.mult), r=["EBs"], w=[tkk, ("A", 0)])
            P.op("dve", lambda e: e.tensor_tensor(out=c4(qkT4[q][:, hl, :]), in0=c4(pqk[:, :]), in1=E4[:], op=ALU.mult),
                 r=["E4"], w=[tqk, ("qk", q, hl)])
            pt0, tt0 = pc()
            for c in range(4):
                cs = slice(c * 128, (c + 1) * 128)
                _mmr(P, pt0[:, cs], A[0][:, c, :], identr[:], True, True, r=[("A", 0), "identr"], w=[tt0])
            P.op("act", lambda e, pt0=pt0: e.activation(out=Bm[0][:], in_=c4(pt0[:, :]), func=AF.Copy), w=[tt0, ("B", 0)])
            P.op("dve", lambda e: e.scalar_tensor_tensor(out=R[:], in0=A[0][:], scalar=-1.0, in1=bc4(g.identf), op0=ALU.mult, op1=ALU.add),
                 r=[("A", 0)], w=["R"])
            for k in range(1, 7):
                ap_, bp_ = (k - 1) % 2, (k - 1) % 2
                an_, bn_ = k % 2, k % 2
                if k <= 5:
                    px, tx = pc()
                    for c in range(4):
                        cs = slice(c * 128, (c + 1) * 128)
                        _mmr(P, px[:, cs], Bm[bp_][:, c, :], A[ap_][:, c, :], True, True, r=[("B", bp_), ("A", ap_)], w=[tx])
                if k <= 5:
                    P.op("act", lambda e, px=px, an_=an_: e.activation(out=A[an_][:], in_=c4(px[:, :]), func=AF.Copy), w=[tx, ("A", an_)])
                    py, ty = pc()
                    for c in range(4):
                        cs = slice(c * 128, (c + 1) * 128)
                        P.op("pe", lambda e, py=py, cs=cs, c=c, an_=an_: e.transpose(out=py[:, cs], in_=A[an_][:, c, :], identity=g.identf),
                             r=[("A", an_)], w=[ty])
                else:
                    py, ty = pc()
                    for c in range(4):
                        cs = slice(c * 128, (c + 1) * 128)
                        _mmr(P, py[:, cs], A[ap_][:, c, :], Bm[bp_][:, c, :], True, True, r=[("B", bp_), ("A", ap_)], w=[ty])
                P.op("act", lambda e, py=py, bn_=bn_: e.activation(out=Bm[bn_][:], in_=c4(py[:, :]), func=AF.Copy), w=[ty, ("B", bn_)])
                pz, tz = pc()
                for c in range(4):
                    cs = slice(c * 128, (c + 1) * 128)
                    _mmr(P, pz[:, cs], Bm[bn_][:, c, :], R[:, c, :], True, True, r=[("B", bn_), "R"], w=[tz])
                P.op("dve", lambda e, pz=pz: e.tensor_tensor(out=R[:], in0=c4(pz[:, :]), in1=R[:], op=ALU.add), w=[tz, "R"])
            pw, tw = pc()
            for c in range(4):
                cs = slice(c * 128, (c + 1) * 128)
                _mmr(P, pw[:, cs], kbg[:, c, :], R[:, c, :], True, True, r=["kbg", "R"], w=[tw])
            P.op("act", lambda e, pw=pw: e.activation(out=wT4[q][:, hl, :], in_=pw[:, :], func=AF.Copy), w=[tw, ("wT", q, hl)])
            pu, tu = pc()
            for c in range(4):
                cs = slice(c * 128, (c + 1) * 128)
                _mmr(P, pu[:, cs], R[:, c, :], vb[:, c, :], True, True, r=["vb", "R"], w=[tu])
            P.op("act", lambda e, pu=pu: e.activation(out=u4[q][:, hl, :, :], in_=c4(pu[:, :]), func=AF.Copy), w=[tu, ("u4", q, hl)])

        def scan_step(t, hp, c):
            q = t % 2
            vq = c % 2
            cs = slice(c * 128, (c + 1) * 128)
            for hl in range(HG):
                _mm(P, PSW[:, hl * 128:(hl + 1) * 128], wT4[q][:, hl, cs], Sb[:, hl, :], True, True, r=[("wT", q, hl), "Sb"], w=["PSW"])
            P.op("dve", lambda e: e.tensor_tensor(out=vnew[vq][:], in0=u4[q][:, :, c, :], in1=PSW[:, :].rearrange("p (h e) -> p h e", h=HG),
                                                  op=ALU.subtract),
                 r=[("u4", q, hl) for hl in range(HG)], w=["PSW", ("vnew", vq)])
            for hl in range(HG):
                _mm(P, PSO[:, hl * 128:(hl + 1) * 128], Sb[:, hl, :], qgT[q][:, hl, cs], True, False, r=[("qg", q, hl), "Sb"], w=["PSO"])
                _mm(P, PSO[:, hl * 128:(hl + 1) * 128], vnew[vq][:, hl, :], qkT4[q][:, hl, cs], False, True,
                    r=[("qk", q, hl), ("vnew", vq)], w=["PSO"])
            for hl in range(HG):
                _mm(P, PSD[:, hl * 128:(hl + 1) * 128], kdec[q][:, hl, c, :], vnew[vq][:, hl, :], True, True,
                    r=[("kdec", q, hl), ("vnew", vq)], w=["PSD"])
            for hl in range(HG):
                P.op("dve", lambda e, hl=hl: e.scalar_tensor_tensor(out=Sf[:, hl, :], in0=Sf[:, hl, :], scalar=egl[q][:, c, hl:hl + 1],
                                                                   in1=PSD[:, hl * 128:(hl + 1) * 128], op0=ALU.mult, op1=ALU.add),
                     r=[("egl", q, hl)], w=["PSD", "Sf"])
            P.op("pool", lambda e: e.tensor_copy(out=Sb[:], in_=Sf[:]), r=["Sf"], w=["Sb"])
            P.op("act", lambda e: e.activation(out=oraw[:, :, cs], in_=PSO[:, :].rearrange("p (h i) -> p h i", h=HG), func=AF.Copy),
                 w=["PSO", "oraw"])

        def post(t, hp, hl):
            h = hp * HG + hl
            q = t % 2
            P.op("act", lambda e: e.activation(out=sqo[:], in_=oraw[:, hl, :], func=AF.Square), r=["oraw"], w=["sqo"])
            _mm(P, PB[:, :], g.onesb, sqo[:], True, True, r=["sqo"], w=["PB"])
            P.op("act", lambda e: e.activation(out=sro[:], in_=PB[:, :], func=AF.Ln, scale=1.0 / 128.0, bias=EPS), w=["PB", "sro"])
            P.op("act", lambda e: e.activation(out=rinvo[:], in_=sro[:], func=AF.Exp, scale=-0.5), r=["sro"], w=["rinvo"])
            for kc in range(KC):
                _mm(P, PA[:, :], wz[:, kc, hl * 128:(hl + 1) * 128], ut[q][:, kc, :], kc == 0, kc == KC - 1, r=[("wz", kc), ("ut", q)], w=["PA"])
            P.op("act", lambda e: e.activation(out=sz[:], in_=PA[:, :], func=AF.Silu), w=["PA", "sz"])
            P.op("dve", lambda e: e.tensor_tensor(out=on[:], in0=oraw[:, hl, :], in1=rinvo[:], op=ALU.mult), r=["oraw", "rinvo"], w=["on"])
            ob_ = oab[hl % 2]
            P.op("dve", lambda e: e.scalar_tensor_tensor(out=ob_[:], in0=on[:], scalar=dnncol, in1=sz[:], op0=ALU.mult, op1=ALU.mult),
                 r=["on", "sz"], w=[("oab", hl % 2)])
            P.op("sp", lambda e: e.dma_start(out=g.oaT[h * 128:(h + 1) * 128, t * T:(t + 1) * T], in_=ob_[:]), r=[("oab", hl % 2)], dma=True)

        load_w(P, wba, w_in, KC, "wba_", cols=(O_B, O_B + 16))
        P.op("pool", lambda e: e.memset(g1[:], 0.0), r=[("wba_", kc) for kc in range(KC)], w=["wba"])
        for hp in range(NPASS):
            v3 = w_in.rearrange("(kc p) n -> p kc n", p=128)
            for part in range(3):
                for kc in range(KC):
                    c0 = part * 1024 + hp * HG * 128
                    P.op("pool", lambda e, part=part, kc=kc, c0=c0: e.dma_start(out=wqkv[:, kc, part * HG * 128:(part + 1) * HG * 128],
                                                                              in_=v3[:, kc, c0:c0 + HG * 128]), w=[("wqkv", kc)], dma=True)
            load_w(P, wz, w_in, KC, "wz", cols=(O_Z + hp * HG * 128, O_Z + (hp + 1) * HG * 128))
            P.op("pool", lambda e: e.memset(halo[:], 0.0), w=[("halo", i) for i in range(3 * HG)])
            P.op("pool", lambda e: e.memset(Sf[:], 0.0), w=["Sf"])
            P.op("pool", lambda e: e.memset(Sb[:], 0.0), w=["Sb"])
            gates(0)
            for hl in range(HG):
                pre(0, hp, hl)
            for t in range(NT):
                if t + 1 < NT:
                    gates(t + 1)
                for c in range(4):
                    scan_step(t, hp, c)
                    if t + 1 < NT:
                        pre(t + 1, hp, c)
                for hl in range(HG):
                    post(t, hp, hl)
        P.emit()


def host_consts():
    cf = np.zeros((128, 2176), np.float32)
    j = np.arange(128)[:, None]
    i = np.arange(128)[None, :]
    cf[:, 0:128] = np.eye(128, dtype=np.float32)
    cf[:, 128:256] = np.where(j <= i, 0.0, NEG)
    cf[:, 256:384] = (j < i).astype(np.float32)
    k = j
    q = i
    prev = np.where(q <= k, 128.0 + q - k, BIGD)
    cur = np.where(q >= k, (q - k) * 1.0, BIGD)
    cf[:, 384:512] = prev
    cf[:, 512:640] = cur
    for h in range(8):
        cf[h, 640 + h * 128: 640 + (h + 1) * 128] = 1.0
    rm = np.ones((512,), np.float32)
    rm[0::128] = 0.0
    cf[0:8, 1664:2176] = rm[None, :]
    cb = np.zeros((128, 512), np.float32)
    cb[:, 0:128] = np.eye(128, dtype=np.float32)
    cb[:, 128:256] = 1.0
    cb[0:64, 256:384] = 1.0
    cb[64:128, 384:512] = 1.0
    return cf, cb


def make_in_maps(inputs, ncores=8):
    f = lambda a: np.ascontiguousarray(np.asarray(a, dtype=np.float32))
    cf, cb = host_consts()
    shared = {}
    shared["ada_w"] = f(inputs["ada_w"])
    shared["ada_bT"] = f(np.asarray(inputs["ada_b"]).reshape(DEPTH, 72, 128).transpose(2, 0, 1).reshape(128, DEPTH * 72))
    ln = np.stack([np.asarray(inputs["ln_ffn1"]), np.asarray(inputs["ln_mix"]), np.asarray(inputs["ln_ffn2"])], axis=1)
    lnT = ln.reshape(DEPTH, 3, 8, 128).transpose(3, 0, 1, 2).reshape(128, DEPTH * 24)
    fn = np.asarray(inputs["final_norm"]).reshape(8, 128).T
    shared["lnT"] = f(np.concatenate([lnT, fn], axis=1))
    cw = np.asarray(inputs["conv_w"])
    shared["conv_wT"] = f(cw.reshape(DEPTH, 4, 24, 128).transpose(3, 0, 2, 1).reshape(128, DEPTH * 96))
    shared["dnnT"] = f(np.asarray(inputs["dn_norm"]).T)
    shared["alog"] = f(np.asarray(inputs["a_log"]).T)
    shared["dtb"] = f(np.asarray(inputs["dt_bias"]).T)
    for nm in ("ffn1_wg", "ffn1_wu", "ffn1_wd", "ffn2_wg", "ffn2_wu", "ffn2_wd", "w_in", "w_a", "w_b", "w_o"):
        shared[nm] = f(inputs[nm])
    shared["cf"] = cf
    shared["cb"] = cb
    x = np.asarray(inputs["x"], dtype=np.float32)
    c = np.asarray(inputs["c"], dtype=np.float32)
    maps = []
    for b in range(ncores):
        m = dict(shared)
        m["xT"] = np.ascontiguousarray(x[b].T)
        m["c_col"] = np.ascontiguousarray(c[b].reshape(8, 128).T)
        maps.append(m)
    return maps


def kernel(**inputs):
    nc = bass.Bass("TRN2", target_bir_lowering=False)
    build(nc)
    maps = make_in_maps(inputs, 8)
    res = run_bass_kernel_spmd(nc, maps, core_ids=list(range(8)))
    out = np.stack([np.ascontiguousarray(r["outT"].T) for r in res.results], axis=0)
    return out.astype(np.float32)
```

```python
import numpy as np
import ml_dtypes
import concourse.bass as bass
import concourse.mybir as mybir
from concourse.bass_utils import run_bass_kernel_spmd
from contextlib import ExitStack

F32 = mybir.dt.float32
BF16 = mybir.dt.bfloat16
AF = mybir.ActivationFunctionType
ALU = mybir.AluOpType

D = 1024
S = 4096
KC = 8
DFF = 2816
NFF = 22
DEPTH = 2
INC = 8464
O_Z, O_B, O_A, O_DQ, O_DK, O_DV, O_GA, O_GB = 3072, 4096, 4104, 4112, 4880, 5648, 6416, 7440
EPS = 1e-6
NEG = -1.0e9
BIGD = 1.0e5
SLOPES = [2.0 ** (-8.0 * (h + 1) / 12.0) for h in range(12)]


ENGS = ("pe", "act", "dve", "pool", "sp")
ENGOBJ = {"pe": "tensor", "act": "scalar", "dve": "vector", "pool": "gpsimd", "sp": "sync"}
EPOCH = 12000
NDMASEM = 24


class Sems:
    def __init__(self, nc, stack):
        self.nc, self.stack = nc, stack
        self.sems = {}
        self.cnt = {e: 0 for e in ENGS}
        self.ndma = 0
        self.lastdma = {}

    def get(self, key):
        if key not in self.sems:
            self.sems[key] = self.stack.enter_context(self.nc.semaphore("s_%s_%s" % key))
        return self.sems[key]


class Prog:
    def __init__(self, nc, sems):
        self.nc, self.S = nc, sems
        self.ops = []
        self.last_w = {}
        self.readers = {}

    def op(self, eng, fn, r=(), w=(), dma=False):
        i = len(self.ops)
        deps = set()
        for b in r:
            lw = self.last_w.get(b)
            if lw is not None:
                deps.add(lw)
        for b in w:
            lw = self.last_w.get(b)
            if lw is not None:
                deps.add(lw)
            for x in self.readers.get(b, ()):
                deps.add(x)
        deps.discard(i)
        for b in r:
            self.readers.setdefault(b, []).append(i)
        for b in w:
            self.last_w[b] = i
            self.readers[b] = []
        self.ops.append(dict(eng=eng, fn=fn, deps=deps, dma=dma))
        return i

    def emit(self):
        nc, S, ops = self.nc, self.S, self.ops
        need = [False] * len(ops)
        for i, o in enumerate(ops):
            if o["dma"]:
                need[i] = True
            for d in o["deps"]:
                od = ops[d]
                if od["dma"] or o["dma"] or od["eng"] != o["eng"]:
                    need[d] = True
        for i, o in enumerate(ops):
            if o["dma"]:
                k = S.ndma % NDMASEM
                o["sem"] = S.get(("dma", k))
                o["val"] = 16 * (S.ndma // NDMASEM + 1)
                o["prev"] = S.lastdma.get(k)
                S.lastdma[k] = (o["sem"], o["val"])
                S.ndma += 1
            elif need[i]:
                e = o["eng"]
                o["sem"] = S.get((e, S.cnt[e] // EPOCH))
                o["val"] = S.cnt[e] % EPOCH + 1
                S.cnt[e] += 1
        per = {e: [] for e in ENGS}
        for i, o in enumerate(ops):
            per[o["eng"]].append(i)
        final_dma = list(S.lastdma.values())
        with nc.Block() as block:
            for e in ENGS:
                lst = per[e]
                if not lst and e != "sp":
                    continue

                def body(eng, lst=lst, e=e):
                    waited = {}

                    def wait(sem, val):
                        k = id(sem)
                        if waited.get(k, 0) >= val:
                            return
                        eng.wait_ge(sem, val)
                        waited[k] = val

                    for i in lst:
                        o = ops[i]
                        for d in sorted(o["deps"]):
                            od = ops[d]
                            if od["dma"] or o["dma"] or od["eng"] != e:
                                wait(od["sem"], od["val"])
                        if o["dma"] and o["prev"] is not None:
                            wait(*o["prev"])
                        ins = o["fn"](eng)
                        if o["dma"]:
                            ins.then_inc(o["sem"], 16)
                        elif need[i]:
                            ins.then_inc(o["sem"], 1)
                    if e == "sp":
                        for sem, val in final_dma:
                            wait(sem, val)

                getattr(block, ENGOBJ[e])(body)


class Ctx:
    pass


def _mm(P, out, lhsT, rhs, start, stop, r, w):
    P.op("pe", lambda e: e.matmul(out, lhsT=lhsT, rhs=rhs, start=start, stop=stop), r=r, w=w)


F32R = mybir.dt.float32r


def _mmr(P, out, lhsT, rhs, start, stop, r, w):
    import os
    P.op("pe", lambda e: e.matmul(out, lhsT=lhsT, rhs=rhs, start=start, stop=stop), r=r, w=w)


def build(nc, debug=False, phases=None):
    g = Ctx()
    g.nc = nc
    dk = "ExternalOutput" if debug else "Internal"

    def din(name, shape, dt=F32):
        return nc.dram_tensor(name, shape, dt, kind="ExternalInput").ap()

    g.xT = din("xT", [D, S])
    g.c_col = din("c_col", [128, KC])
    g.ada_w = din("ada_w", [DEPTH, D, 9 * D])
    g.ada_bT = din("ada_bT", [128, DEPTH * 72])
    g.lnT = din("lnT", [128, DEPTH * 24 + 8])
    g.conv_wT = din("conv_wT", [128, DEPTH * 96])
    g.dnnT = din("dnnT", [128, DEPTH])
    g.alog = din("alog", [8, DEPTH])
    g.dtb = din("dtb", [8, DEPTH])
    g.w = {}
    for nm, shp in (("ffn1_wg", [DEPTH, D, DFF]), ("ffn1_wu", [DEPTH, D, DFF]), ("ffn1_wd", [DEPTH, DFF, D]),
                    ("ffn2_wg", [DEPTH, D, DFF]), ("ffn2_wu", [DEPTH, D, DFF]), ("ffn2_wd", [DEPTH, DFF, D]),
                    ("w_in", [DEPTH, D, INC]), ("w_a", [DEPTH, D, D]), ("w_b", [DEPTH, 768, D]), ("w_o", [DEPTH, D, D])):
        g.w[nm] = din(nm, shp)
    g.cf = din("cf", [128, 2176])
    g.cb = din("cb", [128, 512])
    g.outT = nc.dram_tensor("outT", [D, S], F32, kind="ExternalOutput").ap()
    g.hT = nc.dram_tensor("hT", [D, S], F32, kind=dk).ap()
    g.uT = nc.dram_tensor("uT", [D, S], BF16, kind=dk).ap()
    g.oaT = nc.dram_tensor("oaT", [D, S], BF16, kind=dk).ap()
    g.obT = nc.dram_tensor("obT", [768, S], BF16, kind=dk).ap()
    if debug:
        g.dbg = nc.dram_tensor("dbg", [128, 4096], F32, kind="ExternalOutput").ap()

    with ExitStack() as gst:
        g.sems = Sems(nc, gst)

        def gsb(name, shape, dt):
            return gst.enter_context(nc.sbuf_tensor(name, shape, dt))

        g.cf_sb = gsb("cf_sb", [128, 2176], F32)
        g.cb_sb = gsb("cb_sb", [128, 512], BF16)
        g.identf = g.cf_sb[:, 0:128]
        g.negmask = g.cf_sb[:, 128:256]
        g.strict01 = g.cf_sb[:, 256:384]
        g.D2 = g.cf_sb[:, 384:640]
        g.sel = g.cf_sb[0:8, 640:1664]
        g.resetm = g.cf_sb[0:8, 1664:2176]
        g.identb = g.cb_sb[:, 0:128]
        g.onesb = g.cb_sb[:, 128:256]
        g.headsel = g.cb_sb[:, 256:512]
        g.ccol = gsb("ccol", [128, KC], F32)
        g.cact = gsb("cact", [128, KC], BF16)
        g.adab = gsb("adab", [128, DEPTH * 72], F32)
        g.ln = gsb("ln", [128, DEPTH * 24 + 8], F32)
        g.convw = gsb("convw", [128, DEPTH * 96], F32)
        g.dnn = gsb("dnn", [128, DEPTH], F32)
        g.alog_sb = gsb("alog_sb", [8, DEPTH], F32)
        g.dtb_sb = gsb("dtb_sb", [8, DEPTH], F32)
        g.nA = gsb("nA", [8, DEPTH], F32)
        g.modT = gsb("modT", [128, 72], F32)
        g.gm = gsb("gm", [128, 24], F32)
        g.gate = gsb("gate", [128, 24], F32)
        g.zero8 = gsb("zero8", [128, 8], F32)

        plist = []
        plist.append(("setup", lambda: phase_setup(g)))
        for l in range(DEPTH):
            plist.append(("mod%d" % l, lambda l=l: phase_mod(g, l)))
            plist.append(("ffn1_%d" % l, lambda l=l: phase_ffn(g, l, 0, g.xT if l == 0 else g.hT)))
            plist.append(("u%d" % l, lambda l=l: phase_u(g, l)))
            plist.append(("dn%d" % l, lambda l=l: phase_dn(g, l)))
            plist.append(("da%d" % l, lambda l=l: phase_da(g, l)))
            plist.append(("out%d" % l, lambda l=l: phase_out(g, l)))
            plist.append(("ffn2_%d" % l, lambda l=l: phase_ffn(g, l, 2, g.hT)))
        plist.append(("final", lambda: phase_final(g)))
        for name, fn in plist:
            if phases is None or name in phases:
                fn()
    return nc


def new_phase(g):
    st = ExitStack()
    P = Prog(g.nc, g.sems)
    g.pid = getattr(g, "pid", 0) + 1
    pid = g.pid

    def sb(name, shape, dt):
        return st.enter_context(g.nc.sbuf_tensor("p%d_%s" % (pid, name), shape, dt))

    def ps(name, shape=(128, 512), dt=F32):
        return st.enter_context(g.nc.psum_tensor("p%d_%s" % (pid, name), list(shape), dt))

    return st, P, sb, ps


def phase_setup(g):
    st, P, sb, ps = new_phase(g)
    with st:
        P.op("sp", lambda e: e.dma_start(out=g.cf_sb[:], in_=g.cf), w=["cf"], dma=True)
        P.op("pool", lambda e: e.dma_start(out=g.cb_sb[:], in_=g.cb), w=["cb"], dma=True)
        for nm, dst, src in (("ccol", g.ccol, g.c_col), ("adab", g.adab, g.ada_bT), ("ln", g.ln, g.lnT),
                             ("convw", g.convw, g.conv_wT), ("dnn", g.dnn, g.dnnT),
                             ("alog", g.alog_sb, g.alog), ("dtb", g.dtb_sb, g.dtb)):
            P.op("sp", lambda e, dst=dst, src=src: e.dma_start(out=dst[:], in_=src), w=[nm], dma=True)
        P.op("act", lambda e: e.activation(out=g.cact[:], in_=g.ccol[:], func=AF.Silu), r=["ccol"], w=["cact"])
        P.op("act", lambda e: e.activation(out=g.nA[:], in_=g.alog_sb[:], func=AF.Exp), r=["alog"], w=["nA"])
        P.op("dve", lambda e: e.tensor_scalar(out=g.nA[:], in0=g.nA[:], scalar1=-1.0, scalar2=None, op0=ALU.mult), r=["nA"], w=["nA"])
        P.op("dve", lambda e: e.memset(g.zero8[:], 0.0), w=["zero8"])
        P.emit()


def phase_mod(g, l):
    st, P, sb, ps = new_phase(g)
    NB = 8
    BW = 9 * D // NB
    with st:
        wA = [sb("wA%d" % i, [128, KC, BW], BF16) for i in range(2)]
        pm = ps("pm", (128, 72))
        src = g.ada_w[l].rearrange("(kc p) n -> p kc n", p=128)
        for blk in range(NB):
            buf = wA[blk % 2]
            for kc in range(KC):
                P.op("pool", lambda e, buf=buf, kc=kc, blk=blk: e.dma_start(out=buf[:, kc, :], in_=src[:, kc, blk * BW:(blk + 1) * BW]),
                     w=[("wA", blk % 2, kc)], dma=True)
            for j in range(BW // 128):
                col = blk * (BW // 128) + j
                for kc in range(KC):
                    _mm(P, pm[:, col:col + 1], buf[:, kc, j * 128:(j + 1) * 128], g.cact[:, kc:kc + 1], kc == 0, kc == KC - 1,
                        r=[("wA", blk % 2, kc), "cact"], w=["pm"])
        P.op("dve", lambda e: e.tensor_tensor(out=g.modT[:], in0=pm[:], in1=g.adab[:, l * 72:(l + 1) * 72], op=ALU.add),
             r=["pm"], w=["modT"])
        for v in range(3):
            sc = g.modT[:, (3 * v + 1) * 8:(3 * v + 2) * 8]
            gt = g.modT[:, (3 * v + 2) * 8:(3 * v + 3) * 8]
            lnv = g.ln[:, l * 24 + v * 8: l * 24 + v * 8 + 8]
            P.op("dve", lambda e, v=v, sc=sc, lnv=lnv: e.scalar_tensor_tensor(out=g.gm[:, v * 8:(v + 1) * 8], in0=sc, scalar=1.0, in1=lnv,
                                                                             op0=ALU.add, op1=ALU.mult), r=["modT"], w=["gm"])
            P.op("dve", lambda e, v=v, gt=gt: e.tensor_scalar(out=g.gate[:, v * 8:(v + 1) * 8], in0=gt, scalar1=(1.0 if v == 1 else 0.5),
                                                               scalar2=None, op0=ALU.mult), r=["modT"], w=["gate"])
        P.emit()


def load_w(P, dst, src, nk, tag, cols=None):
    v = src.rearrange("(kc p) n -> p kc n", p=128)
    for kc in range(nk):
        s = v[:, kc, :] if cols is None else v[:, kc, cols[0]:cols[1]]
        P.op("pool", lambda e, kc=kc, s=s: e.dma_start(out=dst[:, kc, :], in_=s), w=[(tag, kc)], dma=True)


def phase_ffn(g, l, v, h_src):
    st, P, sb, ps = new_phase(g)
    T = 256
    NT = S // T
    pre = "ffn1" if v == 0 else "ffn2"
    with st:
        wg = sb("wg", [128, KC, DFF], BF16)
        wu = sb("wu", [128, KC, DFF], BF16)
        wd = sb("wd", [128, NFF, D], BF16)
        hin = [sb("hin%d" % i, [128, KC, T], F32) for i in range(2)]
        sq = sb("sq", [128, KC, T], BF16)
        u = sb("u", [128, KC, T], BF16)
        sr = sb("sr", [128, T], F32)
        rinv = sb("rinv", [128, T], F32)
        tmp = [sb("tmp%d" % i, [128, T], F32) for i in range(2)]
        sg = [sb("sg%d" % i, [128, T], F32) for i in range(2)]
        aT = sb("aT", [128, NFF, T], BF16)
        ssum = ps("ssum")
        pgu = [ps("pgu%d" % i) for i in range(2)]
        pd = [ps("pd%d" % i) for i in range(2)]
        load_w(P, wg, g.w[pre + "_wg"][l], KC, "wg")
        load_w(P, wu, g.w[pre + "_wu"][l], KC, "wu")
        load_w(P, wd, g.w[pre + "_wd"][l], NFF, "wd")
        gm_cols = g.gm[:, v * 8:(v + 1) * 8]
        sh_cols = g.modT[:, (3 * v) * 8:(3 * v + 1) * 8]
        gate_cols = g.gate[:, v * 8:(v + 1) * 8]
        hsv = h_src.rearrange("(kc p) t -> p kc t", p=128)
        hdv = g.hT.rearrange("(kc p) t -> p kc t", p=128)
        import os
        NT = int(os.environ.get("FFN_NT", NT))
        for t in range(NT):
            hb = hin[t % 2]
            tag = "f%d" % (t % 2)
            P.op("sp", lambda e, hb=hb, t=t: e.dma_start(out=hb[:], in_=hsv[:, :, t * T:(t + 1) * T]), w=[tag + "h"], dma=True)
            emit_norm(g, P, hb, T, sq, ssum, sr, rinv, tmp, gm_cols, sh_cols, u, tag)
            for j in range(NFF):
                pb = pgu[j % 2]
                for kc in range(KC):
                    _mm(P, pb[:, 0:T], wg[:, kc, j * 128:(j + 1) * 128], u[:, kc, :], kc == 0, kc == KC - 1,
                        r=[("wg", kc), tag + "u"], w=[("pgu", j % 2)])
                for kc in range(KC):
                    _mm(P, pb[:, T:2 * T], wu[:, kc, j * 128:(j + 1) * 128], u[:, kc, :], kc == 0, kc == KC - 1,
                        r=[("wu", kc), tag + "u"], w=[("pgu", j % 2)])
                sgb = sg[j % 2]
                P.op("act", lambda e, pb=pb, sgb=sgb: e.activation(out=sgb[:], in_=pb[:, 0:T], func=AF.Silu),
                     w=[("pgu", j % 2), ("sg", j % 2)])
                P.op("dve", lambda e, pb=pb, sgb=sgb, j=j: e.tensor_tensor(out=aT[:, j, :], in0=pb[:, T:2 * T], in1=sgb[:], op=ALU.mult),
                     r=[("sg", j % 2)], w=[("pgu", j % 2), ("aT", j)])
            for m in range(KC):
                pb = pd[m % 2]
                for j in range(NFF):
                    _mm(P, pb[:, 0:T], wd[:, j, m * 128:(m + 1) * 128], aT[:, j, :], j == 0, j == NFF - 1,
                        r=[("wd", j), ("aT", j)], w=[("pd", m % 2)])
                P.op("dve", lambda e, pb=pb, m=m, hb=hb: e.scalar_tensor_tensor(out=hb[:, m, :], in0=pb[:, 0:T], scalar=gate_cols[:, m:m + 1],
                                                                               in1=hb[:, m, :], op0=ALU.mult, op1=ALU.add),
                     w=[("pd", m % 2), tag + "h"])
            P.op("sp", lambda e, hb=hb, t=t: e.dma_start(out=hdv[:, :, t * T:(t + 1) * T], in_=hb[:]), r=[tag + "h"], w=["hT_dram"], dma=True)
        P.emit()


def phase_u(g, l):
    st, P, sb, ps = new_phase(g)
    T = 512
    NT = S // T
    with st:
        hin = [sb("hin%d" % i, [128, KC, T], F32) for i in range(2)]
        sq = sb("sq", [128, KC, T], BF16)
        u = [sb("u%d" % i, [128, KC, T], BF16) for i in range(2)]
        sr = sb("sr", [128, T], F32)
        rinv = sb("rinv", [128, T], F32)
        tmp = [sb("tmp%d" % i, [128, T], F32) for i in range(2)]
        ssum = ps("ssum")
        gm_cols = g.gm[:, 8:16]
        sh_cols = g.modT[:, 24:32]
        hsv = g.hT.rearrange("(kc p) t -> p kc t", p=128)
        udv = g.uT.rearrange("(kc p) t -> p kc t", p=128)
        for t in range(NT):
            hb = hin[t % 2]
            ub = u[t % 2]
            tag = "n%d" % (t % 2)
            P.op("sp", lambda e, hb=hb, t=t: e.dma_start(out=hb[:], in_=hsv[:, :, t * T:(t + 1) * T]), w=[tag + "h"], dma=True)
            emit_norm(g, P, hb, T, sq, ssum, sr, rinv, tmp, gm_cols, sh_cols, ub, tag)
            P.op("sp", lambda e, ub=ub, t=t: e.dma_start(out=udv[:, :, t * T:(t + 1) * T], in_=ub[:]), r=[tag + "u"], dma=True)
        P.emit()


def phase_final(g):
    st, P, sb, ps = new_phase(g)
    T = 512
    NT = S // T
    with st:
        hin = [sb("hin%d" % i, [128, KC, T], F32) for i in range(2)]
        sq = sb("sq", [128, KC, T], BF16)
        o = [sb("o%d" % i, [128, KC, T], F32) for i in range(2)]
        sr = sb("sr", [128, T], F32)
        rinv = sb("rinv", [128, T], F32)
        ssum = ps("ssum")
        gm_cols = g.ln[:, DEPTH * 24:DEPTH * 24 + 8]
        hsv = g.hT.rearrange("(kc p) t -> p kc t", p=128)
        odv = g.outT.rearrange("(kc p) t -> p kc t", p=128)
        for t in range(NT):
            hb = hin[t % 2]
            ob = o[t % 2]
            tag = "n%d" % (t % 2)
            P.op("sp", lambda e, hb=hb, t=t: e.dma_start(out=hb[:], in_=hsv[:, :, t * T:(t + 1) * T]), w=[tag + "h"], dma=True)
            emit_norm(g, P, hb, T, sq, ssum, sr, rinv, None, gm_cols, None, ob, tag, out_f32=True)
            P.op("sp", lambda e, ob=ob, t=t: e.dma_start(out=odv[:, :, t * T:(t + 1) * T], in_=ob[:]), r=[tag + "u"], dma=True)
        P.emit()


def emit_norm(g, P, hin, T, sq, ssum, sr, rinv, tmp, gm_cols, sh_cols, out_tile, tag, out_f32=False):
    hflat = hin[:].rearrange("p a b -> p (a b)")
    sqflat = sq[:].rearrange("p a b -> p (a b)")
    P.op("act", lambda e: e.activation(out=sqflat, in_=hflat, func=AF.Square), r=[tag + "h"], w=["n_sq"])
    for kc in range(KC):
        _mm(P, ssum[:, 0:T], g.onesb, sq[:, kc, :], kc == 0, kc == KC - 1, r=["n_sq"], w=["n_ssum"])
    P.op("act", lambda e: e.activation(out=sr[:, 0:T], in_=ssum[:, 0:T], func=AF.Sqrt, scale=1.0 / D, bias=EPS),
         w=["n_ssum", "n_sr"])
    P.op("dve", lambda e: e.reciprocal(out=rinv[:, 0:T], in_=sr[:, 0:T]), r=["n_sr"], w=["n_rinv"])
    for kc in range(KC):
        if out_f32:
            P.op("dve", lambda e, kc=kc: e.scalar_tensor_tensor(out=out_tile[:, kc, :], in0=hin[:, kc, :], scalar=gm_cols[:, kc:kc + 1],
                                                               in1=rinv[:, 0:T], op0=ALU.mult, op1=ALU.mult),
                 r=[tag + "h", "n_rinv"], w=[tag + "u"])
        else:
            tb = tmp[kc % 2]
            P.op("dve", lambda e, kc=kc, tb=tb: e.scalar_tensor_tensor(out=tb[:, 0:T], in0=hin[:, kc, :], scalar=gm_cols[:, kc:kc + 1],
                                                                      in1=rinv[:, 0:T], op0=ALU.mult, op1=ALU.mult),
                 r=[tag + "h", "n_rinv"], w=[("n_tmp", kc % 2)])
            P.op("act", lambda e, kc=kc, tb=tb: e.activation(out=out_tile[:, kc, :], in_=tb[:, 0:T], func=AF.Identity,
                                                             bias=sh_cols[:, kc:kc + 1]),
                 r=[("n_tmp", kc % 2)], w=[tag + "u"])


AX = mybir.AxisListType


def phase_out(g, l):
    st, P, sb, ps = new_phase(g)
    T = 256
    NT = S // T
    with st:
        wa = sb("wa", [128, 8, D], BF16)
        wb = sb("wb", [128, 6, D], BF16)
        wo = sb("wo", [128, 8, D], BF16)
        wga = sb("wga", [128, 8, D], BF16)
        wgb = sb("wgb", [128, 8, D], BF16)
        load_w(P, wa, g.w["w_a"][l], 8, "wa")
        load_w(P, wb, g.w["w_b"][l], 6, "wb")
        load_w(P, wo, g.w["w_o"][l], 8, "wo")
        load_w(P, wga, g.w["w_in"][l], 8, "wga", cols=(O_GA, O_GA + D))
        load_w(P, wgb, g.w["w_in"][l], 8, "wgb", cols=(O_GB, O_GB + D))
        ut = [sb("ut%d" % i, [128, 8, T], BF16) for i in range(2)]
        oa = [sb("oa%d" % i, [128, 8, T], BF16) for i in range(2)]
        ob = [sb("ob%d" % i, [128, 6, T], BF16) for i in range(2)]
        hb_ = [sb("hb%d" % i, [128, 8, T], F32) for i in range(2)]
        mg = sb("mg", [128, 8, T], BF16)
        sgab = [sb("sgab%d" % i, [128, 2 * T], F32) for i in range(2)]
        t12 = [sb("t12%d" % i, [128, 2 * T], F32) for i in range(2)]
        bA = [ps("bA%d" % i) for i in range(2)]
        bB = [ps("bB%d" % i) for i in range(2)]
        bC = [ps("bC%d" % i) for i in range(2)]
        gate_cols = g.gate[:, 8:16]
        uv = g.uT.rearrange("(kc p) t -> p kc t", p=128)
        oav = g.oaT.rearrange("(kc p) t -> p kc t", p=128)
        obv = g.obT.rearrange("(kc p) t -> p kc t", p=128)
        hv = g.hT.rearrange("(kc p) t -> p kc t", p=128)
        for t in range(NT):
            q = t % 2
            sl = slice(t * T, (t + 1) * T)
            P.op("sp", lambda e, q=q, sl=sl: e.dma_start(out=ut[q][:], in_=uv[:, :, sl]), w=[("ut", q)], dma=True)
            P.op("sp", lambda e, q=q, sl=sl: e.dma_start(out=oa[q][:], in_=oav[:, :, sl]), w=[("oa", q)], dma=True)
            P.op("sp", lambda e, q=q, sl=sl: e.dma_start(out=ob[q][:], in_=obv[:, :, sl]), w=[("ob", q)], dma=True)
            P.op("sp", lambda e, q=q, sl=sl: e.dma_start(out=hb_[q][:], in_=hv[:, :, sl]), w=[("hb", q)], dma=True)
            for m in range(8):
                mi = m % 2
                ms = slice(m * 128, (m + 1) * 128)
                for c in range(8):
                    _mm(P, bA[mi][:, 0:T], wa[:, c, ms], oa[q][:, c, :], c == 0, c == 7, r=[("wa", c), ("oa", q)], w=[("bA", mi)])
                for c in range(6):
                    _mm(P, bA[mi][:, T:2 * T], wb[:, c, ms], ob[q][:, c, :], c == 0, c == 5, r=[("wb", c), ("ob", q)], w=[("bA", mi)])
                for c in range(8):
                    _mm(P, bB[mi][:, 0:T], wga[:, c, ms], ut[q][:, c, :], c == 0, c == 7, r=[("wga", c), ("ut", q)], w=[("bB", mi)])
                for c in range(8):
                    _mm(P, bB[mi][:, T:2 * T], wgb[:, c, ms], ut[q][:, c, :], c == 0, c == 7, r=[("wgb", c), ("ut", q)], w=[("bB", mi)])
                P.op("act", lambda e, mi=mi: e.activation(out=sgab[mi][:], in_=bB[mi][:, :], func=AF.Sigmoid), w=[("bB", mi), ("sgab", mi)])
                P.op("dve", lambda e, mi=mi: e.tensor_tensor(out=t12[mi][:], in0=bA[mi][:, :], in1=sgab[mi][:], op=ALU.mult),
                     r=[("sgab", mi)], w=[("bA", mi), ("t12", mi)])
                P.op("pool", lambda e, mi=mi, m=m: e.tensor_tensor(out=mg[:, m, :], in0=t12[mi][:, 0:T], in1=t12[mi][:, T:2 * T], op=ALU.add),
                     r=[("t12", mi)], w=[("mg", m)])
            for m in range(8):
                mi = m % 2
                ms = slice(m * 128, (m + 1) * 128)
                for c in range(8):
                    _mm(P, bC[mi][:, 0:T], wo[:, c, ms], mg[:, c, :], c == 0, c == 7, r=[("wo", c), ("mg", c)], w=[("bC", mi)])
                P.op("dve", lambda e, mi=mi, m=m, q=q: e.scalar_tensor_tensor(out=hb_[q][:, m, :], in0=bC[mi][:, 0:T], scalar=gate_cols[:, m:m + 1],
                                                                             in1=hb_[q][:, m, :], op0=ALU.mult, op1=ALU.add),
                     w=[("bC", mi), ("hb", q)])
            P.op("sp", lambda e, q=q, sl=sl: e.dma_start(out=hv[:, :, sl], in_=hb_[q][:]), r=[("hb", q)], w=["hT_dram"], dma=True)
        P.emit()


def phase_da(g, l):
    st, P, sb, ps = new_phase(g)
    import os
    NG = int(os.environ.get("DA_NG", 6))
    with st:
        uT = sb("uT", [128, KC, S], BF16)
        udv = g.uT.rearrange("(kc p) t -> p kc t", p=128)
        for kc in range(KC):
            P.op("sp", lambda e, kc=kc: e.dma_start(out=uT[:, kc, :], in_=udv[:, kc, :]), w=[("uT", kc)], dma=True)
        wq = [sb("wq%d" % i, [128, KC, 128], BF16) for i in range(2)]
        wk = [sb("wk%d" % i, [128, KC, 128], BF16) for i in range(2)]
        wv = [sb("wv%d" % i, [128, KC, 128], BF16) for i in range(2)]
        QT = sb("QT", [128, S], BF16)
        KT = sb("KT", [128, S], BF16)
        VT = sb("VT", [128, S], BF16)
        acc = sb("acc", [128, 2, S], F32)
        sqt = sb("sqt", [128, 512], BF16)
        mx = sb("mx", [128, 4], F32)
        tm = sb("tm", [128, 1], F32)
        prod = sb("prod", [128, 2], F32)
        negm = sb("negm", [128, 2], F32)
        Vaug = [sb("Vaug%d" % i, [128, 2, 128], BF16) for i in range(3)]
        sbt = [sb("sbt%d" % i, [128, 2, 256], F32) for i in range(2)]
        PT = [sb("PT%d" % i, [128, 2, 256], BF16) for i in range(2)]
        rl = sb("rl", [128, S], F32)
        rl2 = sb("rl2", [128, S], F32)
        ob = sb("ob", [128, S], BF16)
        B = [ps("B%d" % i) for i in range(7)]
        B.insert(1, None)
        pvt = ps("pvt", (128, 1024), BF16)
        pqk = B[0]
        pss = B[0]
        pst = [[B[2], B[3]], [B[4], B[5]]]
        ppv = [B[6], B[7]]
        for i in range(3):
            P.op("pool", lambda e, i=i: e.memset(Vaug[i][:], 1.0), w=[("V", i)])
        vi = 0
        bi = 0
        for gi in range(NG):
            gq = gi % 2
            w_in = g.w["w_in"][l]
            load_w(P, wq[gq], w_in, KC, ("wq", gq), cols=(O_DQ + gi * 128, O_DQ + (gi + 1) * 128))
            load_w(P, wk[gq], w_in, KC, ("wk", gq), cols=(O_DK + gi * 128, O_DK + (gi + 1) * 128))
            load_w(P, wv[gq], w_in, KC, ("wv", gq), cols=(O_DV + gi * 128, O_DV + (gi + 1) * 128))
            P.op("dve", lambda e: e.memset(mx[:], 0.0), w=["mx"])
            for t in range(8):
                sl = slice(t * 512, (t + 1) * 512)
                for (W, wn, dst, dn, mc) in ((wq[gq], "wq", QT, "QT", 0), (wk[gq], "wk", KT, "KT", 2), (wv[gq], "wv", VT, "VT", -1)):
                    for kc in range(KC):
                        _mm(P, pqk[:, :], W[:, kc, :], uT[:, kc, sl], kc == 0, kc == KC - 1, r=[((wn, gq), kc), ("uT", kc)], w=["B0"])
                    P.op("act", lambda e, dst=dst, sl=sl: e.activation(out=dst[:, sl], in_=pqk[:, :], func=AF.Copy), w=["B0", dn])
                    if mc < 0:
                        continue
                    P.op("act", lambda e: e.activation(out=sqt[:], in_=pqk[:, :], func=AF.Square), w=["B0", "sqt"])
                    for hh in range(2):
                        _mm(P, pss[:, :], g.headsel[:, hh * 128:(hh + 1) * 128], sqt[:], True, True, r=["sqt"], w=["B0"])
                        P.op("dve", lambda e: e.reduce_max(out=tm[:, 0:1], in_=pss[:, :], axis=AX.X), w=["B0", "tm"])
                        P.op("dve", lambda e, c=mc + hh: e.tensor_tensor(out=mx[:, c:c + 1], in0=mx[:, c:c + 1], in1=tm[:, 0:1], op=ALU.max),
                             r=["tm"], w=["mx"])
            P.op("dve", lambda e: e.tensor_tensor(out=prod[:], in0=mx[:, 0:2], in1=mx[:, 2:4], op=ALU.mult), r=["mx"], w=["prod"])
            P.op("act", lambda e: e.activation(out=prod[:], in_=prod[:], func=AF.Sqrt), w=["prod"])
            P.op("dve", lambda e: e.tensor_scalar(out=negm[:], in0=prod[:], scalar1=-0.125, scalar2=None, op0=ALU.mult), r=["prod"], w=["negm"])
            blocks = []
            for pi, r_ in enumerate((1, 4, 16)):
                nblk = 32 // r_
                for p_ in range(r_):
                    for b in range(nblk):
                        blocks.append((pi, r_, p_, b))

            def mk(pi, r_, p_, b, vi_, bi_):
                def tok(bb):
                    s0 = p_ + r_ * 128 * bb
                    return slice(s0, s0 + r_ * 127 + 1, r_)
                cur = vi_ % 3
                prv = (vi_ - 1) % 3
                bq = bi_ % 2
                tb = tok(b)
                lo = 128 if b == 0 else 0

                def stageA():
                    P.op("pe", lambda e: e.transpose(out=pvt[:, 0:128], in_=VT[:, tb], identity=g.identb), r=["VT"], w=["pvt"])
                    P.op("act", lambda e: e.activation(out=Vaug[cur][:, :, 64:128],
                                                       in_=pvt[:, 0:128].rearrange("p (h d) -> p h d", h=2), func=AF.Copy),
                         w=["pvt", ("V", cur)])
                    for hh in range(2):
                        rows = slice(64 * hh, 64 * hh + 64)
                        pb = pst[bq][hh]
                        btok = "B%d" % (2 + 2 * bq + hh)
                        if b > 0:
                            _mm(P, pb[:, 0:128], KT[rows, tok(b - 1)], QT[rows, tb], True, True, r=["KT", "QT"], w=[btok])
                        _mm(P, pb[:, 128:256], KT[rows, tb], QT[rows, tb], True, True, r=["KT", "QT"], w=[btok])
                        cc = -8.0 * SLOPES[gi * 2 + hh] * r_
                        P.op("dve", lambda e, pb=pb, hh=hh, cc=cc: e.scalar_tensor_tensor(
                            out=sbt[bq][:, hh, lo:256], in0=g.D2[:, lo:256], scalar=cc, in1=pb[:, lo:256], op0=ALU.mult, op1=ALU.add),
                            w=[btok, ("sbt", bq, hh)])
                        P.op("act", lambda e, hh=hh: e.activation(out=PT[bq][:, hh, lo:256], in_=sbt[bq][:, hh, lo:256],
                                                                 func=AF.Exp, scale=0.125, bias=negm[:, hh:hh + 1]),
                             r=[("sbt", bq, hh), "negm"], w=[("PT", bq, hh)])

                def stageB():
                    pp = ppv[bq]
                    ptok = "B%d" % (6 + bq)
                    for hh in range(2):
                        if b > 0:
                            _mm(P, pp[:, hh * 128:(hh + 1) * 128], Vaug[prv][:, hh, :], PT[bq][:, hh, 0:128], True, False,
                                r=[("V", prv), ("PT", bq, hh)], w=[ptok])
                        _mm(P, pp[:, hh * 128:(hh + 1) * 128], Vaug[cur][:, hh, :], PT[bq][:, hh, 128:256], b == 0, True,
                            r=[("V", cur), ("PT", bq, hh)], w=[ptok])
                    ppv3 = pp[:, 0:256].rearrange("p (h q) -> p h q", h=2)
                    if pi == 0:
                        P.op("dve", lambda e: e.tensor_copy(out=acc[:, :, tb], in_=ppv3), w=[ptok, "acc"])
                    else:
                        P.op("dve", lambda e: e.tensor_tensor(out=acc[:, :, tb], in0=ppv3, in1=acc[:, :, tb], op=ALU.add),
                             w=[ptok, "acc"])
                return stageA, stageB

            prevB = None
            for blk in blocks:
                sA, sB = mk(*blk, vi, bi)
                vi += 1
                bi += 1
                sA()
                if prevB is not None:
                    prevB()
                prevB = sB
            prevB()
            for hh in range(2):
                P.op("dve", lambda e, hh=hh: e.reciprocal(out=rl[0:64, :], in_=acc[0:64, hh, :]), r=["acc"], w=["rl"])
                P.op("act", lambda e: e.activation(out=rl2[64:128, :], in_=rl[0:64, :], func=AF.Copy), r=["rl"], w=["rl2"])
                P.op("pool", lambda e, hh=hh: e.tensor_tensor(out=ob[64:128, :], in0=acc[64:128, hh, :], in1=rl2[64:128, :], op=ALU.mult),
                     r=["acc", "rl2"], w=["ob"])
                r0 = (2 * gi + hh) * 64
                P.op("sp", lambda e, r0=r0: e.dma_start(out=g.obT[r0:r0 + 64, :], in_=ob[64:128, :]), r=["ob"], dma=True)
        P.emit()


def phase_dn(g, l):
    st, P, sb, ps = new_phase(g)
    import os
    HG = 4
    NPASS = int(os.environ.get("DN_NPASS", 2))
    NT = int(os.environ.get("DN_NT", 8))
    T = 512
    with st:
        wqkv = sb("wqkv", [128, KC, 3 * HG * 128], BF16)
        wz = sb("wz", [128, KC, HG * 128], BF16)
        wba = sb("wba", [128, KC, 16], BF16)
        ut = [sb("ut%d" % i, [128, KC, T], BF16) for i in range(2)]
        betaT = sb("betaT", [8, T], F32)
        g1 = sb("g1", [8, T], F32)
        g2 = sb("g2", [8, T], F32)
        g3 = sb("g3", [8, T], F32)
        gcT = sb("gcT", [8, T], F32)
        tk = sb("tk", [128, 4, 16], F32)
        egc = sb("egc", [128, 4, 8], F32)
        negc = sb("negc", [128, 4, 8], F32)
        bege = sb("bege", [128, 4, 8], F32)
        glb = sb("glb", [128, 4, 8], F32)
        dl = sb("dl", [128, 4, 8], F32)
        edl = sb("edl", [128, 4, 8], F32)
        egl = [sb("egl%d" % i, [128, 4, HG], F32) for i in range(2)]
        halo = sb("halo", [128, 3 * HG, 3], F32)
        xpre = [sb("xpre%d" % i, [128, T + 3], F32) for i in range(2)]
        yb = [sb("yb%d" % i, [128, T], F32) for i in range(2)]
        sbf = [sb("sbf%d" % i, [128, T], F32) for i in range(2)]
        sqh = sb("sqh", [128, T], BF16)
        ctmp = sb("ctmp", [128, T], F32)
        srn = sb("srn", [128, T], F32)
        rinvn = sb("rinvn", [128, T], F32)
        qT = [sb("qT%d" % i, [128, T], BF16) for i in range(2)]
        kT = [sb("kT%d" % i, [128, T], BF16) for i in range(2)]
        vT = [sb("vT%d" % i, [128, T], BF16) for i in range(2)]
        egcb = sb("egcb", [128, T], F32)
        tE = sb("tE", [128, 4, 128], F32)
        E4 = sb("E4", [128, 4, 128], F32)
        BBs = sb("BBs", [128, 4, 128], F32)
        EBs = sb("EBs", [128, 4, 128], F32)
        import os as _os
        DT_T = F32R if _os.environ.get("USE_F32R") else F32
        A = [sb("A%d" % i, [128, 4, 128], DT_T) for i in range(2)]
        Bm = [sb("Bm%d" % i, [128, 4, 128], DT_T) for i in range(2)]
        R = sb("R", [128, 4, 128], DT_T)
        kbg = sb("kbg", [128, 4, 128], DT_T)
        vb = sb("vb", [128, 4, 128], DT_T)
        identr = sb("identr", [128, 128], DT_T)
        P.op("act", lambda e: e.activation(out=identr[:], in_=g.identf, func=AF.Copy), w=["identr"])
        wT4 = [sb("wT4%d" % i, [128, HG, T], BF16) for i in range(2)]
        u4 = [sb("u4%d" % i, [128, HG, 4, 128], F32) for i in range(2)]
        qgT = [sb("qgT%d" % i, [128, HG, T], BF16) for i in range(2)]
        qkT4 = [sb("qkT4%d" % i, [128, HG, T], BF16) for i in range(2)]
        kdec = [sb("kdec%d" % i, [128, HG, 4, 128], BF16) for i in range(2)]
        Sf = sb("Sf", [128, HG, 128], F32)
        Sb = sb("Sb", [128, HG, 128], BF16)
        vnew = [sb("vnew%d" % i, [128, HG, 128], BF16) for i in range(2)]
        oraw = sb("oraw", [128, HG, T], F32)
        sqo = sb("sqo", [128, T], BF16)
        sro = sb("sro", [128, T], F32)
        rinvo = sb("rinvo", [128, T], F32)
        sz = sb("sz", [128, T], F32)
        on = sb("on", [128, T], F32)
        oab = [sb("oab%d" % i, [128, T], BF16) for i in range(2)]
        PA = ps("PA")
        PB = ps("PB")
        PC = [ps("PC%d" % i) for i in range(2)]
        PTt = ps("PTt", (128, 1024), BF16)
        PSW = ps("PSW")
        PSO = ps("PSO")
        PSD = ps("PSD")
        pcn = [0]

        def pc():
            i = pcn[0] % 2
            pcn[0] += 1
            return PC[i], ("PC", i)

        def c4(ap):
            return ap.rearrange("p (c i) -> p c i", c=4)

        def bc4(ap2):
            return ap2.unsqueeze(1).to_broadcast([128, 4, 128])

        def colbc(ap_c):
            return ap_c.unsqueeze(2).to_broadcast([128, 4, 128])

        w_in = g.w["w_in"][l]
        udv = g.uT.rearrange("(kc p) t -> p kc t", p=128)
        nAcol = g.nA[:, l:l + 1]
        dtbcol = g.dtb_sb[:, l:l + 1]
        dnncol = g.dnn[:, l:l + 1]

        def gates(t):
            q = t % 2
            P.op("sp", lambda e: e.dma_start(out=ut[q][:], in_=udv[:, :, t * T:(t + 1) * T]), w=[("ut", q)], dma=True)
            for kc in range(KC):
                _mm(P, PA[0:8, :], wba[:, kc, 0:8], ut[q][:, kc, :], kc == 0, kc == KC - 1, r=["wba", ("ut", q)], w=["PA"])
            P.op("act", lambda e: e.activation(out=betaT[:], in_=PA[0:8, :], func=AF.Sigmoid), w=["PA", "betaT"])
            for kc in range(KC):
                _mm(P, PA[0:8, :], wba[:, kc, 8:16], ut[q][:, kc, :], kc == 0, kc == KC - 1, r=["wba", ("ut", q)], w=["PA"])
            P.op("dve", lambda e: e.tensor_scalar(out=g1[:], in0=PA[0:8, :], scalar1=dtbcol, scalar2=None, op0=ALU.add), w=["PA", "g1"])
            P.op("dve", lambda e: e.tensor_scalar(out=g2[:], in0=g1[:], scalar1=-1.0, scalar2=None, op0=ALU.mult), r=["g1"], w=["g2"])
            P.op("dve", lambda e: e.tensor_tensor(out=g2[:], in0=g2[:], in1=g1[:], op=ALU.max), r=["g1"], w=["g2"])
            P.op("act", lambda e: e.activation(out=g2[:], in_=g2[:], func=AF.Exp, scale=-1.0), w=["g2"])
            P.op("act", lambda e: e.activation(out=g2[:], in_=g2[:], func=AF.Ln, bias=1.0), w=["g2"])
            P.op("dve", lambda e: e.tensor_scalar(out=g1[:], in0=g1[:], scalar1=0.0, scalar2=None, op0=ALU.max), w=["g1"])
            P.op("dve", lambda e: e.tensor_tensor(out=g1[:], in0=g1[:], in1=g2[:], op=ALU.add), r=["g2"], w=["g1"])
            P.op("dve", lambda e: e.tensor_scalar(out=g3[:], in0=g1[:], scalar1=nAcol, scalar2=None, op0=ALU.mult), r=["g1"], w=["g3"])
            P.op("dve", lambda e: e.tensor_tensor_scan(out=gcT[:], data0=g.resetm, data1=g3[:], initial=0.0, op0=ALU.mult, op1=ALU.add),
                 r=["g3"], w=["gcT"])
            for c in range(4):
                cs = slice(c * 128, (c + 1) * 128)
                _mm(P, PB[:, c * 16:c * 16 + 8], gcT[0:8, cs], g.identf[0:8, 0:8], True, True, r=["gcT"], w=["PB"])
                _mm(P, PB[:, c * 16 + 8:c * 16 + 16], betaT[0:8, cs], g.identf[0:8, 0:8], True, True, r=["betaT"], w=["PB"])
            tkf = tk[:].rearrange("p c k -> p (c k)")
            P.op("act", lambda e: e.activation(out=tkf, in_=PB[:, 0:64], func=AF.Copy), w=["PB", "tk"])
            P.op("act", lambda e: e.activation(out=egc[:], in_=tk[:, :, 0:8], func=AF.Exp), r=["tk"], w=["egc"])
            P.op("dve", lambda e: e.tensor_scalar(out=negc[:], in0=tk[:, :, 0:8], scalar1=-1.0, scalar2=None, op0=ALU.mult), r=["tk"], w=["negc"])
            P.op("dve", lambda e: e.tensor_tensor(out=bege[:], in0=tk[:, :, 8:16], in1=egc[:], op=ALU.mult), r=["tk", "egc"], w=["bege"])

        def pre(t, hp, hl):
            h = hp * HG + hl
            q = t % 2
            ws = hl % 2
            for part in range(3):
                fcl = part * HG + hl
                fcg = part * 8 + h
                xi = part % 2
                xp = xpre[xi]
                for kc in range(KC):
                    _mm(P, PA[:, :], wqkv[:, kc, fcl * 128:(fcl + 1) * 128], ut[q][:, kc, :], kc == 0, kc == KC - 1,
                        r=[("wqkv", kc), ("ut", q)], w=["PA"])
                P.op("pool", lambda e, xp=xp, fcl=fcl: e.tensor_copy(out=xp[:, 0:3], in_=halo[:, fcl, :]), r=[("halo", fcl)], w=[("xpre", xi)])
                P.op("act", lambda e, xp=xp: e.activation(out=xp[:, 3:T + 3], in_=PA[:, :], func=AF.Copy), w=["PA", ("xpre", xi)])
                P.op("pool", lambda e, xp=xp, fcl=fcl: e.tensor_copy(out=halo[:, fcl, :], in_=xp[:, T:T + 3]), r=[("xpre", xi)], w=[("halo", fcl)])
                y = yb[xi]
                cwb = l * 96 + fcg * 4
                P.op("dve", lambda e, xp=xp, y=y, cwb=cwb: e.tensor_scalar(out=y[:], in0=xp[:, 0:T], scalar1=g.convw[:, cwb:cwb + 1], scalar2=None,
                                                                       op0=ALU.mult), r=[("xpre", xi)], w=[("y", xi)])
                for j in range(1, 4):
                    P.op("dve", lambda e, xp=xp, y=y, cwb=cwb, j=j: e.scalar_tensor_tensor(out=y[:], in0=xp[:, j:j + T],
                                                                                      scalar=g.convw[:, cwb + j:cwb + j + 1],
                                                                                      in1=y[:], op0=ALU.mult, op1=ALU.add),
                         r=[("xpre", xi)], w=[("y", xi)])
                if part == 2:
                    P.op("act", lambda e, y=y: e.activation(out=vT[ws][:], in_=y[:], func=AF.Silu), r=[("y", xi)], w=[("vT", ws)])
                else:
                    s_ = sbf[xi]
                    dst = qT[ws] if part == 0 else kT[ws]
                    dn_ = ("qT", ws) if part == 0 else ("kT", ws)
                    P.op("act", lambda e, y=y, s_=s_: e.activation(out=s_[:], in_=y[:], func=AF.Silu), r=[("y", xi)], w=[("sbf", xi)])
                    P.op("act", lambda e, s_=s_: e.activation(out=sqh[:], in_=s_[:], func=AF.Square), r=[("sbf", xi)], w=["sqh"])
                    _mm(P, PB[:, :], g.onesb, sqh[:], True, True, r=["sqh"], w=["PB"])
                    scl = 128.0 if part == 0 else 1.0
                    P.op("act", lambda e, scl=scl: e.activation(out=srn[:], in_=PB[:, :], func=AF.Ln, scale=scl, bias=EPS * scl), w=["PB", "srn"])
                    P.op("act", lambda e: e.activation(out=rinvn[:], in_=srn[:], func=AF.Exp, scale=-0.5), r=["srn"], w=["rinvn"])
                    P.op("dve", lambda e, s_=s_, dst=dst: e.tensor_tensor(out=dst[:], in0=s_[:], in1=rinvn[:], op=ALU.mult),
                         r=[("sbf", xi), "rinvn"], w=[dn_])
            selh = g.sel[:, h * 128:(h + 1) * 128]
            _mm(P, PB[:, :], selh, gcT[:], True, True, r=["gcT"], w=["PB"])
            P.op("act", lambda e: e.activation(out=glb[:, :, h], in_=PB[:, 127:512:128], func=AF.Copy), w=["PB", ("glb", h)])
            P.op("act", lambda e: e.activation(out=egcb[:], in_=PB[:, :], func=AF.Exp), w=["PB", "egcb"])
            P.op("dve", lambda e: e.tensor_tensor(out=tE[:], in0=c4(PB[:, :]), in1=bc4(g.negmask), op=ALU.add), w=["PB", "tE"])
            P.op("pool", lambda e: e.tensor_tensor(out=qgT[q][:, hl, :], in0=qT[ws][:], in1=egcb[:], op=ALU.mult),
                 r=[("qT", ws), "egcb"], w=[("qg", q, hl)])
            P.op("dve", lambda e: e.tensor_tensor(out=dl[:, :, h], in0=glb[:, :, h], in1=tk[:, :, h], op=ALU.subtract),
                 r=[("glb", h), "tk"], w=[("dl", h)])
            P.op("act", lambda e: e.activation(out=edl[:, :, h], in_=dl[:, :, h], func=AF.Exp), r=[("dl", h)], w=[("edl", h)])
            P.op("act", lambda e: e.activation(out=egl[q][:, :, hl], in_=glb[:, :, h], func=AF.Exp), r=[("glb", h)], w=[("egl", q, hl)])
            for c in range(4):
                P.op("act", lambda e, c=c: e.activation(out=E4[:, c, :], in_=tE[:, c, :], func=AF.Exp, bias=negc[:, c, h:h + 1]),
                     r=["tE", "negc"], w=["E4"])
            _mm(P, PB[:, :], selh, betaT[:], True, True, r=["betaT"], w=["PB"])
            P.op("dve", lambda e: e.tensor_tensor(out=BBs[:], in0=c4(PB[:, :]), in1=bc4(g.strict01), op=ALU.mult), w=["PB", "BBs"])
            P.op("pool", lambda e: e.tensor_tensor(out=EBs[:], in0=E4[:], in1=BBs[:], op=ALU.mult), r=["E4", "BBs"], w=["EBs"])
            for c in range(4):
                cs = slice(c * 128, (c + 1) * 128)
                P.op("pe", lambda e, cs=cs: e.transpose(out=PTt[:, cs], in_=kT[ws][:, cs], identity=g.identb), r=[("kT", ws)], w=["PT"])
            for c in range(4):
                cs = slice(c * 128, (c + 1) * 128)
                P.op("pe", lambda e, cs=cs, c=c: e.transpose(out=PTt[:, 512 + c * 128:512 + (c + 1) * 128], in_=vT[ws][:, cs], identity=g.identb),
                     r=[("vT", ws)], w=["PT"])
            P.op("dve", lambda e: e.tensor_tensor(out=kbg[:], in0=c4(PTt[:, 0:512]), in1=colbc(bege[:, :, h]), op=ALU.mult),
                 r=["bege"], w=["PT", "kbg"])
            P.op("dve", lambda e: e.tensor_tensor(out=kdec[q][:, hl, :, :], in0=c4(PTt[:, 0:512]), in1=colbc(edl[:, :, h]), op=ALU.mult),
                 r=[("edl", h)], w=["PT", ("kdec", q, hl)])
            P.op("dve", lambda e: e.tensor_tensor(out=vb[:], in0=c4(PTt[:, 512:1024]), in1=colbc(tk[:, :, 8 + h]), op=ALU.mult),
                 r=["tk"], w=["PT", "vb"])
            pkk, tkk = pc()
            for c in range(4):
                cs = slice(c * 128, (c + 1) * 128)
                _mm(P, pkk[:, cs], kT[ws][:, cs], kT[ws][:, cs], True, True, r=[("kT", ws)], w=[tkk])
            pqk, tqk = pc()
            for c in range(4):
                cs = slice(c * 128, (c + 1) * 128)
                _mm(P, pqk[:, cs], kT[ws][:, cs], qT[ws][:, cs], True, True, r=[("kT", ws), ("qT", ws)], w=[tqk])
            P.op("dve", lambda e: e.tensor_tensor(out=A[0][:], in0=c4(pkk[:, :]), in1=EBs[:], op=ALU.mult), r=["EBs"], w=[tkk, ("A", 0)])
            P.op("dve", lambda e: e.tensor_tensor(out=c4(qkT4[q][:, hl, :]), in0=c4(pqk[:, :]), in1=E4[:], op=ALU.mult),
                 r=["E4"], w=[tqk, ("qk", q, hl)])
            pt0, tt0 = pc()
            for c in range(4):
                cs = slice(c * 128, (c + 1) * 128)
                _mmr(P, pt0[:, cs], A[0][:, c, :], identr[:], True, True, r=[("A", 0), "identr"], w=[tt0])
            P.op("act", lambda e, pt0=pt0: e.activation(out=Bm[0][:], in_=c4(pt0[:, :]), func=AF.Copy), w=[tt0, ("B", 0)])
            P.op("dve", lambda e: e.scalar_tensor_tensor(out=R[:], in0=A[0][:], scalar=-1.0, in1=bc4(g.identf), op0=ALU.mult, op1=ALU.add),
                 r=[("A", 0)], w=["R"])
            for k in range(1, 7):
                ap_, bp_ = (k - 1) % 2, (k - 1) % 2
                an_, bn_ = k % 2, k % 2
                if k <= 5:
                    px, tx = pc()
                    for c in range(4):
                        cs = slice(c * 128, (c + 1) * 128)
                        _mmr(P, px[:, cs], Bm[bp_][:, c, :], A[ap_][:, c, :], True, True, r=[("B", bp_), ("A", ap_)], w=[tx])
                if k <= 5:
                    P.op("act", lambda e, px=px, an_=an_: e.activation(out=A[an_][:], in_=c4(px[:, :]), func=AF.Copy), w=[tx, ("A", an_)])
                    py, ty = pc()
                    for c in range(4):
                        cs = slice(c * 128, (c + 1) * 128)
                        P.op("pe", lambda e, py=py, cs=cs, c=c, an_=an_: e.transpose(out=py[:, cs], in_=A[an_][:, c, :], identity=g.identf),
                             r=[("A", an_)], w=[ty])
                else:
                    py, ty = pc()
                    for c in range(4):
                        cs = slice(c * 128, (c + 1) * 128)
                        _mmr(P, py[:, cs], A[ap_][:, c, :], Bm[bp_][:, c, :], True, True, r=[("B", bp_), ("A", ap_)], w=[ty])
                P.op("act", lambda e, py=py, bn_=bn_: e.activation(out=Bm[bn_][:], in_=c4(py[:, :]), func=AF.Copy), w=[ty, ("B", bn_)])
                pz, tz = pc()
                for c in range(4):
                    cs = slice(c * 128, (c + 1) * 128)
                    _mmr(P, pz[:, cs], Bm[bn_][:, c, :], R[:, c, :], True, True, r=[("B", bn_), "R"], w=[tz])
                P.op("dve", lambda e, pz=pz: e.tensor_tensor(out=R[:], in0=c4(pz[:, :]), in1=R[:], op=ALU.add), w=[tz, "R"])
            pw, tw = pc()
            for c in range(4):
                cs = slice(c * 128, (c + 1) * 128)
                _mmr(P, pw[:, cs], kbg[:, c, :], R[:, c, :], True, True, r=["kbg", "R"], w=[tw])
            P.op("act", lambda e, pw=pw: e.activation(out=wT4[q][:, hl, :], in_=pw[:, :], func=AF.Copy), w=[tw, ("wT", q, hl)])
            pu, tu = pc()
            for c in range(4):
                cs = slice(c * 128, (c + 1) * 128)
                _mmr(P, pu[:, cs], R[:, c, :], vb[:, c, :], True, True, r=["vb", "R"], w=[tu])
            P.op("act", lambda e, pu=pu: e.activation(out=u4[q][:, hl, :, :], in_=c4(pu[:, :]), func=AF.Copy), w=[tu, ("u4", q, hl)])

        def scan_step(t, hp, c):
            q = t % 2
            vq = c % 2
            cs = slice(c * 128, (c + 1) * 128)
            for hl in range(HG):
                _mm(P, PSW[:, hl * 128:(hl + 1) * 128], wT4[q][:, hl, cs], Sb[:, hl, :], True, True, r=[("wT", q, hl), "Sb"], w=["PSW"])
            P.op("dve", lambda e: e.tensor_tensor(out=vnew[vq][:], in0=u4[q][:, :, c, :], in1=PSW[:, :].rearrange("p (h e) -> p h e", h=HG),
                                                  op=ALU.subtract),
                 r=[("u4", q, hl) for hl in range(HG)], w=["PSW", ("vnew", vq)])
            for hl in range(HG):
                _mm(P, PSO[:, hl * 128:(hl + 1) * 128], Sb[:, hl, :], qgT[q][:, hl, cs], True, False, r=[("qg", q, hl), "Sb"], w=["PSO"])
                _mm(P, PSO[:, hl * 128:(hl + 1) * 128], vnew[vq][:, hl, :], qkT4[q][:, hl, cs], False, True,
                    r=[("qk", q, hl), ("vnew", vq)], w=["PSO"])
            for hl in range(HG):
                _mm(P, PSD[:, hl * 128:(hl + 1) * 128], kdec[q][:, hl, c, :], vnew[vq][:, hl, :], True, True,
                    r=[("kdec", q, hl), ("vnew", vq)], w=["PSD"])
            for hl in range(HG):
                P.op("dve", lambda e, hl=hl: e.scalar_tensor_tensor(out=Sf[:, hl, :], in0=Sf[:, hl, :], scalar=egl[q][:, c, hl:hl + 1],
                                                                   in1=PSD[:, hl * 128:(hl + 1) * 128], op0=ALU.mult, op1=ALU.add),
                     r=[("egl", q, hl)], w=["PSD", "Sf"])
            P.op("pool", lambda e: e.tensor_copy(out=Sb[:], in_=Sf[:]), r=["Sf"], w=["Sb"])
            P.op("act", lambda e: e.activation(out=oraw[:, :, cs], in_=PSO[:, :].rearrange("p (h i) -> p h i", h=HG), func=AF.Copy),
                 w=["PSO", "oraw"])

        def post(t, hp, hl):
            h = hp * HG + hl
            q = t % 2
            P.op("act", lambda e: e.activation(out=sqo[:], in_=oraw[:, hl, :], func=AF.Square), r=["oraw"], w=["sqo"])
            _mm(P, PB[:, :], g.onesb, sqo[:], True, True, r=["sqo"], w=["PB"])
            P.op("act", lambda e: e.activation(out=sro[:], in_=PB[:, :], func=AF.Ln, scale=1.0 / 128.0, bias=EPS), w=["PB", "sro"])
            P.op("act", lambda e: e.activation(out=rinvo[:], in_=sro[:], func=AF.Exp, scale=-0.5), r=["sro"], w=["rinvo"])
            for kc in range(KC):
                _mm(P, PA[:, :], wz[:, kc, hl * 128:(hl + 1) * 128], ut[q][:, kc, :], kc == 0, kc == KC - 1, r=[("wz", kc), ("ut", q)], w=["PA"])
            P.op("act", lambda e: e.activation(out=sz[:], in_=PA[:, :], func=AF.Silu), w=["PA", "sz"])
            P.op("dve", lambda e: e.tensor_tensor(out=on[:], in0=oraw[:, hl, :], in1=rinvo[:], op=ALU.mult), r=["oraw", "rinvo"], w=["on"])
            ob_ = oab[hl % 2]
            P.op("dve", lambda e: e.scalar_tensor_tensor(out=ob_[:], in0=on[:], scalar=dnncol, in1=sz[:], op0=ALU.mult, op1=ALU.mult),
                 r=["on", "sz"], w=[("oab", hl % 2)])
            P.op("sp", lambda e: e.dma_start(out=g.oaT[h * 128:(h + 1) * 128, t * T:(t + 1) * T], in_=ob_[:]), r=[("oab", hl % 2)], dma=True)

        load_w(P, wba, w_in, KC, "wba_", cols=(O_B, O_B + 16))
        P.op("pool", lambda e: e.memset(g1[:], 0.0), r=[("wba_", kc) for kc in range(KC)], w=["wba"])
        for hp in range(NPASS):
            v3 = w_in.rearrange("(kc p) n -> p kc n", p=128)
            for part in range(3):
                for kc in range(KC):
                    c0 = part * 1024 + hp * HG * 128
                    P.op("pool", lambda e, part=part, kc=kc, c0=c0: e.dma_start(out=wqkv[:, kc, part * HG * 128:(part + 1) * HG * 128],
                                                                              in_=v3[:, kc, c0:c0 + HG * 128]), w=[("wqkv", kc)], dma=True)
            load_w(P, wz, w_in, KC, "wz", cols=(O_Z + hp * HG * 128, O_Z + (hp + 1) * HG * 128))
            P.op("pool", lambda e: e.memset(halo[:], 0.0), w=[("halo", i) for i in range(3 * HG)])
            P.op("pool", lambda e: e.memset(Sf[:], 0.0), w=["Sf"])
            P.op("pool", lambda e: e.memset(Sb[:], 0.0), w=["Sb"])
            gates(0)
            for hl in range(HG):
                pre(0, hp, hl)
            for t in range(NT):
                if t + 1 < NT:
                    gates(t + 1)
                for c in range(4):
                    scan_step(t, hp, c)
                    if t + 1 < NT:
                        pre(t + 1, hp, c)
                for hl in range(HG):
                    post(t, hp, hl)
        P.emit()


def host_consts():
    cf = np.zeros((128, 2176), np.float32)
    j = np.arange(128)[:, None]
    i = np.arange(128)[None, :]
    cf[:, 0:128] = np.eye(128, dtype=np.float32)
    cf[:, 128:256] = np.where(j <= i, 0.0, NEG)
    cf[:, 256:384] = (j < i).astype(np.float32)
    k = j
    q = i
    prev = np.where(q <= k, 128.0 + q - k, BIGD)
    cur = np.where(q >= k, (q - k) * 1.0, BIGD)
    cf[:, 384:512] = prev
    cf[:, 512:640] = cur
    for h in range(8):
        cf[h, 640 + h * 128: 640 + (h + 1) * 128] = 1.0
    rm = np.ones((512,), np.float32)
    rm[0::128] = 0.0
    cf[0:8, 1664:2176] = rm[None, :]
    cb = np.zeros((128, 512), np.float32)
    cb[:, 0:128] = np.eye(128, dtype=np.float32)
    cb[:, 128:256] = 1.0
    cb[0:64, 256:384] = 1.0
    cb[64:128, 384:512] = 1.0
    return cf, cb


def make_in_maps(inputs, ncores=8):
    f = lambda a: np.ascontiguousarray(np.asarray(a, dtype=np.float32))
    cf, cb = host_consts()
    shared = {}
    shared["ada_w"] = f(inputs["ada_w"])
    shared["ada_bT"] = f(np.asarray(inputs["ada_b"]).reshape(DEPTH, 72, 128).transpose(2, 0, 1).reshape(128, DEPTH * 72))
    ln = np.stack([np.asarray(inputs["ln_ffn1"]), np.asarray(inputs["ln_mix"]), np.asarray(inputs["ln_ffn2"])], axis=1)
    lnT = ln.reshape(DEPTH, 3, 8, 128).transpose(3, 0, 1, 2).reshape(128, DEPTH * 24)
    fn = np.asarray(inputs["final_norm"]).reshape(8, 128).T
    shared["lnT"] = f(np.concatenate([lnT, fn], axis=1))
    cw = np.asarray(inputs["conv_w"])
    shared["conv_wT"] = f(cw.reshape(DEPTH, 4, 24, 128).transpose(3, 0, 2, 1).reshape(128, DEPTH * 96))
    shared["dnnT"] = f(np.asarray(inputs["dn_norm"]).T)
    shared["alog"] = f(np.asarray(inputs["a_log"]).T)
    shared["dtb"] = f(np.asarray(inputs["dt_bias"]).T)
    for nm in ("ffn1_wg", "ffn1_wu", "ffn1_wd", "ffn2_wg", "ffn2_wu", "ffn2_wd", "w_in", "w_a", "w_b", "w_o"):
        shared[nm] = f(inputs[nm])
    shared["cf"] = cf
    shared["cb"] = cb
    x = np.asarray(inputs["x"], dtype=np.float32)
    c = np.asarray(inputs["c"], dtype=np.float32)
    maps = []
    for b in range(ncores):
        m = dict(shared)
        m["xT"] = np.ascontiguousarray(x[b].T)
        m["c_col"] = np.ascontiguousarray(c[b].reshape(8, 128).T)
        maps.append(m)
    return maps


def kernel(**inputs):
    nc = bass.Bass("TRN2", target_bir_lowering=False)
    build(nc)
    maps = make_in_maps(inputs, 8)
    res = run_bass_kernel_spmd(nc, maps, core_ids=list(range(8)))
    out = np.stack([np.ascontiguousarray(r["outT"].T) for r in res.results], axis=0)
    return out.astype(np.float32)
```

```python
import numpy as np
import ml_dtypes
import concourse.bass as bass
import concourse.mybir as mybir
from concourse.bass_utils import run_bass_kernel_spmd
from contextlib import ExitStack

F32 = mybir.dt.float32
BF16 = mybir.dt.bfloat16
AF = mybir.ActivationFunctionType
ALU = mybir.AluOpType

D = 1024
S = 4096
KC = 8
DFF = 2816
NFF = 22
DEPTH = 2
INC = 8464
O_Z, O_B, O_A, O_DQ, O_DK, O_DV, O_GA, O_GB = 3072, 4096, 4104, 4112, 4880, 5648, 6416, 7440
EPS = 1e-6
NEG = -1.0e9
BIGD = 1.0e5
SLOPES = [2.0 ** (-8.0 * (h + 1) / 12.0) for h in range(12)]


ENGS = ("pe", "act", "dve", "pool", "sp")
ENGOBJ = {"pe": "tensor", "act": "scalar", "dve": "vector", "pool": "gpsimd", "sp": "sync"}
EPOCH = 12000
NDMASEM = 24


class Sems:
    def __init__(self, nc, stack):
        self.nc, self.stack = nc, stack
        self.sems = {}
        self.cnt = {e: 0 for e in ENGS}
        self.ndma = 0
        self.lastdma = {}

    def get(self, key):
        if key not in self.sems:
            self.sems[key] = self.stack.enter_context(self.nc.semaphore("s_%s_%s" % key))
        return self.sems[key]


class Prog:
    def __init__(self, nc, sems):
        self.nc, self.S = nc, sems
        self.ops = []
        self.last_w = {}
        self.readers = {}

    def op(self, eng, fn, r=(), w=(), dma=False):
        i = len(self.ops)
        deps = set()
        for b in r:
            lw = self.last_w.get(b)
            if lw is not None:
                deps.add(lw)
        for b in w:
            lw = self.last_w.get(b)
            if lw is not None:
                deps.add(lw)
            for x in self.readers.get(b, ()):
                deps.add(x)
        deps.discard(i)
        for b in r:
            self.readers.setdefault(b, []).append(i)
        for b in w:
            self.last_w[b] = i
            self.readers[b] = []
        self.ops.append(dict(eng=eng, fn=fn, deps=deps, dma=dma))
        return i

    def emit(self):
        nc, S, ops = self.nc, self.S, self.ops
        need = [False] * len(ops)
        for i, o in enumerate(ops):
            if o["dma"]:
                need[i] = True
            for d in o["deps"]:
                od = ops[d]
                if od["dma"] or o["dma"] or od["eng"] != o["eng"]:
                    need[d] = True
        for i, o in enumerate(ops):
            if o["dma"]:
                k = S.ndma % NDMASEM
                o["sem"] = S.get(("dma", k))
                o["val"] = 16 * (S.ndma // NDMASEM + 1)
                o["prev"] = S.lastdma.get(k)
                S.lastdma[k] = (o["sem"], o["val"])
                S.ndma += 1
            elif need[i]:
                e = o["eng"]
                o["sem"] = S.get((e, S.cnt[e] // EPOCH))
                o["val"] = S.cnt[e] % EPOCH + 1
                S.cnt[e] += 1
        per = {e: [] for e in ENGS}
        for i, o in enumerate(ops):
            per[o["eng"]].append(i)
        final_dma = list(S.lastdma.values())
        with nc.Block() as block:
            for e in ENGS:
                lst = per[e]
                if not lst and e != "sp":
                    continue

                def body(eng, lst=lst, e=e):
                    waited = {}

                    def wait(sem, val):
                        k = id(sem)
                        if waited.get(k, 0) >= val:
                            return
                        eng.wait_ge(sem, val)
                        waited[k] = val

                    for i in lst:
                        o = ops[i]
                        for d in sorted(o["deps"]):
                            od = ops[d]
                            if od["dma"] or o["dma"] or od["eng"] != e:
                                wait(od["sem"], od["val"])
                        if o["dma"] and o["prev"] is not None:
                            wait(*o["prev"])
                        ins = o["fn"](eng)
                        if o["dma"]:
                            ins.then_inc(o["sem"], 16)
                        elif need[i]:
                            ins.then_inc(o["sem"], 1)
                    if e == "sp":
                        for sem, val in final_dma:
                            wait(sem, val)

                getattr(block, ENGOBJ[e])(body)


class Ctx:
    pass


def _mm(P, out, lhsT, rhs, start, stop, r, w):
    P.op("pe", lambda e: e.matmul(out, lhsT=lhsT, rhs=rhs, start=start, stop=stop), r=r, w=w)


F32R = mybir.dt.float32r


def _mmr(P, out, lhsT, rhs, start, stop, r, w):
    import os
    P.op("pe", lambda e: e.matmul(out, lhsT=lhsT, rhs=rhs, start=start, stop=stop), r=r, w=w)


def build(nc, debug=False, phases=None):
    g = Ctx()
    g.nc = nc
    dk = "ExternalOutput" if debug else "Internal"

    def din(name, shape, dt=F32):
        return nc.dram_tensor(name, shape, dt, kind="ExternalInput").ap()

    g.xT = din("xT", [D, S])
    g.c_col = din("c_col", [128, KC])
    g.ada_w = din("ada_w", [DEPTH, D, 9 * D])
    g.ada_bT = din("ada_bT", [128, DEPTH * 72])
    g.lnT = din("lnT", [128, DEPTH * 24 + 8])
    g.conv_wT = din("conv_wT", [128, DEPTH * 96])
    g.dnnT = din("dnnT", [128, DEPTH])
    g.alog = din("alog", [8, DEPTH])
    g.dtb = din("dtb", [8, DEPTH])
    g.w = {}
    for nm, shp in (("ffn1_wg", [DEPTH, D, DFF]), ("ffn1_wu", [DEPTH, D, DFF]), ("ffn1_wd", [DEPTH, DFF, D]),
                    ("ffn2_wg", [DEPTH, D, DFF]), ("ffn2_wu", [DEPTH, D, DFF]), ("ffn2_wd", [DEPTH, DFF, D]),
                    ("w_in", [DEPTH, D, INC]), ("w_a", [DEPTH, D, D]), ("w_b", [DEPTH, 768, D]), ("w_o", [DEPTH, D, D])):
        g.w[nm] = din(nm, shp)
    g.cf = din("cf", [128, 2176])
    g.cb = din("cb", [128, 512])
    g.outT = nc.dram_tensor("outT", [D, S], F32, kind="ExternalOutput").ap()
    g.hT = nc.dram_tensor("hT", [D, S], F32, kind=dk).ap()
    g.uT = nc.dram_tensor("uT", [D, S], BF16, kind=dk).ap()
    g.oaT = nc.dram_tensor("oaT", [D, S], BF16, kind=dk).ap()
    g.obT = nc.dram_tensor("obT", [768, S], BF16, kind=dk).ap()
    if debug:
        g.dbg = nc.dram_tensor("dbg", [128, 4096], F32, kind="ExternalOutput").ap()

    with ExitStack() as gst:
        g.sems = Sems(nc, gst)

        def gsb(name, shape, dt):
            return gst.enter_context(nc.sbuf_tensor(name, shape, dt))

        g.cf_sb = gsb("cf_sb", [128, 2176], F32)
        g.cb_sb = gsb("cb_sb", [128, 512], BF16)
        g.identf = g.cf_sb[:, 0:128]
        g.negmask = g.cf_sb[:, 128:256]
        g.strict01 = g.cf_sb[:, 256:384]
        g.D2 = g.cf_sb[:, 384:640]
        g.sel = g.cf_sb[0:8, 640:1664]
        g.resetm = g.cf_sb[0:8, 1664:2176]
        g.identb = g.cb_sb[:, 0:128]
        g.onesb = g.cb_sb[:, 128:256]
        g.headsel = g.cb_sb[:, 256:512]
        g.ccol = gsb("ccol", [128, KC], F32)
        g.cact = gsb("cact", [128, KC], BF16)
        g.adab = gsb("adab", [128, DEPTH * 72], F32)
        g.ln = gsb("ln", [128, DEPTH * 24 + 8], F32)
        g.convw = gsb("convw", [128, DEPTH * 96], F32)
        g.dnn = gsb("dnn", [128, DEPTH], F32)
        g.alog_sb = gsb("alog_sb", [8, DEPTH], F32)
        g.dtb_sb = gsb("dtb_sb", [8, DEPTH], F32)
        g.nA = gsb("nA", [8, DEPTH], F32)
        g.modT = gsb("modT", [128, 72], F32)
        g.gm = gsb("gm", [128, 24], F32)
        g.gate = gsb("gate", [128, 24], F32)
        g.zero8 = gsb("zero8", [128, 8], F32)

        plist = []
        plist.append(("setup", lambda: phase_setup(g)))
        for l in range(DEPTH):
            plist.append(("mod%d" % l, lambda l=l: phase_mod(g, l)))
            plist.append(("ffn1_%d" % l, lambda l=l: phase_ffn(g, l, 0, g.xT if l == 0 else g.hT)))
            plist.append(("u%d" % l, lambda l=l: phase_u(g, l)))
            plist.append(("dn%d" % l, lambda l=l: phase_dn(g, l)))
            plist.append(("da%d" % l, lambda l=l: phase_da(g, l)))
            plist.append(("out%d" % l, lambda l=l: phase_out(g, l)))
            plist.append(("ffn2_%d" % l, lambda l=l: phase_ffn(g, l, 2, g.hT)))
        plist.append(("final", lambda: phase_final(g)))
        for name, fn in plist:
            if phases is None or name in phases:
                fn()
    return nc


def new_phase(g):
    st = ExitStack()
    P = Prog(g.nc, g.sems)
    g.pid = getattr(g, "pid", 0) + 1
    pid = g.pid

    def sb(name, shape, dt):
        return st.enter_context(g.nc.sbuf_tensor("p%d_%s" % (pid, name), shape, dt))

    def ps(name, shape=(128, 512), dt=F32):
        return st.enter_context(g.nc.psum_tensor("p%d_%s" % (pid, name), list(shape), dt))

    return st, P, sb, ps


def phase_setup(g):
    st, P, sb, ps = new_phase(g)
    with st:
        P.op("sp", lambda e: e.dma_start(out=g.cf_sb[:], in_=g.cf), w=["cf"], dma=True)
        P.op("pool", lambda e: e.dma_start(out=g.cb_sb[:], in_=g.cb), w=["cb"], dma=True)
        for nm, dst, src in (("ccol", g.ccol, g.c_col), ("adab", g.adab, g.ada_bT), ("ln", g.ln, g.lnT),
                             ("convw", g.convw, g.conv_wT), ("dnn", g.dnn, g.dnnT),
                             ("alog", g.alog_sb, g.alog), ("dtb", g.dtb_sb, g.dtb)):
            P.op("sp", lambda e, dst=dst, src=src: e.dma_start(out=dst[:], in_=src), w=[nm], dma=True)
        P.op("act", lambda e: e.activation(out=g.cact[:], in_=g.ccol[:], func=AF.Silu), r=["ccol"], w=["cact"])
        P.op("act", lambda e: e.activation(out=g.nA[:], in_=g.alog_sb[:], func=AF.Exp), r=["alog"], w=["nA"])
        P.op("dve", lambda e: e.tensor_scalar(out=g.nA[:], in0=g.nA[:], scalar1=-1.0, scalar2=None, op0=ALU.mult), r=["nA"], w=["nA"])
        P.op("dve", lambda e: e.memset(g.zero8[:], 0.0), w=["zero8"])
        P.emit()


def phase_mod(g, l):
    st, P, sb, ps = new_phase(g)
    NB = 8
    BW = 9 * D // NB
    with st:
        wA = [sb("wA%d" % i, [128, KC, BW], BF16) for i in range(2)]
        pm = ps("pm", (128, 72))
        src = g.ada_w[l].rearrange("(kc p) n -> p kc n", p=128)
        for blk in range(NB):
            buf = wA[blk % 2]
            for kc in range(KC):
                P.op("pool", lambda e, buf=buf, kc=kc, blk=blk: e.dma_start(out=buf[:, kc, :], in_=src[:, kc, blk * BW:(blk + 1) * BW]),
                     w=[("wA", blk % 2, kc)], dma=True)
            for j in range(BW // 128):
                col = blk * (BW // 128) + j
                for kc in range(KC):
                    _mm(P, pm[:, col:col + 1], buf[:, kc, j * 128:(j + 1) * 128], g.cact[:, kc:kc + 1], kc == 0, kc == KC - 1,
                        r=[("wA", blk % 2, kc), "cact"], w=["pm"])
        P.op("dve", lambda e: e.tensor_tensor(out=g.modT[:], in0=pm[:], in1=g.adab[:, l * 72:(l + 1) * 72], op=ALU.add),
             r=["pm"], w=["modT"])
        for v in range(3):
            sc = g.modT[:, (3 * v + 1) * 8:(3 * v + 2) * 8]
            gt = g.modT[:, (3 * v + 2) * 8:(3 * v + 3) * 8]
            lnv = g.ln[:, l * 24 + v * 8: l * 24 + v * 8 + 8]
            P.op("dve", lambda e, v=v, sc=sc, lnv=lnv: e.scalar_tensor_tensor(out=g.gm[:, v * 8:(v + 1) * 8], in0=sc, scalar=1.0, in1=lnv,
                                                                             op0=ALU.add, op1=ALU.mult), r=["modT"], w=["gm"])
            P.op("dve", lambda e, v=v, gt=gt: e.tensor_scalar(out=g.gate[:, v * 8:(v + 1) * 8], in0=gt, scalar1=(1.0 if v == 1 else 0.5),
                                                               scalar2=None, op0=ALU.mult), r=["modT"], w=["gate"])
        P.emit()


def load_w(P, dst, src, nk, tag, cols=None):
    v = src.rearrange("(kc p) n -> p kc n", p=128)
    for kc in range(nk):
        s = v[:, kc, :] if cols is None else v[:, kc, cols[0]:cols[1]]
        P.op("pool", lambda e, kc=kc, s=s: e.dma_start(out=dst[:, kc, :], in_=s), w=[(tag, kc)], dma=True)


def phase_ffn(g, l, v, h_src):
    st, P, sb, ps = new_phase(g)
    T = 256
    NT = S // T
    pre = "ffn1" if v == 0 else "ffn2"
    with st:
        wg = sb("wg", [128, KC, DFF], BF16)
        wu = sb("wu", [128, KC, DFF], BF16)
        wd = sb("wd", [128, NFF, D], BF16)
        hin = [sb("hin%d" % i, [128, KC, T], F32) for i in range(2)]
        sq = sb("sq", [128, KC, T], BF16)
        u = sb("u", [128, KC, T], BF16)
        sr = sb("sr", [128, T], F32)
        rinv = sb("rinv", [128, T], F32)
        tmp = [sb("tmp%d" % i, [128, T], F32) for i in range(2)]
        sg = [sb("sg%d" % i, [128, T], F32) for i in range(2)]
        aT = sb("aT", [128, NFF, T], BF16)
        ssum = ps("ssum")
        pgu = [ps("pgu%d" % i) for i in range(2)]
        pd = [ps("pd%d" % i) for i in range(2)]
        load_w(P, wg, g.w[pre + "_wg"][l], KC, "wg")
        load_w(P, wu, g.w[pre + "_wu"][l], KC, "wu")
        load_w(P, wd, g.w[pre + "_wd"][l], NFF, "wd")
        gm_cols = g.gm[:, v * 8:(v + 1) * 8]
        sh_cols = g.modT[:, (3 * v) * 8:(3 * v + 1) * 8]
        gate_cols = g.gate[:, v * 8:(v + 1) * 8]
        hsv = h_src.rearrange("(kc p) t -> p kc t", p=128)
        hdv = g.hT.rearrange("(kc p) t -> p kc t", p=128)
        def f_load(t):
            hb = hin[t % 2]
            tag = "f%d" % (t % 2)
            P.op("sp", lambda e: e.dma_start(out=hb[:], in_=hsv[:, :, t * T:(t + 1) * T]), w=[tag + "h"], dma=True)

        def f_norm(t):
            emit_norm(g, P, hin[t % 2], T, sq, ssum, sr, rinv, tmp, gm_cols, sh_cols, u, "f%d" % (t % 2), utok="ffn_u")

        def f_gateup(t):
            tag = "f%d" % (t % 2)
            for j in range(NFF):
                pb = pgu[j % 2]
                for kc in range(KC):
                    _mm(P, pb[:, 0:T], wg[:, kc, j * 128:(j + 1) * 128], u[:, kc, :], kc == 0, kc == KC - 1,
                        r=[("wg", kc), "ffn_u"], w=[("pgu", j % 2)])
                for kc in range(KC):
                    _mm(P, pb[:, T:2 * T], wu[:, kc, j * 128:(j + 1) * 128], u[:, kc, :], kc == 0, kc == KC - 1,
                        r=[("wu", kc), "ffn_u"], w=[("pgu", j % 2)])
                sgb = sg[j % 2]
                P.op("act", lambda e, pb=pb, sgb=sgb: e.activation(out=sgb[:], in_=pb[:, 0:T], func=AF.Silu),
                     w=[("pgu", j % 2), ("sg", j % 2)])
                P.op("dve", lambda e, pb=pb, sgb=sgb, j=j: e.tensor_tensor(out=aT[:, j, :], in0=pb[:, T:2 * T], in1=sgb[:], op=ALU.mult),
                     r=[("sg", j % 2)], w=[("pgu", j % 2), ("aT", j)])

        def f_down(t):
            hb = hin[t % 2]
            tag = "f%d" % (t % 2)
            for m in range(KC):
                pb = pd[m % 2]
                for j in range(NFF):
                    _mm(P, pb[:, 0:T], wd[:, j, m * 128:(m + 1) * 128], aT[:, j, :], j == 0, j == NFF - 1,
                        r=[("wd", j), ("aT", j)], w=[("pd", m % 2)])
                P.op("dve", lambda e, pb=pb, m=m: e.scalar_tensor_tensor(out=hb[:, m, :], in0=pb[:, 0:T], scalar=gate_cols[:, m:m + 1],
                                                                        in1=hb[:, m, :], op0=ALU.mult, op1=ALU.add),
                     w=[("pd", m % 2), tag + "h"])
            P.op("sp", lambda e: e.dma_start(out=hdv[:, :, t * T:(t + 1) * T], in_=hb[:]), r=[tag + "h"], w=["hT_dram"], dma=True)

        f_load(0)
        f_norm(0)
        for t in range(NT):
            if t + 1 < NT:
                f_load(t + 1)
            f_gateup(t)
            if t + 1 < NT:
                f_norm(t + 1)
            f_down(t)
        P.emit()


def phase_u(g, l):
    st, P, sb, ps = new_phase(g)
    T = 512
    NT = S // T
    with st:
        hin = [sb("hin%d" % i, [128, KC, T], F32) for i in range(2)]
        sq = sb("sq", [128, KC, T], BF16)
        u = [sb("u%d" % i, [128, KC, T], BF16) for i in range(2)]
        sr = sb("sr", [128, T], F32)
        rinv = sb("rinv", [128, T], F32)
        tmp = [sb("tmp%d" % i, [128, T], F32) for i in range(2)]
        ssum = ps("ssum")
        gm_cols = g.gm[:, 8:16]
        sh_cols = g.modT[:, 24:32]
        hsv = g.hT.rearrange("(kc p) t -> p kc t", p=128)
        udv = g.uT.rearrange("(kc p) t -> p kc t", p=128)
        for t in range(NT):
            hb = hin[t % 2]
            ub = u[t % 2]
            tag = "n%d" % (t % 2)
            P.op("sp", lambda e, hb=hb, t=t: e.dma_start(out=hb[:], in_=hsv[:, :, t * T:(t + 1) * T]), w=[tag + "h"], dma=True)
            emit_norm(g, P, hb, T, sq, ssum, sr, rinv, tmp, gm_cols, sh_cols, ub, tag)
            P.op("sp", lambda e, ub=ub, t=t: e.dma_start(out=udv[:, :, t * T:(t + 1) * T], in_=ub[:]), r=[tag + "u"], dma=True)
        P.emit()


def phase_final(g):
    st, P, sb, ps = new_phase(g)
    T = 512
    NT = S // T
    with st:
        hin = [sb("hin%d" % i, [128, KC, T], F32) for i in range(2)]
        sq = sb("sq", [128, KC, T], BF16)
        o = [sb("o%d" % i, [128, KC, T], F32) for i in range(2)]
        sr = sb("sr", [128, T], F32)
        rinv = sb("rinv", [128, T], F32)
        ssum = ps("ssum")
        gm_cols = g.ln[:, DEPTH * 24:DEPTH * 24 + 8]
        hsv = g.hT.rearrange("(kc p) t -> p kc t", p=128)
        odv = g.outT.rearrange("(kc p) t -> p kc t", p=128)
        for t in range(NT):
            hb = hin[t % 2]
            ob = o[t % 2]
            tag = "n%d" % (t % 2)
            P.op("sp", lambda e, hb=hb, t=t: e.dma_start(out=hb[:], in_=hsv[:, :, t * T:(t + 1) * T]), w=[tag + "h"], dma=True)
            emit_norm(g, P, hb, T, sq, ssum, sr, rinv, None, gm_cols, None, ob, tag, out_f32=True)
            P.op("sp", lambda e, ob=ob, t=t: e.dma_start(out=odv[:, :, t * T:(t + 1) * T], in_=ob[:]), r=[tag + "u"], dma=True)
        P.emit()


def emit_norm(g, P, hin, T, sq, ssum, sr, rinv, tmp, gm_cols, sh_cols, out_tile, tag, out_f32=False, utok=None):
    utok = utok or (tag + "u")
    hflat = hin[:].rearrange("p a b -> p (a b)")
    sqflat = sq[:].rearrange("p a b -> p (a b)")
    P.op("act", lambda e: e.activation(out=sqflat, in_=hflat, func=AF.Square), r=[tag + "h"], w=["n_sq"])
    for kc in range(KC):
        _mm(P, ssum[:, 0:T], g.onesb, sq[:, kc, :], kc == 0, kc == KC - 1, r=["n_sq"], w=["n_ssum"])
    P.op("act", lambda e: e.activation(out=sr[:, 0:T], in_=ssum[:, 0:T], func=AF.Sqrt, scale=1.0 / D, bias=EPS),
         w=["n_ssum", "n_sr"])
    P.op("dve", lambda e: e.reciprocal(out=rinv[:, 0:T], in_=sr[:, 0:T]), r=["n_sr"], w=["n_rinv"])
    for kc in range(KC):
        if out_f32:
            P.op("dve", lambda e, kc=kc: e.scalar_tensor_tensor(out=out_tile[:, kc, :], in0=hin[:, kc, :], scalar=gm_cols[:, kc:kc + 1],
                                                               in1=rinv[:, 0:T], op0=ALU.mult, op1=ALU.mult),
                 r=[tag + "h", "n_rinv"], w=[utok])
        else:
            tb = tmp[kc % 2]
            P.op("dve", lambda e, kc=kc, tb=tb: e.scalar_tensor_tensor(out=tb[:, 0:T], in0=hin[:, kc, :], scalar=gm_cols[:, kc:kc + 1],
                                                                      in1=rinv[:, 0:T], op0=ALU.mult, op1=ALU.mult),
                 r=[tag + "h", "n_rinv"], w=[("n_tmp", kc % 2)])
            P.op("act", lambda e, kc=kc, tb=tb: e.activation(out=out_tile[:, kc, :], in_=tb[:, 0:T], func=AF.Identity,
                                                             bias=sh_cols[:, kc:kc + 1]),
                 r=[("n_tmp", kc % 2)], w=[utok])


AX = mybir.AxisListType


def phase_out(g, l):
    st, P, sb, ps = new_phase(g)
    T = 256
    NT = S // T
    with st:
        wa = sb("wa", [128, 8, D], BF16)
        wb = sb("wb", [128, 6, D], BF16)
        wo = sb("wo", [128, 8, D], BF16)
        wga = sb("wga", [128, 8, D], BF16)
        wgb = sb("wgb", [128, 8, D], BF16)
        load_w(P, wa, g.w["w_a"][l], 8, "wa")
        load_w(P, wb, g.w["w_b"][l], 6, "wb")
        load_w(P, wo, g.w["w_o"][l], 8, "wo")
        load_w(P, wga, g.w["w_in"][l], 8, "wga", cols=(O_GA, O_GA + D))
        load_w(P, wgb, g.w["w_in"][l], 8, "wgb", cols=(O_GB, O_GB + D))
        ut = [sb("ut%d" % i, [128, 8, T], BF16) for i in range(2)]
        oa = [sb("oa%d" % i, [128, 8, T], BF16) for i in range(2)]
        ob = [sb("ob%d" % i, [128, 6, T], BF16) for i in range(2)]
        hb_ = [sb("hb%d" % i, [128, 8, T], F32) for i in range(2)]
        mg = sb("mg", [128, 8, T], BF16)
        sgab = [sb("sgab%d" % i, [128, 2 * T], F32) for i in range(2)]
        t12 = [sb("t12%d" % i, [128, 2 * T], F32) for i in range(2)]
        bA = [ps("bA%d" % i) for i in range(2)]
        bB = [ps("bB%d" % i) for i in range(2)]
        bC = [ps("bC%d" % i) for i in range(2)]
        gate_cols = g.gate[:, 8:16]
        uv = g.uT.rearrange("(kc p) t -> p kc t", p=128)
        oav = g.oaT.rearrange("(kc p) t -> p kc t", p=128)
        obv = g.obT.rearrange("(kc p) t -> p kc t", p=128)
        hv = g.hT.rearrange("(kc p) t -> p kc t", p=128)
        for t in range(NT):
            q = t % 2
            sl = slice(t * T, (t + 1) * T)
            P.op("sp", lambda e, q=q, sl=sl: e.dma_start(out=ut[q][:], in_=uv[:, :, sl]), w=[("ut", q)], dma=True)
            P.op("sp", lambda e, q=q, sl=sl: e.dma_start(out=oa[q][:], in_=oav[:, :, sl]), w=[("oa", q)], dma=True)
            P.op("sp", lambda e, q=q, sl=sl: e.dma_start(out=ob[q][:], in_=obv[:, :, sl]), w=[("ob", q)], dma=True)
            P.op("sp", lambda e, q=q, sl=sl: e.dma_start(out=hb_[q][:], in_=hv[:, :, sl]), w=[("hb", q)], dma=True)
            for m in range(8):
                mi = m % 2
                ms = slice(m * 128, (m + 1) * 128)
                for c in range(8):
                    _mm(P, bA[mi][:, 0:T], wa[:, c, ms], oa[q][:, c, :], c == 0, c == 7, r=[("wa", c), ("oa", q)], w=[("bA", mi)])
                for c in range(6):
                    _mm(P, bA[mi][:, T:2 * T], wb[:, c, ms], ob[q][:, c, :], c == 0, c == 5, r=[("wb", c), ("ob", q)], w=[("bA", mi)])
                for c in range(8):
                    _mm(P, bB[mi][:, 0:T], wga[:, c, ms], ut[q][:, c, :], c == 0, c == 7, r=[("wga", c), ("ut", q)], w=[("bB", mi)])
                for c in range(8):
                    _mm(P, bB[mi][:, T:2 * T], wgb[:, c, ms], ut[q][:, c, :], c == 0, c == 7, r=[("wgb", c), ("ut", q)], w=[("bB", mi)])
                P.op("act", lambda e, mi=mi: e.activation(out=sgab[mi][:], in_=bB[mi][:, :], func=AF.Sigmoid), w=[("bB", mi), ("sgab", mi)])
                P.op("dve", lambda e, mi=mi: e.tensor_tensor(out=t12[mi][:], in0=bA[mi][:, :], in1=sgab[mi][:], op=ALU.mult),
                     r=[("sgab", mi)], w=[("bA", mi), ("t12", mi)])
                P.op("pool", lambda e, mi=mi, m=m: e.tensor_tensor(out=mg[:, m, :], in0=t12[mi][:, 0:T], in1=t12[mi][:, T:2 * T], op=ALU.add),
                     r=[("t12", mi)], w=[("mg", m)])
            for m in range(8):
                mi = m % 2
                ms = slice(m * 128, (m + 1) * 128)
                for c in range(8):
                    _mm(P, bC[mi][:, 0:T], wo[:, c, ms], mg[:, c, :], c == 0, c == 7, r=[("wo", c), ("mg", c)], w=[("bC", mi)])
                P.op("dve", lambda e, mi=mi, m=m, q=q: e.scalar_tensor_tensor(out=hb_[q][:, m, :], in0=bC[mi][:, 0:T], scalar=gate_cols[:, m:m + 1],
                                                                             in1=hb_[q][:, m, :], op0=ALU.mult, op1=ALU.add),
                     w=[("bC", mi), ("hb", q)])
            P.op("sp", lambda e, q=q, sl=sl: e.dma_start(out=hv[:, :, sl], in_=hb_[q][:]), r=[("hb", q)], w=["hT_dram"], dma=True)
        P.emit()


def phase_da(g, l):
    st, P, sb, ps = new_phase(g)
    import os
    NG = int(os.environ.get("DA_NG", 6))
    with st:
        uT = sb("uT", [128, KC, S], BF16)
        udv = g.uT.rearrange("(kc p) t -> p kc t", p=128)
        for kc in range(KC):
            P.op("sp", lambda e, kc=kc: e.dma_start(out=uT[:, kc, :], in_=udv[:, kc, :]), w=[("uT", kc)], dma=True)
        wq = [sb("wq%d" % i, [128, KC, 128], BF16) for i in range(2)]
        wk = [sb("wk%d" % i, [128, KC, 128], BF16) for i in range(2)]
        wv = [sb("wv%d" % i, [128, KC, 128], BF16) for i in range(2)]
        QT = sb("QT", [128, S], BF16)
        KT = sb("KT", [128, S], BF16)
        VT = sb("VT", [128, S], BF16)
        acc = sb("acc", [128, 2, S], F32)
        sqt = sb("sqt", [128, 512], BF16)
        mx = sb("mx", [128, 4], F32)
        tm = sb("tm", [128, 2], F32)
        pjn = [0]
        prod = sb("prod", [128, 2], F32)
        negm = sb("negm", [128, 2], F32)
        Vaug = [sb("Vaug%d" % i, [128, 2, 128], BF16) for i in range(3)]
        sbt = [sb("sbt%d" % i, [128, 2, 256], F32) for i in range(2)]
        PT = [sb("PT%d" % i, [128, 2, 256], BF16) for i in range(2)]
        rl = sb("rl", [128, S], F32)
        rl2 = sb("rl2", [128, S], F32)
        ob = sb("ob", [128, S], BF16)
        B = [ps("B%d" % i) for i in range(7)]
        B.insert(1, None)
        pvt = ps("pvt", (128, 1024), BF16)
        pqk = B[0]
        pss = B[0]
        pst = [[B[2], B[3]], [B[4], B[5]]]
        ppv = [B[6], B[7]]
        for i in range(3):
            P.op("pool", lambda e, i=i: e.memset(Vaug[i][:], 1.0), w=[("V", i)])
        vi = 0
        bi = 0
        for gi in range(NG):
            gq = gi % 2
            w_in = g.w["w_in"][l]
            load_w(P, wq[gq], w_in, KC, ("wq", gq), cols=(O_DQ + gi * 128, O_DQ + (gi + 1) * 128))
            load_w(P, wk[gq], w_in, KC, ("wk", gq), cols=(O_DK + gi * 128, O_DK + (gi + 1) * 128))
            load_w(P, wv[gq], w_in, KC, ("wv", gq), cols=(O_DV + gi * 128, O_DV + (gi + 1) * 128))
            P.op("dve", lambda e: e.memset(mx[:], 0.0), w=["mx"])
            for t in range(8):
                sl = slice(t * 512, (t + 1) * 512)
                for (W, wn, dst, dn, mc) in ((wq[gq], "wq", QT, "QT", 0), (wk[gq], "wk", KT, "KT", 2), (wv[gq], "wv", VT, "VT", -1)):
                    pjn[0] += 1
                    pq_, pqt = (B[0], "B0") if pjn[0] % 2 == 0 else (B[2], "B2")
                    for kc in range(KC):
                        _mm(P, pq_[:, :], W[:, kc, :], uT[:, kc, sl], kc == 0, kc == KC - 1, r=[((wn, gq), kc), ("uT", kc)], w=[pqt])
                    P.op("act", lambda e, dst=dst, sl=sl, pq_=pq_: e.activation(out=dst[:, sl], in_=pq_[:, :], func=AF.Copy), w=[pqt, dn])
                    if mc < 0:
                        continue
                    P.op("act", lambda e, pq_=pq_: e.activation(out=sqt[:], in_=pq_[:, :], func=AF.Square), w=[pqt, "sqt"])
                    for hh in range(2):
                        ps_, pst_ = (B[3], "B3") if hh == 0 else (B[4], "B4")
                        _mm(P, ps_[:, :], g.headsel[:, hh * 128:(hh + 1) * 128], sqt[:], True, True, r=["sqt"], w=[pst_])
                        P.op("dve", lambda e, ps_=ps_, hh=hh: e.reduce_max(out=tm[:, hh:hh + 1], in_=ps_[:, :], axis=AX.X), w=[pst_, ("tm", hh)])
                        P.op("dve", lambda e, c=mc + hh, hh=hh: e.tensor_tensor(out=mx[:, c:c + 1], in0=mx[:, c:c + 1], in1=tm[:, hh:hh + 1], op=ALU.max),
                             r=[("tm", hh)], w=["mx"])
            P.op("dve", lambda e: e.tensor_tensor(out=prod[:], in0=mx[:, 0:2], in1=mx[:, 2:4], op=ALU.mult), r=["mx"], w=["prod"])
            P.op("act", lambda e: e.activation(out=prod[:], in_=prod[:], func=AF.Sqrt), w=["prod"])
            P.op("dve", lambda e: e.tensor_scalar(out=negm[:], in0=prod[:], scalar1=-0.125, scalar2=None, op0=ALU.mult), r=["prod"], w=["negm"])
            blocks = []
            for pi, r_ in enumerate((1, 4, 16)):
                nblk = 32 // r_
                for p_ in range(r_):
                    for b in range(nblk):
                        blocks.append((pi, r_, p_, b))

            def mk(pi, r_, p_, b, vi_, bi_):
                def tok(bb):
                    s0 = p_ + r_ * 128 * bb
                    return slice(s0, s0 + r_ * 127 + 1, r_)
                cur = vi_ % 3
                prv = (vi_ - 1) % 3
                bq = bi_ % 2
                tb = tok(b)
                lo = 128 if b == 0 else 0

                def stageA():
                    P.op("pe", lambda e: e.transpose(out=pvt[:, 0:128], in_=VT[:, tb], identity=g.identb), r=["VT"], w=["pvt"])
                    P.op("act", lambda e: e.activation(out=Vaug[cur][:, :, 64:128],
                                                       in_=pvt[:, 0:128].rearrange("p (h d) -> p h d", h=2), func=AF.Copy),
                         w=["pvt", ("V", cur)])
                    for hh in range(2):
                        rows = slice(64 * hh, 64 * hh + 64)
                        pb = pst[bq][hh]
                        btok = "B%d" % (2 + 2 * bq + hh)
                        if b > 0:
                            _mm(P, pb[:, 0:128], KT[rows, tok(b - 1)], QT[rows, tb], True, True, r=["KT", "QT"], w=[btok])
                        _mm(P, pb[:, 128:256], KT[rows, tb], QT[rows, tb], True, True, r=["KT", "QT"], w=[btok])
                        cc = -8.0 * SLOPES[gi * 2 + hh] * r_
                        P.op("dve", lambda e, pb=pb, hh=hh, cc=cc: e.scalar_tensor_tensor(
                            out=sbt[bq][:, hh, lo:256], in0=g.D2[:, lo:256], scalar=cc, in1=pb[:, lo:256], op0=ALU.mult, op1=ALU.add),
                            w=[btok, ("sbt", bq, hh)])
                        P.op("act", lambda e, hh=hh: e.activation(out=PT[bq][:, hh, lo:256], in_=sbt[bq][:, hh, lo:256],
                                                                 func=AF.Exp, scale=0.125, bias=negm[:, hh:hh + 1]),
                             r=[("sbt", bq, hh), "negm"], w=[("PT", bq, hh)])

                def stageB():
                    pp = ppv[bq]
                    ptok = "B%d" % (6 + bq)
                    for hh in range(2):
                        if b > 0:
                            _mm(P, pp[:, hh * 128:(hh + 1) * 128], Vaug[prv][:, hh, :], PT[bq][:, hh, 0:128], True, False,
                                r=[("V", prv), ("PT", bq, hh)], w=[ptok])
                        _mm(P, pp[:, hh * 128:(hh + 1) * 128], Vaug[cur][:, hh, :], PT[bq][:, hh, 128:256], b == 0, True,
                            r=[("V", cur), ("PT", bq, hh)], w=[ptok])
                    ppv3 = pp[:, 0:256].rearrange("p (h q) -> p h q", h=2)
                    if pi == 0:
                        P.op("dve", lambda e: e.tensor_copy(out=acc[:, :, tb], in_=ppv3), w=[ptok, "acc"])
                    else:
                        P.op("dve", lambda e: e.tensor_tensor(out=acc[:, :, tb], in0=ppv3, in1=acc[:, :, tb], op=ALU.add),
                             w=[ptok, "acc"])
                return stageA, stageB

            prevB = None
            for blk in blocks:
                sA, sB = mk(*blk, vi, bi)
                vi += 1
                bi += 1
                sA()
                if prevB is not None:
                    prevB()
                prevB = sB
            prevB()
            for hh in range(2):
                P.op("dve", lambda e, hh=hh: e.reciprocal(out=rl[0:64, :], in_=acc[0:64, hh, :]), r=["acc"], w=["rl"])
                P.op("act", lambda e: e.activation(out=rl2[64:128, :], in_=rl[0:64, :], func=AF.Copy), r=["rl"], w=["rl2"])
                P.op("pool", lambda e, hh=hh: e.tensor_tensor(out=ob[64:128, :], in0=acc[64:128, hh, :], in1=rl2[64:128, :], op=ALU.mult),
                     r=["acc", "rl2"], w=["ob"])
                r0 = (2 * gi + hh) * 64
                P.op("sp", lambda e, r0=r0: e.dma_start(out=g.obT[r0:r0 + 64, :], in_=ob[64:128, :]), r=["ob"], dma=True)
        P.emit()


def phase_dn(g, l):
    st, P, sb, ps = new_phase(g)
    import os
    HG = 4
    NPASS = int(os.environ.get("DN_NPASS", 2))
    NT = int(os.environ.get("DN_NT", 8))
    T = 512
    with st:
        wqkv = sb("wqkv", [128, KC, 3 * HG * 128], BF16)
        wz = sb("wz", [128, KC, HG * 128], BF16)
        wba = sb("wba", [128, KC, 16], BF16)
        ut = [sb("ut%d" % i, [128, KC, T], BF16) for i in range(2)]
        betaT = sb("betaT", [8, T], F32)
        g1 = sb("g1", [8, T], F32)
        g2 = sb("g2", [8, T], F32)
        g3 = sb("g3", [8, T], F32)
        gcT = sb("gcT", [8, T], F32)
        tk = sb("tk", [128, 4, 16], F32)
        egc = sb("egc", [128, 4, 8], F32)
        negc = sb("negc", [128, 4, 8], F32)
        bege = sb("bege", [128, 4, 8], F32)
        glb = sb("glb", [128, 4, 8], F32)
        dl = sb("dl", [128, 4, 8], F32)
        edl = sb("edl", [128, 4, 8], F32)
        egl = [sb("egl%d" % i, [128, 4, HG], F32) for i in range(2)]
        halo = sb("halo", [128, 3 * HG, 3], F32)
        xpre = [sb("xpre%d" % i, [128, T + 3], F32) for i in range(2)]
        yb = [sb("yb%d" % i, [128, T], F32) for i in range(2)]
        sbf = [sb("sbf%d" % i, [128, T], F32) for i in range(2)]
        sqh = sb("sqh", [128, T], BF16)
        ctmp = sb("ctmp", [128, T], F32)
        srn = sb("srn", [128, T], F32)
        rinvn = sb("rinvn", [128, T], F32)
        qT = [sb("qT%d" % i, [128, T], BF16) for i in range(2)]
        kT = [sb("kT%d" % i, [128, T], BF16) for i in range(2)]
        vT = [sb("vT%d" % i, [128, T], BF16) for i in range(2)]
        egcb = sb("egcb", [128, T], F32)
        tE = sb("tE", [128, 4, 128], F32)
        E4 = sb("E4", [128, 4, 128], F32)
        BBs = sb("BBs", [128, 4, 128], F32)
        EBs = sb("EBs", [128, 4, 128], F32)
        import os as _os
        DT_T = F32R if _os.environ.get("USE_F32R") else F32
        A = [sb("A%d" % i, [128, 4, 128], DT_T) for i in range(2)]
        Bm = [sb("Bm%d" % i, [128, 4, 128], DT_T) for i in range(2)]
        R = sb("R", [128, 4, 128], DT_T)
        kbg = sb("kbg", [128, 4, 128], DT_T)
        vb = sb("vb", [128, 4, 128], DT_T)
        identr = sb("identr", [128, 128], DT_T)
        P.op("act", lambda e: e.activation(out=identr[:], in_=g.identf, func=AF.Copy), w=["identr"])
        wT4 = [sb("wT4%d" % i, [128, HG, T], BF16) for i in range(2)]
        u4 = [sb("u4%d" % i, [128, HG, 4, 128], F32) for i in range(2)]
        qgT = [sb("qgT%d" % i, [128, HG, T], BF16) for i in range(2)]
        qkT4 = [sb("qkT4%d" % i, [128, HG, T], BF16) for i in range(2)]
        kdec = [sb("kdec%d" % i, [128, HG, 4, 128], BF16) for i in range(2)]
        Sf = sb("Sf", [128, HG, 128], F32)
        Sb = sb("Sb", [128, HG, 128], BF16)
        vnew = [sb("vnew%d" % i, [128, HG, 128], BF16) for i in range(2)]
        oraw = sb("oraw", [128, HG, T], F32)
        sqo = sb("sqo", [128, T], BF16)
        sro = sb("sro", [128, T], F32)
        rinvo = sb("rinvo", [128, T], F32)
        sz = sb("sz", [128, T], F32)
        on = sb("on", [128, T], F32)
        oab = [sb("oab%d" % i, [128, T], BF16) for i in range(2)]
        PA = ps("PA")
        PB = ps("PB")
        PC = [ps("PC%d" % i) for i in range(2)]
        PTt = ps("PTt", (128, 1024), BF16)
        PSW = ps("PSW")
        PSO = ps("PSO")
        PSD = ps("PSD")
        pcn = [0]

        def pc():
            i = pcn[0] % 2
            pcn[0] += 1
            return PC[i], ("PC", i)

        def c4(ap):
            return ap.rearrange("p (c i) -> p c i", c=4)

        def bc4(ap2):
            return ap2.unsqueeze(1).to_broadcast([128, 4, 128])

        def colbc(ap_c):
            return ap_c.unsqueeze(2).to_broadcast([128, 4, 128])

        w_in = g.w["w_in"][l]
        udv = g.uT.rearrange("(kc p) t -> p kc t", p=128)
        nAcol = g.nA[:, l:l + 1]
        dtbcol = g.dtb_sb[:, l:l + 1]
        dnncol = g.dnn[:, l:l + 1]

        def gates(t):
            q = t % 2
            P.op("sp", lambda e: e.dma_start(out=ut[q][:], in_=udv[:, :, t * T:(t + 1) * T]), w=[("ut", q)], dma=True)
            for kc in range(KC):
                _mm(P, PA[0:8, :], wba[:, kc, 0:8], ut[q][:, kc, :], kc == 0, kc == KC - 1, r=["wba", ("ut", q)], w=["PA"])
            P.op("act", lambda e: e.activation(out=betaT[:], in_=PA[0:8, :], func=AF.Sigmoid), w=["PA", "betaT"])
            for kc in range(KC):
                _mm(P, PA[0:8, :], wba[:, kc, 8:16], ut[q][:, kc, :], kc == 0, kc == KC - 1, r=["wba", ("ut", q)], w=["PA"])
            P.op("dve", lambda e: e.tensor_scalar(out=g1[:], in0=PA[0:8, :], scalar1=dtbcol, scalar2=None, op0=ALU.add), w=["PA", "g1"])
            P.op("dve", lambda e: e.tensor_scalar(out=g2[:], in0=g1[:], scalar1=-1.0, scalar2=None, op0=ALU.mult), r=["g1"], w=["g2"])
            P.op("dve", lambda e: e.tensor_tensor(out=g2[:], in0=g2[:], in1=g1[:], op=ALU.max), r=["g1"], w=["g2"])
            P.op("act", lambda e: e.activation(out=g2[:], in_=g2[:], func=AF.Exp, scale=-1.0), w=["g2"])
            P.op("act", lambda e: e.activation(out=g2[:], in_=g2[:], func=AF.Ln, bias=1.0), w=["g2"])
            P.op("dve", lambda e: e.tensor_scalar(out=g1[:], in0=g1[:], scalar1=0.0, scalar2=None, op0=ALU.max), w=["g1"])
            P.op("dve", lambda e: e.tensor_tensor(out=g1[:], in0=g1[:], in1=g2[:], op=ALU.add), r=["g2"], w=["g1"])
            P.op("dve", lambda e: e.tensor_scalar(out=g3[:], in0=g1[:], scalar1=nAcol, scalar2=None, op0=ALU.mult), r=["g1"], w=["g3"])
            P.op("dve", lambda e: e.tensor_tensor_scan(out=gcT[:], data0=g.resetm, data1=g3[:], initial=0.0, op0=ALU.mult, op1=ALU.add),
                 r=["g3"], w=["gcT"])
            for c in range(4):
                cs = slice(c * 128, (c + 1) * 128)
                _mm(P, PB[:, c * 16:c * 16 + 8], gcT[0:8, cs], g.identf[0:8, 0:8], True, True, r=["gcT"], w=["PB"])
                _mm(P, PB[:, c * 16 + 8:c * 16 + 16], betaT[0:8, cs], g.identf[0:8, 0:8], True, True, r=["betaT"], w=["PB"])
            tkf = tk[:].rearrange("p c k -> p (c k)")
            P.op("act", lambda e: e.activation(out=tkf, in_=PB[:, 0:64], func=AF.Copy), w=["PB", "tk"])
            P.op("act", lambda e: e.activation(out=egc[:], in_=tk[:, :, 0:8], func=AF.Exp), r=["tk"], w=["egc"])
            P.op("dve", lambda e: e.tensor_scalar(out=negc[:], in0=tk[:, :, 0:8], scalar1=-1.0, scalar2=None, op0=ALU.mult), r=["tk"], w=["negc"])
            P.op("dve", lambda e: e.tensor_tensor(out=bege[:], in0=tk[:, :, 8:16], in1=egc[:], op=ALU.mult), r=["tk", "egc"], w=["bege"])

        def pre(t, hp, hl):
            h = hp * HG + hl
            q = t % 2
            ws = hl % 2
            for part in range(3):
                fcl = part * HG + hl
                fcg = part * 8 + h
                xi = part % 2
                xp = xpre[xi]
                for kc in range(KC):
                    _mm(P, PA[:, :], wqkv[:, kc, fcl * 128:(fcl + 1) * 128], ut[q][:, kc, :], kc == 0, kc == KC - 1,
                        r=[("wqkv", kc), ("ut", q)], w=["PA"])
                P.op("pool", lambda e, xp=xp, fcl=fcl: e.tensor_copy(out=xp[:, 0:3], in_=halo[:, fcl, :]), r=[("halo", fcl)], w=[("xpre", xi)])
                P.op("act", lambda e, xp=xp: e.activation(out=xp[:, 3:T + 3], in_=PA[:, :], func=AF.Copy), w=["PA", ("xpre", xi)])
                P.op("pool", lambda e, xp=xp, fcl=fcl: e.tensor_copy(out=halo[:, fcl, :], in_=xp[:, T:T + 3]), r=[("xpre", xi)], w=[("halo", fcl)])
                y = yb[xi]
                cwb = l * 96 + fcg * 4
                P.op("dve", lambda e, xp=xp, y=y, cwb=cwb: e.tensor_scalar(out=y[:], in0=xp[:, 0:T], scalar1=g.convw[:, cwb:cwb + 1], scalar2=None,
                                                                       op0=ALU.mult), r=[("xpre", xi)], w=[("y", xi)])
                for j in range(1, 4):
                    P.op("dve", lambda e, xp=xp, y=y, cwb=cwb, j=j: e.scalar_tensor_tensor(out=y[:], in0=xp[:, j:j + T],
                                                                                      scalar=g.convw[:, cwb + j:cwb + j + 1],
                                                                                      in1=y[:], op0=ALU.mult, op1=ALU.add),
                         r=[("xpre", xi)], w=[("y", xi)])
                if part == 2:
                    P.op("act", lambda e, y=y: e.activation(out=vT[ws][:], in_=y[:], func=AF.Silu), r=[("y", xi)], w=[("vT", ws)])
                else:
                    s_ = sbf[xi]
                    dst = qT[ws] if part == 0 else kT[ws]
                    dn_ = ("qT", ws) if part == 0 else ("kT", ws)
                    P.op("act", lambda e, y=y, s_=s_: e.activation(out=s_[:], in_=y[:], func=AF.Silu), r=[("y", xi)], w=[("sbf", xi)])
                    P.op("act", lambda e, s_=s_: e.activation(out=sqh[:], in_=s_[:], func=AF.Square), r=[("sbf", xi)], w=["sqh"])
                    _mm(P, PB[:, :], g.onesb, sqh[:], True, True, r=["sqh"], w=["PB"])
                    scl = 128.0 if part == 0 else 1.0
                    P.op("act", lambda e, scl=scl: e.activation(out=srn[:], in_=PB[:, :], func=AF.Ln, scale=scl, bias=EPS * scl), w=["PB", "srn"])
                    P.op("act", lambda e: e.activation(out=rinvn[:], in_=srn[:], func=AF.Exp, scale=-0.5), r=["srn"], w=["rinvn"])
                    P.op("dve", lambda e, s_=s_, dst=dst: e.tensor_tensor(out=dst[:], in0=s_[:], in1=rinvn[:], op=ALU.mult),
                         r=[("sbf", xi), "rinvn"], w=[dn_])
            selh = g.sel[:, h * 128:(h + 1) * 128]
            _mm(P, PB[:, :], selh, gcT[:], True, True, r=["gcT"], w=["PB"])
            P.op("act", lambda e: e.activation(out=glb[:, :, h], in_=PB[:, 127:512:128], func=AF.Copy), w=["PB", ("glb", h)])
            P.op("act", lambda e: e.activation(out=egcb[:], in_=PB[:, :], func=AF.Exp), w=["PB", "egcb"])
            P.op("dve", lambda e: e.tensor_tensor(out=tE[:], in0=c4(PB[:, :]), in1=bc4(g.negmask), op=ALU.add), w=["PB", "tE"])
            P.op("pool", lambda e: e.tensor_tensor(out=qgT[q][:, hl, :], in0=qT[ws][:], in1=egcb[:], op=ALU.mult),
                 r=[("qT", ws), "egcb"], w=[("qg", q, hl)])
            P.op("dve", lambda e: e.tensor_tensor(out=dl[:, :, h], in0=glb[:, :, h], in1=tk[:, :, h], op=ALU.subtract),
                 r=[("glb", h), "tk"], w=[("dl", h)])
            P.op("act", lambda e: e.activation(out=edl[:, :, h], in_=dl[:, :, h], func=AF.Exp), r=[("dl", h)], w=[("edl", h)])
            P.op("act", lambda e: e.activation(out=egl[q][:, :, hl], in_=glb[:, :, h], func=AF.Exp), r=[("glb", h)], w=[("egl", q, hl)])
            for c in range(4):
                P.op("act", lambda e, c=c: e.activation(out=E4[:, c, :], in_=tE[:, c, :], func=AF.Exp, bias=negc[:, c, h:h + 1]),
                     r=["tE", "negc"], w=["E4"])
            _mm(P, PB[:, :], selh, betaT[:], True, True, r=["betaT"], w=["PB"])
            P.op("dve", lambda e: e.tensor_tensor(out=BBs[:], in0=c4(PB[:, :]), in1=bc4(g.strict01), op=ALU.mult), w=["PB", "BBs"])
            P.op("pool", lambda e: e.tensor_tensor(out=EBs[:], in0=E4[:], in1=BBs[:], op=ALU.mult), r=["E4", "BBs"], w=["EBs"])
            for c in range(4):
                cs = slice(c * 128, (c + 1) * 128)
                P.op("pe", lambda e, cs=cs: e.transpose(out=PTt[:, cs], in_=kT[ws][:, cs], identity=g.identb), r=[("kT", ws)], w=["PT"])
            for c in range(4):
                cs = slice(c * 128, (c + 1) * 128)
                P.op("pe", lambda e, cs=cs, c=c: e.transpose(out=PTt[:, 512 + c * 128:512 + (c + 1) * 128], in_=vT[ws][:, cs], identity=g.identb),
                     r=[("vT", ws)], w=["PT"])
            P.op("dve", lambda e: e.tensor_tensor(out=kbg[:], in0=c4(PTt[:, 0:512]), in1=colbc(bege[:, :, h]), op=ALU.mult),
                 r=["bege"], w=["PT", "kbg"])
            P.op("dve", lambda e: e.tensor_tensor(out=kdec[q][:, hl, :, :], in0=c4(PTt[:, 0:512]), in1=colbc(edl[:, :, h]), op=ALU.mult),
                 r=[("edl", h)], w=["PT", ("kdec", q, hl)])
            P.op("dve", lambda e: e.tensor_tensor(out=vb[:], in0=c4(PTt[:, 512:1024]), in1=colbc(tk[:, :, 8 + h]), op=ALU.mult),
                 r=["tk"], w=["PT", "vb"])
            pkk, tkk = pc()
            for c in range(4):
                cs = slice(c * 128, (c + 1) * 128)
                _mm(P, pkk[:, cs], kT[ws][:, cs], kT[ws][:, cs], True, True, r=[("kT", ws)], w=[tkk])
            pqk, tqk = pc()
            for c in range(4):
                cs = slice(c * 128, (c + 1) * 128)
                _mm(P, pqk[:, cs], kT[ws][:, cs], qT[ws][:, cs], True, True, r=[("kT", ws), ("qT", ws)], w=[tqk])
            P.op("dve", lambda e: e.tensor_tensor(out=A[0][:], in0=c4(pkk[:, :]), in1=EBs[:], op=ALU.mult), r=["EBs"], w=[tkk, ("A", 0)])
            P.op("dve", lambda e: e.tensor_tensor(out=c4(qkT4[q][:, hl, :]), in0=c4(pqk[:, :]), in1=E4[:], op=ALU.mult),
                 r=["E4"], w=[tqk, ("qk", q, hl)])
            pt0, tt0 = pc()
            for c in range(4):
                cs = slice(c * 128, (c + 1) * 128)
                _mmr(P, pt0[:, cs], A[0][:, c, :], identr[:], True, True, r=[("A", 0), "identr"], w=[tt0])
            P.op("act", lambda e, pt0=pt0: e.activation(out=Bm[0][:], in_=c4(pt0[:, :]), func=AF.Copy), w=[tt0, ("B", 0)])
            P.op("dve", lambda e: e.scalar_tensor_tensor(out=R[:], in0=A[0][:], scalar=-1.0, in1=bc4(g.identf), op0=ALU.mult, op1=ALU.add),
                 r=[("A", 0)], w=["R"])
            for k in range(1, 7):
                ap_, bp_ = (k - 1) % 2, (k - 1) % 2
                an_, bn_ = k % 2, k % 2
                if k <= 5:
                    px, tx = pc()
                    for c in range(4):
                        cs = slice(c * 128, (c + 1) * 128)
                        _mmr(P, px[:, cs], Bm[bp_][:, c, :], A[ap_][:, c, :], True, True, r=[("B", bp_), ("A", ap_)], w=[tx])
                if k <= 5:
                    P.op("act", lambda e, px=px, an_=an_: e.activation(out=A[an_][:], in_=c4(px[:, :]), func=AF.Copy), w=[tx, ("A", an_)])
                    py, ty = pc()
                    for c in range(4):
                        cs = slice(c * 128, (c + 1) * 128)
                        P.op("pe", lambda e, py=py, cs=cs, c=c, an_=an_: e.transpose(out=py[:, cs], in_=A[an_][:, c, :], identity=g.identf),
                             r=[("A", an_)], w=[ty])
                else:
                    py, ty = pc()
                    for c in range(4):
                        cs = slice(c * 128, (c + 1) * 128)
                        _mmr(P, py[:, cs], A[ap_][:, c, :], Bm[bp_][:, c, :], True, True, r=[("B", bp_), ("A", ap_)], w=[ty])
                P.op("act", lambda e, py=py, bn_=bn_: e.activation(out=Bm[bn_][:], in_=c4(py[:, :]), func=AF.Copy), w=[ty, ("B", bn_)])
                pz, tz = pc()
                for c in range(4):
                    cs = slice(c * 128, (c + 1) * 128)
                    _mmr(P, pz[:, cs], Bm[bn_][:, c, :], R[:, c, :], True, True, r=[("B", bn_), "R"], w=[tz])
                P.op("dve", lambda e, pz=pz: e.tensor_tensor(out=R[:], in0=c4(pz[:, :]), in1=R[:], op=ALU.add), w=[tz, "R"])
            pw, tw = pc()
            for c in range(4):
                cs = slice(c * 128, (c + 1) * 128)
                _mmr(P, pw[:, cs], kbg[:, c, :], R[:, c, :], True, True, r=["kbg", "R"], w=[tw])
            P.op("act", lambda e, pw=pw: e.activation(out=wT4[q][:, hl, :], in_=pw[:, :], func=AF.Copy), w=[tw, ("wT", q, hl)])
            pu, tu = pc()
            for c in range(4):
                cs = slice(c * 128, (c + 1) * 128)
                _mmr(P, pu[:, cs], R[:, c, :], vb[:, c, :], True, True, r=["vb", "R"], w=[tu])
            P.op("act", lambda e, pu=pu: e.activation(out=u4[q][:, hl, :, :], in_=c4(pu[:, :]), func=AF.Copy), w=[tu, ("u4", q, hl)])

        def scan_step(t, hp, c):
            q = t % 2
            vq = c % 2
            cs = slice(c * 128, (c + 1) * 128)
            for hl in range(HG):
                _mm(P, PSW[:, hl * 128:(hl + 1) * 128], wT4[q][:, hl, cs], Sb[:, hl, :], True, True, r=[("wT", q, hl), "Sb"], w=["PSW"])
            P.op("dve", lambda e: e.tensor_tensor(out=vnew[vq][:], in0=u4[q][:, :, c, :], in1=PSW[:, :].rearrange("p (h e) -> p h e", h=HG),
                                                  op=ALU.subtract),
                 r=[("u4", q, hl) for hl in range(HG)], w=["PSW", ("vnew", vq)])
            for hl in range(HG):
                _mm(P, PSO[:, hl * 128:(hl + 1) * 128], Sb[:, hl, :], qgT[q][:, hl, cs], True, False, r=[("qg", q, hl), "Sb"], w=["PSO"])
                _mm(P, PSO[:, hl * 128:(hl + 1) * 128], vnew[vq][:, hl, :], qkT4[q][:, hl, cs], False, True,
                    r=[("qk", q, hl), ("vnew", vq)], w=["PSO"])
            for hl in range(HG):
                _mm(P, PSD[:, hl * 128:(hl + 1) * 128], kdec[q][:, hl, c, :], vnew[vq][:, hl, :], True, True,
                    r=[("kdec", q, hl), ("vnew", vq)], w=["PSD"])
            for hl in range(HG):
                P.op("dve", lambda e, hl=hl: e.scalar_tensor_tensor(out=Sf[:, hl, :], in0=Sf[:, hl, :], scalar=egl[q][:, c, hl:hl + 1],
                                                                   in1=PSD[:, hl * 128:(hl + 1) * 128], op0=ALU.mult, op1=ALU.add),
                     r=[("egl", q, hl)], w=["PSD", "Sf"])
            P.op("pool", lambda e: e.tensor_copy(out=Sb[:], in_=Sf[:]), r=["Sf"], w=["Sb"])
            P.op("act", lambda e: e.activation(out=oraw[:, :, cs], in_=PSO[:, :].rearrange("p (h i) -> p h i", h=HG), func=AF.Copy),
                 w=["PSO", "oraw"])

        def post(t, hp, hl):
            h = hp * HG + hl
            q = t % 2
            P.op("act", lambda e: e.activation(out=sqo[:], in_=oraw[:, hl, :], func=AF.Square), r=["oraw"], w=["sqo"])
            _mm(P, PB[:, :], g.onesb, sqo[:], True, True, r=["sqo"], w=["PB"])
            P.op("act", lambda e: e.activation(out=sro[:], in_=PB[:, :], func=AF.Ln, scale=1.0 / 128.0, bias=EPS), w=["PB", "sro"])
            P.op("act", lambda e: e.activation(out=rinvo[:], in_=sro[:], func=AF.Exp, scale=-0.5), r=["sro"], w=["rinvo"])
            for kc in range(KC):
                _mm(P, PA[:, :], wz[:, kc, hl * 128:(hl + 1) * 128], ut[q][:, kc, :], kc == 0, kc == KC - 1, r=[("wz", kc), ("ut", q)], w=["PA"])
            P.op("act", lambda e: e.activation(out=sz[:], in_=PA[:, :], func=AF.Silu), w=["PA", "sz"])
            P.op("dve", lambda e: e.tensor_tensor(out=on[:], in0=oraw[:, hl, :], in1=rinvo[:], op=ALU.mult), r=["oraw", "rinvo"], w=["on"])
            ob_ = oab[hl % 2]
            P.op("dve", lambda e: e.scalar_tensor_tensor(out=ob_[:], in0=on[:], scalar=dnncol, in1=sz[:], op0=ALU.mult, op1=ALU.mult),
                 r=["on", "sz"], w=[("oab", hl % 2)])
            P.op("sp", lambda e: e.dma_start(out=g.oaT[h * 128:(h + 1) * 128, t * T:(t + 1) * T], in_=ob_[:]), r=[("oab", hl % 2)], dma=True)

        load_w(P, wba, w_in, KC, "wba_", cols=(O_B, O_B + 16))
        P.op("pool", lambda e: e.memset(g1[:], 0.0), r=[("wba_", kc) for kc in range(KC)], w=["wba"])
        for hp in range(NPASS):
            v3 = w_in.rearrange("(kc p) n -> p kc n", p=128)
            for part in range(3):
                for kc in range(KC):
                    c0 = part * 1024 + hp * HG * 128
                    P.op("pool", lambda e, part=part, kc=kc, c0=c0: e.dma_start(out=wqkv[:, kc, part * HG * 128:(part + 1) * HG * 128],
                                                                              in_=v3[:, kc, c0:c0 + HG * 128]), w=[("wqkv", kc)], dma=True)
            load_w(P, wz, w_in, KC, "wz", cols=(O_Z + hp * HG * 128, O_Z + (hp + 1) * HG * 128))
            P.op("pool", lambda e: e.memset(halo[:], 0.0), w=[("halo", i) for i in range(3 * HG)])
            P.op("pool", lambda e: e.memset(Sf[:], 0.0), w=["Sf"])
            P.op("pool", lambda e: e.memset(Sb[:], 0.0), w=["Sb"])
            gates(0)
            for hl in range(HG):
                pre(0, hp, hl)
            for t in range(NT):
                if t + 1 < NT:
                    gates(t + 1)
                for c in range(4):
                    scan_step(t, hp, c)
                    if t + 1 < NT:
                        pre(t + 1, hp, c)
                for hl in range(HG):
                    post(t, hp, hl)
        P.emit()


def host_consts():
    cf = np.zeros((128, 2176), np.float32)
    j = np.arange(128)[:, None]
    i = np.arange(128)[None, :]
    cf[:, 0:128] = np.eye(128, dtype=np.float32)
    cf[:, 128:256] = np.where(j <= i, 0.0, NEG)
    cf[:, 256:384] = (j < i).astype(np.float32)
    k = j
    q = i
    prev = np.where(q <= k, 128.0 + q - k, BIGD)
    cur = np.where(q >= k, (q - k) * 1.0, BIGD)
    cf[:, 384:512] = prev
    cf[:, 512:640] = cur
    for h in range(8):
        cf[h, 640 + h * 128: 640 + (h + 1) * 128] = 1.0
    rm = np.ones((512,), np.float32)
    rm[0::128] = 0.0
    cf[0:8, 1664:2176] = rm[None, :]
    cb = np.zeros((128, 512), np.float32)
    cb[:, 0:128] = np.eye(128, dtype=np.float32)
    cb[:, 128:256] = 1.0
    cb[0:64, 256:384] = 1.0
    cb[64:128, 384:512] = 1.0
    return cf, cb


def make_in_maps(inputs, ncores=8):
    f = lambda a: np.ascontiguousarray(np.asarray(a, dtype=np.float32))
    cf, cb = host_consts()
    shared = {}
    shared["ada_w"] = f(inputs["ada_w"])
    shared["ada_bT"] = f(np.asarray(inputs["ada_b"]).reshape(DEPTH, 72, 128).transpose(2, 0, 1).reshape(128, DEPTH * 72))
    ln = np.stack([np.asarray(inputs["ln_ffn1"]), np.asarray(inputs["ln_mix"]), np.asarray(inputs["ln_ffn2"])], axis=1)
    lnT = ln.reshape(DEPTH, 3, 8, 128).transpose(3, 0, 1, 2).reshape(128, DEPTH * 24)
    fn = np.asarray(inputs["final_norm"]).reshape(8, 128).T
    shared["lnT"] = f(np.concatenate([lnT, fn], axis=1))
    cw = np.asarray(inputs["conv_w"])
    shared["conv_wT"] = f(cw.reshape(DEPTH, 4, 24, 128).transpose(3, 0, 2, 1).reshape(128, DEPTH * 96))
    shared["dnnT"] = f(np.asarray(inputs["dn_norm"]).T)
    shared["alog"] = f(np.asarray(inputs["a_log"]).T)
    shared["dtb"] = f(np.asarray(inputs["dt_bias"]).T)
    for nm in ("ffn1_wg", "ffn1_wu", "ffn1_wd", "ffn2_wg", "ffn2_wu", "ffn2_wd", "w_in", "w_a", "w_b", "w_o"):
        shared[nm] = f(inputs[nm])
    shared["cf"] = cf
    shared["cb"] = cb
    x = np.asarray(inputs["x"], dtype=np.float32)
    c = np.asarray(inputs["c"], dtype=np.float32)
    maps = []
    for b in range(ncores):
        m = dict(shared)
        m["xT"] = np.ascontiguousarray(x[b].T)
        m["c_col"] = np.ascontiguousarray(c[b].reshape(8, 128).T)
        maps.append(m)
    return maps


def kernel(**inputs):
    nc = bass.Bass("TRN2", target_bir_lowering=False)
    build(nc)
    maps = make_in_maps(inputs, 8)
    res = run_bass_kernel_spmd(nc, maps, core_ids=list(range(8)))
    out = np.stack([np.ascontiguousarray(r["outT"].T) for r in res.results], axis=0)
    return out.astype(np.float32)
```

```python
import numpy as np
import ml_dtypes
import concourse.bass as bass
import concourse.mybir as mybir
from concourse.bass_utils import run_bass_kernel_spmd
from contextlib import ExitStack

F32 = mybir.dt.float32
BF16 = mybir.dt.bfloat16
AF = mybir.ActivationFunctionType
ALU = mybir.AluOpType

D = 1024
S = 4096
KC = 8
DFF = 2816
NFF = 22
DEPTH = 2
INC = 8464
O_Z, O_B, O_A, O_DQ, O_DK, O_DV, O_GA, O_GB = 3072, 4096, 4104, 4112, 4880, 5648, 6416, 7440
EPS = 1e-6
NEG = -1.0e9
BIGD = 1.0e5
SLOPES = [2.0 ** (-8.0 * (h + 1) / 12.0) for h in range(12)]


ENGS = ("pe", "act", "dve", "pool", "sp")
ENGOBJ = {"pe": "tensor", "act": "scalar", "dve": "vector", "pool": "gpsimd", "sp": "sync"}
EPOCH = 12000
NDMASEM = 24


class Sems:
    def __init__(self, nc, stack):
        self.nc, self.stack = nc, stack
        self.sems = {}
        self.cnt = {e: 0 for e in ENGS}
        self.ndma = 0
        self.lastdma = {}

    def get(self, key):
        if key not in self.sems:
            self.sems[key] = self.stack.enter_context(self.nc.semaphore("s_%s_%s" % key))
        return self.sems[key]


class Prog:
    def __init__(self, nc, sems):
        self.nc, self.S = nc, sems
        self.ops = []
        self.last_w = {}
        self.readers = {}

    def op(self, eng, fn, r=(), w=(), dma=False):
        i = len(self.ops)
        deps = set()
        for b in r:
            lw = self.last_w.get(b)
            if lw is not None:
                deps.add(lw)
        for b in w:
            lw = self.last_w.get(b)
            if lw is not None:
                deps.add(lw)
            for x in self.readers.get(b, ()):
                deps.add(x)
        deps.discard(i)
        for b in r:
            self.readers.setdefault(b, []).append(i)
        for b in w:
            self.last_w[b] = i
            self.readers[b] = []
        self.ops.append(dict(eng=eng, fn=fn, deps=deps, dma=dma))
        return i

    def emit(self):
        nc, S, ops = self.nc, self.S, self.ops
        need = [False] * len(ops)
        for i, o in enumerate(ops):
            if o["dma"]:
                need[i] = True
            for d in o["deps"]:
                od = ops[d]
                if od["dma"] or o["dma"] or od["eng"] != o["eng"]:
                    need[d] = True
        for i, o in enumerate(ops):
            if o["dma"]:
                k = S.ndma % NDMASEM
                o["sem"] = S.get(("dma", k))
                o["val"] = 16 * (S.ndma // NDMASEM + 1)
                o["prev"] = S.lastdma.get(k)
                S.lastdma[k] = (o["sem"], o["val"])
                S.ndma += 1
            elif need[i]:
                e = o["eng"]
                o["sem"] = S.get((e, S.cnt[e] // EPOCH))
                o["val"] = S.cnt[e] % EPOCH + 1
                S.cnt[e] += 1
        per = {e: [] for e in ENGS}
        for i, o in enumerate(ops):
            per[o["eng"]].append(i)
        final_dma = list(S.lastdma.values())
        with nc.Block() as block:
            for e in ENGS:
                lst = per[e]
                if not lst and e != "sp":
                    continue

                def body(eng, lst=lst, e=e):
                    waited = {}

                    def wait(sem, val):
                        k = id(sem)
                        if waited.get(k, 0) >= val:
                            return
                        eng.wait_ge(sem, val)
                        waited[k] = val

                    for i in lst:
                        o = ops[i]
                        for d in sorted(o["deps"]):
                            od = ops[d]
                            if od["dma"] or o["dma"] or od["eng"] != e:
                                wait(od["sem"], od["val"])
                        if o["dma"] and o["prev"] is not None:
                            wait(*o["prev"])
                        ins = o["fn"](eng)
                        if o["dma"]:
                            ins.then_inc(o["sem"], 16)
                        elif need[i]:
                            ins.then_inc(o["sem"], 1)
                    if e == "sp":
                        for sem, val in final_dma:
                            wait(sem, val)

                getattr(block, ENGOBJ[e])(body)


class Ctx:
    pass


def _mm(P, out, lhsT, rhs, start, stop, r, w):
    P.op("pe", lambda e: e.matmul(out, lhsT=lhsT, rhs=rhs, start=start, stop=stop), r=r, w=w)


F32R = mybir.dt.float32r


def _fill(P, g, pbank, tok):
    import os
    n = int(os.environ.get("FILL", "1"))
    for _ in range(n):
        P.op("pe", lambda e: e.matmul(pbank[:, :], lhsT=g.onesb, rhs=g.cb_sb[:, 0:512], start=True, stop=True), r=["cb"], w=[tok])


def _mmr(P, out, lhsT, rhs, start, stop, r, w):
    import os
    P.op("pe", lambda e: e.matmul(out, lhsT=lhsT, rhs=rhs, start=start, stop=stop), r=r, w=w)


def build(nc, debug=False, phases=None):
    g = Ctx()
    g.nc = nc
    dk = "ExternalOutput" if debug else "Internal"

    def din(name, shape, dt=F32):
        return nc.dram_tensor(name, shape, dt, kind="ExternalInput").ap()

    g.xT = din("xT", [D, S])
    g.c_col = din("c_col", [128, KC])
    g.ada_w = din("ada_w", [DEPTH, D, 9 * D])
    g.ada_bT = din("ada_bT", [128, DEPTH * 72])
    g.lnT = din("lnT", [128, DEPTH * 24 + 8])
    g.conv_wT = din("conv_wT", [128, DEPTH * 96])
    g.dnnT = din("dnnT", [128, DEPTH])
    g.alog = din("alog", [8, DEPTH])
    g.dtb = din("dtb", [8, DEPTH])
    g.w = {}
    for nm, shp in (("ffn1_wg", [DEPTH, D, DFF]), ("ffn1_wu", [DEPTH, D, DFF]), ("ffn1_wd", [DEPTH, DFF, D]),
                    ("ffn2_wg", [DEPTH, D, DFF]), ("ffn2_wu", [DEPTH, D, DFF]), ("ffn2_wd", [DEPTH, DFF, D]),
                    ("w_in", [DEPTH, D, INC]), ("w_a", [DEPTH, D, D]), ("w_b", [DEPTH, 768, D]), ("w_o", [DEPTH, D, D])):
        g.w[nm] = din(nm, shp)
    g.cf = din("cf", [128, 2176])
    g.cb = din("cb", [128, 512])
    g.outT = nc.dram_tensor("outT", [D, S], F32, kind="ExternalOutput").ap()
    g.hT = nc.dram_tensor("hT", [D, S], F32, kind=dk).ap()
    g.uT = nc.dram_tensor("uT", [D, S], BF16, kind=dk).ap()
    g.oaT = nc.dram_tensor("oaT", [D, S], BF16, kind=dk).ap()
    g.obT = nc.dram_tensor("obT", [768, S], BF16, kind=dk).ap()
    if debug:
        g.dbg = nc.dram_tensor("dbg", [128, 4096], F32, kind="ExternalOutput").ap()

    with ExitStack() as gst:
        g.sems = Sems(nc, gst)

        def gsb(name, shape, dt):
            return gst.enter_context(nc.sbuf_tensor(name, shape, dt))

        g.cf_sb = gsb("cf_sb", [128, 2176], F32)
        g.cb_sb = gsb("cb_sb", [128, 512], BF16)
        g.identf = g.cf_sb[:, 0:128]
        g.negmask = g.cf_sb[:, 128:256]
        g.strict01 = g.cf_sb[:, 256:384]
        g.D2 = g.cf_sb[:, 384:640]
        g.sel = g.cf_sb[0:8, 640:1664]
        g.resetm = g.cf_sb[0:8, 1664:2176]
        g.identb = g.cb_sb[:, 0:128]
        g.onesb = g.cb_sb[:, 128:256]
        g.headsel = g.cb_sb[:, 256:512]
        g.ccol = gsb("ccol", [128, KC], F32)
        g.cact = gsb("cact", [128, KC], BF16)
        g.adab = gsb("adab", [128, DEPTH * 72], F32)
        g.ln = gsb("ln", [128, DEPTH * 24 + 8], F32)
        g.convw = gsb("convw", [128, DEPTH * 96], F32)
        g.dnn = gsb("dnn", [128, DEPTH], F32)
        g.alog_sb = gsb("alog_sb", [8, DEPTH], F32)
        g.dtb_sb = gsb("dtb_sb", [8, DEPTH], F32)
        g.nA = gsb("nA", [8, DEPTH], F32)
        g.modT = gsb("modT", [128, 72], F32)
        g.gm = gsb("gm", [128, 24], F32)
        g.gate = gsb("gate", [128, 24], F32)
        g.zero8 = gsb("zero8", [128, 8], F32)

        plist = []
        plist.append(("setup", lambda: phase_setup(g)))
        for l in range(DEPTH):
            plist.append(("mod%d" % l, lambda l=l: phase_mod(g, l)))
            plist.append(("ffn1_%d" % l, lambda l=l: phase_ffn(g, l, 0, g.xT if l == 0 else g.hT)))
            plist.append(("u%d" % l, lambda l=l: phase_u(g, l)))
            plist.append(("dn%d" % l, lambda l=l: phase_dn(g, l)))
            plist.append(("da%d" % l, lambda l=l: phase_da(g, l)))
            plist.append(("out%d" % l, lambda l=l: phase_out(g, l)))
            plist.append(("ffn2_%d" % l, lambda l=l: phase_ffn(g, l, 2, g.hT)))
        plist.append(("final", lambda: phase_final(g)))
        for name, fn in plist:
            if phases is None or name in phases:
                fn()
    return nc


def new_phase(g):
    st = ExitStack()
    P = Prog(g.nc, g.sems)
    g.pid = getattr(g, "pid", 0) + 1
    pid = g.pid

    def sb(name, shape, dt):
        return st.enter_context(g.nc.sbuf_tensor("p%d_%s" % (pid, name), shape, dt))

    def ps(name, shape=(128, 512), dt=F32):
        return st.enter_context(g.nc.psum_tensor("p%d_%s" % (pid, name), list(shape), dt))

    return st, P, sb, ps


def phase_setup(g):
    st, P, sb, ps = new_phase(g)
    with st:
        P.op("sp", lambda e: e.dma_start(out=g.cf_sb[:], in_=g.cf), w=["cf"], dma=True)
        P.op("pool", lambda e: e.dma_start(out=g.cb_sb[:], in_=g.cb), w=["cb"], dma=True)
        for nm, dst, src in (("ccol", g.ccol, g.c_col), ("adab", g.adab, g.ada_bT), ("ln", g.ln, g.lnT),
                             ("convw", g.convw, g.conv_wT), ("dnn", g.dnn, g.dnnT),
                             ("alog", g.alog_sb, g.alog), ("dtb", g.dtb_sb, g.dtb)):
            P.op("sp", lambda e, dst=dst, src=src: e.dma_start(out=dst[:], in_=src), w=[nm], dma=True)
        P.op("act", lambda e: e.activation(out=g.cact[:], in_=g.ccol[:], func=AF.Silu), r=["ccol"], w=["cact"])
        P.op("act", lambda e: e.activation(out=g.nA[:], in_=g.alog_sb[:], func=AF.Exp), r=["alog"], w=["nA"])
        P.op("dve", lambda e: e.tensor_scalar(out=g.nA[:], in0=g.nA[:], scalar1=-1.0, scalar2=None, op0=ALU.mult), r=["nA"], w=["nA"])
        P.op("dve", lambda e: e.memset(g.zero8[:], 0.0), w=["zero8"])
        P.emit()


def phase_mod(g, l):
    st, P, sb, ps = new_phase(g)
    NB = 8
    BW = 9 * D // NB
    with st:
        wA = [sb("wA%d" % i, [128, KC, BW], BF16) for i in range(2)]
        pm = ps("pm", (128, 72))
        src = g.ada_w[l].rearrange("(kc p) n -> p kc n", p=128)
        for blk in range(NB):
            buf = wA[blk % 2]
            for kc in range(KC):
                P.op("pool", lambda e, buf=buf, kc=kc, blk=blk: e.dma_start(out=buf[:, kc, :], in_=src[:, kc, blk * BW:(blk + 1) * BW]),
                     w=[("wA", blk % 2, kc)], dma=True)
            for j in range(BW // 128):
                col = blk * (BW // 128) + j
                for kc in range(KC):
                    _mm(P, pm[:, col:col + 1], buf[:, kc, j * 128:(j + 1) * 128], g.cact[:, kc:kc + 1], kc == 0, kc == KC - 1,
                        r=[("wA", blk % 2, kc), "cact"], w=["pm"])
        P.op("dve", lambda e: e.tensor_tensor(out=g.modT[:], in0=pm[:], in1=g.adab[:, l * 72:(l + 1) * 72], op=ALU.add),
             r=["pm"], w=["modT"])
        for v in range(3):
            sc = g.modT[:, (3 * v + 1) * 8:(3 * v + 2) * 8]
            gt = g.modT[:, (3 * v + 2) * 8:(3 * v + 3) * 8]
            lnv = g.ln[:, l * 24 + v * 8: l * 24 + v * 8 + 8]
            P.op("dve", lambda e, v=v, sc=sc, lnv=lnv: e.scalar_tensor_tensor(out=g.gm[:, v * 8:(v + 1) * 8], in0=sc, scalar=1.0, in1=lnv,
                                                                             op0=ALU.add, op1=ALU.mult), r=["modT"], w=["gm"])
            P.op("dve", lambda e, v=v, gt=gt: e.tensor_scalar(out=g.gate[:, v * 8:(v + 1) * 8], in0=gt, scalar1=(1.0 if v == 1 else 0.5),
                                                               scalar2=None, op0=ALU.mult), r=["modT"], w=["gate"])
        P.emit()


def load_w(P, dst, src, nk, tag, cols=None):
    v = src.rearrange("(kc p) n -> p kc n", p=128)
    for kc in range(nk):
        s = v[:, kc, :] if cols is None else v[:, kc, cols[0]:cols[1]]
        P.op("pool", lambda e, kc=kc, s=s: e.dma_start(out=dst[:, kc, :], in_=s), w=[(tag, kc)], dma=True)


def phase_ffn(g, l, v, h_src):
    st, P, sb, ps = new_phase(g)
    T = 256
    NT = S // T
    pre = "ffn1" if v == 0 else "ffn2"
    with st:
        wg = sb("wg", [128, KC, DFF], BF16)
        wu = sb("wu", [128, KC, DFF], BF16)
        wd = sb("wd", [128, NFF, D], BF16)
        hin = [sb("hin%d" % i, [128, KC, T], F32) for i in range(2)]
        sq = sb("sq", [128, KC, T], BF16)
        u = sb("u", [128, KC, T], BF16)
        sr = sb("sr", [128, T], F32)
        rinv = sb("rinv", [128, T], F32)
        tmp = [sb("tmp%d" % i, [128, T], F32) for i in range(2)]
        sg = [sb("sg%d" % i, [128, T], F32) for i in range(2)]
        aT = sb("aT", [128, NFF, T], BF16)
        ssum = ps("ssum")
        pgu = [ps("pgu%d" % i) for i in range(2)]
        pd = [ps("pd%d" % i) for i in range(2)]
        load_w(P, wg, g.w[pre + "_wg"][l], KC, "wg")
        load_w(P, wu, g.w[pre + "_wu"][l], KC, "wu")
        load_w(P, wd, g.w[pre + "_wd"][l], NFF, "wd")
        gm_cols = g.gm[:, v * 8:(v + 1) * 8]
        sh_cols = g.modT[:, (3 * v) * 8:(3 * v + 1) * 8]
        gate_cols = g.gate[:, v * 8:(v + 1) * 8]
        hsv = h_src.rearrange("(kc p) t -> p kc t", p=128)
        hdv = g.hT.rearrange("(kc p) t -> p kc t", p=128)
        def f_load(t):
            hb = hin[t % 2]
            tag = "f%d" % (t % 2)
            P.op("sp", lambda e: e.dma_start(out=hb[:], in_=hsv[:, :, t * T:(t + 1) * T]), w=[tag + "h"], dma=True)

        def f_norm(t):
            emit_norm(g, P, hin[t % 2], T, sq, ssum, sr, rinv, tmp, gm_cols, sh_cols, u, "f%d" % (t % 2), utok="ffn_u")

        def f_gateup(t):
            tag = "f%d" % (t % 2)
            for j in range(NFF):
                pb = pgu[j % 2]
                for kc in range(KC):
                    _mm(P, pb[:, 0:T], wg[:, kc, j * 128:(j + 1) * 128], u[:, kc, :], kc == 0, kc == KC - 1,
                        r=[("wg", kc), "ffn_u"], w=[("pgu", j % 2)])
                for kc in range(KC):
                    _mm(P, pb[:, T:2 * T], wu[:, kc, j * 128:(j + 1) * 128], u[:, kc, :], kc == 0, kc == KC - 1,
                        r=[("wu", kc), "ffn_u"], w=[("pgu", j % 2)])
                sgb = sg[j % 2]
                P.op("act", lambda e, pb=pb, sgb=sgb: e.activation(out=sgb[:], in_=pb[:, 0:T], func=AF.Silu),
                     w=[("pgu", j % 2), ("sg", j % 2)])
                P.op("dve", lambda e, pb=pb, sgb=sgb, j=j: e.tensor_tensor(out=aT[:, j, :], in0=pb[:, T:2 * T], in1=sgb[:], op=ALU.mult),
                     r=[("sg", j % 2)], w=[("pgu", j % 2), ("aT", j)])

        def f_down(t):
            hb = hin[t % 2]
            tag = "f%d" % (t % 2)
            for m in range(KC):
                pb = pd[m % 2]
                for j in range(NFF):
                    _mm(P, pb[:, 0:T], wd[:, j, m * 128:(m + 1) * 128], aT[:, j, :], j == 0, j == NFF - 1,
                        r=[("wd", j), ("aT", j)], w=[("pd", m % 2)])
                P.op("dve", lambda e, pb=pb, m=m: e.scalar_tensor_tensor(out=hb[:, m, :], in0=pb[:, 0:T], scalar=gate_cols[:, m:m + 1],
                                                                        in1=hb[:, m, :], op0=ALU.mult, op1=ALU.add),
                     w=[("pd", m % 2), tag + "h"])
            P.op("sp", lambda e: e.dma_start(out=hdv[:, :, t * T:(t + 1) * T], in_=hb[:]), r=[tag + "h"], w=["hT_dram"], dma=True)

        f_load(0)
        f_norm(0)
        for t in range(NT):
            if t + 1 < NT:
                f_load(t + 1)
            f_gateup(t)
            if t + 1 < NT:
                f_norm(t + 1)
            f_down(t)
        P.emit()


def phase_u(g, l):
    st, P, sb, ps = new_phase(g)
    T = 512
    NT = S // T
    with st:
        hin = [sb("hin%d" % i, [128, KC, T], F32) for i in range(2)]
        sq = sb("sq", [128, KC, T], BF16)
        u = [sb("u%d" % i, [128, KC, T], BF16) for i in range(2)]
        sr = sb("sr", [128, T], F32)
        rinv = sb("rinv", [128, T], F32)
        tmp = [sb("tmp%d" % i, [128, T], F32) for i in range(2)]
        ssum = ps("ssum")
        gm_cols = g.gm[:, 8:16]
        sh_cols = g.modT[:, 24:32]
        hsv = g.hT.rearrange("(kc p) t -> p kc t", p=128)
        udv = g.uT.rearrange("(kc p) t -> p kc t", p=128)
        for t in range(NT):
            hb = hin[t % 2]
            ub = u[t % 2]
            tag = "n%d" % (t % 2)
            P.op("sp", lambda e, hb=hb, t=t: e.dma_start(out=hb[:], in_=hsv[:, :, t * T:(t + 1) * T]), w=[tag + "h"], dma=True)
            emit_norm(g, P, hb, T, sq, ssum, sr, rinv, tmp, gm_cols, sh_cols, ub, tag)
            P.op("sp", lambda e, ub=ub, t=t: e.dma_start(out=udv[:, :, t * T:(t + 1) * T], in_=ub[:]), r=[tag + "u"], dma=True)
        P.emit()


def phase_final(g):
    st, P, sb, ps = new_phase(g)
    T = 512
    NT = S // T
    with st:
        hin = [sb("hin%d" % i, [128, KC, T], F32) for i in range(2)]
        sq = sb("sq", [128, KC, T], BF16)
        o = [sb("o%d" % i, [128, KC, T], F32) for i in range(2)]
        sr = sb("sr", [128, T], F32)
        rinv = sb("rinv", [128, T], F32)
        ssum = ps("ssum")
        gm_cols = g.ln[:, DEPTH * 24:DEPTH * 24 + 8]
        hsv = g.hT.rearrange("(kc p) t -> p kc t", p=128)
        odv = g.outT.rearrange("(kc p) t -> p kc t", p=128)
        for t in range(NT):
            hb = hin[t % 2]
            ob = o[t % 2]
            tag = "n%d" % (t % 2)
            P.op("sp", lambda e, hb=hb, t=t: e.dma_start(out=hb[:], in_=hsv[:, :, t * T:(t + 1) * T]), w=[tag + "h"], dma=True)
            emit_norm(g, P, hb, T, sq, ssum, sr, rinv, None, gm_cols, None, ob, tag, out_f32=True)
            P.op("sp", lambda e, ob=ob, t=t: e.dma_start(out=odv[:, :, t * T:(t + 1) * T], in_=ob[:]), r=[tag + "u"], dma=True)
        P.emit()


def emit_norm(g, P, hin, T, sq, ssum, sr, rinv, tmp, gm_cols, sh_cols, out_tile, tag, out_f32=False, utok=None):
    utok = utok or (tag + "u")
    hflat = hin[:].rearrange("p a b -> p (a b)")
    sqflat = sq[:].rearrange("p a b -> p (a b)")
    P.op("act", lambda e: e.activation(out=sqflat, in_=hflat, func=AF.Square), r=[tag + "h"], w=["n_sq"])
    for kc in range(KC):
        _mm(P, ssum[:, 0:T], g.onesb, sq[:, kc, :], kc == 0, kc == KC - 1, r=["n_sq"], w=["n_ssum"])
    P.op("act", lambda e: e.activation(out=sr[:, 0:T], in_=ssum[:, 0:T], func=AF.Sqrt, scale=1.0 / D, bias=EPS),
         w=["n_ssum", "n_sr"])
    P.op("dve", lambda e: e.reciprocal(out=rinv[:, 0:T], in_=sr[:, 0:T]), r=["n_sr"], w=["n_rinv"])
    for kc in range(KC):
        if out_f32:
            P.op("dve", lambda e, kc=kc: e.scalar_tensor_tensor(out=out_tile[:, kc, :], in0=hin[:, kc, :], scalar=gm_cols[:, kc:kc + 1],
                                                               in1=rinv[:, 0:T], op0=ALU.mult, op1=ALU.mult),
                 r=[tag + "h", "n_rinv"], w=[utok])
        else:
            tb = tmp[kc % 2]
            P.op("dve", lambda e, kc=kc, tb=tb: e.scalar_tensor_tensor(out=tb[:, 0:T], in0=hin[:, kc, :], scalar=gm_cols[:, kc:kc + 1],
                                                                      in1=rinv[:, 0:T], op0=ALU.mult, op1=ALU.mult),
                 r=[tag + "h", "n_rinv"], w=[("n_tmp", kc % 2)])
            P.op("act", lambda e, kc=kc, tb=tb: e.activation(out=out_tile[:, kc, :], in_=tb[:, 0:T], func=AF.Identity,
                                                             bias=sh_cols[:, kc:kc + 1]),
                 r=[("n_tmp", kc % 2)], w=[utok])


AX = mybir.AxisListType


def phase_out(g, l):
    st, P, sb, ps = new_phase(g)
    T = 256
    NT = S // T
    with st:
        wa = sb("wa", [128, 8, D], BF16)
        wb = sb("wb", [128, 6, D], BF16)
        wo = sb("wo", [128, 8, D], BF16)
        wga = sb("wga", [128, 8, D], BF16)
        wgb = sb("wgb", [128, 8, D], BF16)
        load_w(P, wa, g.w["w_a"][l], 8, "wa")
        load_w(P, wb, g.w["w_b"][l], 6, "wb")
        load_w(P, wo, g.w["w_o"][l], 8, "wo")
        load_w(P, wga, g.w["w_in"][l], 8, "wga", cols=(O_GA, O_GA + D))
        load_w(P, wgb, g.w["w_in"][l], 8, "wgb", cols=(O_GB, O_GB + D))
        ut = [sb("ut%d" % i, [128, 8, T], BF16) for i in range(2)]
        oa = [sb("oa%d" % i, [128, 8, T], BF16) for i in range(2)]
        ob = [sb("ob%d" % i, [128, 6, T], BF16) for i in range(2)]
        hb_ = [sb("hb%d" % i, [128, 8, T], F32) for i in range(2)]
        mg = sb("mg", [128, 8, T], BF16)
        sgab = [sb("sgab%d" % i, [128, 2 * T], F32) for i in range(2)]
        t12 = [sb("t12%d" % i, [128, 2 * T], F32) for i in range(2)]
        bA = [ps("bA%d" % i) for i in range(2)]
        bB = [ps("bB%d" % i) for i in range(2)]
        bC = [ps("bC%d" % i) for i in range(2)]
        gate_cols = g.gate[:, 8:16]
        uv = g.uT.rearrange("(kc p) t -> p kc t", p=128)
        oav = g.oaT.rearrange("(kc p) t -> p kc t", p=128)
        obv = g.obT.rearrange("(kc p) t -> p kc t", p=128)
        hv = g.hT.rearrange("(kc p) t -> p kc t", p=128)
        for t in range(NT):
            q = t % 2
            sl = slice(t * T, (t + 1) * T)
            P.op("sp", lambda e, q=q, sl=sl: e.dma_start(out=ut[q][:], in_=uv[:, :, sl]), w=[("ut", q)], dma=True)
            P.op("sp", lambda e, q=q, sl=sl: e.dma_start(out=oa[q][:], in_=oav[:, :, sl]), w=[("oa", q)], dma=True)
            P.op("sp", lambda e, q=q, sl=sl: e.dma_start(out=ob[q][:], in_=obv[:, :, sl]), w=[("ob", q)], dma=True)
            P.op("sp", lambda e, q=q, sl=sl: e.dma_start(out=hb_[q][:], in_=hv[:, :, sl]), w=[("hb", q)], dma=True)
            for m in range(8):
                mi = m % 2
                ms = slice(m * 128, (m + 1) * 128)
                for c in range(8):
                    _mm(P, bA[mi][:, 0:T], wa[:, c, ms], oa[q][:, c, :], c == 0, c == 7, r=[("wa", c), ("oa", q)], w=[("bA", mi)])
                for c in range(6):
                    _mm(P, bA[mi][:, T:2 * T], wb[:, c, ms], ob[q][:, c, :], c == 0, c == 5, r=[("wb", c), ("ob", q)], w=[("bA", mi)])
                for c in range(8):
                    _mm(P, bB[mi][:, 0:T], wga[:, c, ms], ut[q][:, c, :], c == 0, c == 7, r=[("wga", c), ("ut", q)], w=[("bB", mi)])
                for c in range(8):
                    _mm(P, bB[mi][:, T:2 * T], wgb[:, c, ms], ut[q][:, c, :], c == 0, c == 7, r=[("wgb", c), ("ut", q)], w=[("bB", mi)])
                P.op("act", lambda e, mi=mi: e.activation(out=sgab[mi][:], in_=bB[mi][:, :], func=AF.Sigmoid), w=[("bB", mi), ("sgab", mi)])
                P.op("dve", lambda e, mi=mi: e.tensor_tensor(out=t12[mi][:], in0=bA[mi][:, :], in1=sgab[mi][:], op=ALU.mult),
                     r=[("sgab", mi)], w=[("bA", mi), ("t12", mi)])
                P.op("pool", lambda e, mi=mi, m=m: e.tensor_tensor(out=mg[:, m, :], in0=t12[mi][:, 0:T], in1=t12[mi][:, T:2 * T], op=ALU.add),
                     r=[("t12", mi)], w=[("mg", m)])
            for m in range(8):
                mi = m % 2
                ms = slice(m * 128, (m + 1) * 128)
                for c in range(8):
                    _mm(P, bC[mi][:, 0:T], wo[:, c, ms], mg[:, c, :], c == 0, c == 7, r=[("wo", c), ("mg", c)], w=[("bC", mi)])
                P.op("dve", lambda e, mi=mi, m=m, q=q: e.scalar_tensor_tensor(out=hb_[q][:, m, :], in0=bC[mi][:, 0:T], scalar=gate_cols[:, m:m + 1],
                                                                             in1=hb_[q][:, m, :], op0=ALU.mult, op1=ALU.add),
                     w=[("bC", mi), ("hb", q)])
            P.op("sp", lambda e, q=q, sl=sl: e.dma_start(out=hv[:, :, sl], in_=hb_[q][:]), r=[("hb", q)], w=["hT_dram"], dma=True)
        P.emit()


def phase_da(g, l):
    st, P, sb, ps = new_phase(g)
    import os
    NG = int(os.environ.get("DA_NG", 6))
    with st:
        uT = sb("uT", [128, KC, S], BF16)
        udv = g.uT.rearrange("(kc p) t -> p kc t", p=128)
        for kc in range(KC):
            P.op("sp", lambda e, kc=kc: e.dma_start(out=uT[:, kc, :], in_=udv[:, kc, :]), w=[("uT", kc)], dma=True)
        wq = [sb("wq%d" % i, [128, KC, 128], BF16) for i in range(2)]
        wk = [sb("wk%d" % i, [128, KC, 128], BF16) for i in range(2)]
        wv = [sb("wv%d" % i, [128, KC, 128], BF16) for i in range(2)]
        QT = sb("QT", [128, S], BF16)
        KT = sb("KT", [128, S], BF16)
        VT = sb("VT", [128, S], BF16)
        acc = sb("acc", [128, 2, S], F32)
        sqt = sb("sqt", [128, 512], BF16)
        mx = sb("mx", [128, 4], F32)
        tm = sb("tm", [128, 2], F32)
        pjn = [0]
        prod = sb("prod", [128, 2], F32)
        negm = sb("negm", [128, 2], F32)
        Vaug = [sb("Vaug%d" % i, [128, 2, 128], BF16) for i in range(3)]
        sbt = [sb("sbt%d" % i, [128, 2, 256], F32) for i in range(2)]
        PT = [sb("PT%d" % i, [128, 2, 256], BF16) for i in range(2)]
        rl = sb("rl", [128, S], F32)
        rl2 = sb("rl2", [128, S], F32)
        ob = sb("ob", [128, S], BF16)
        B = [ps("B%d" % i) for i in range(7)]
        B.insert(1, None)
        pvt = ps("pvt", (128, 1024), BF16)
        pqk = B[0]
        pss = B[0]
        pst = [[B[2], B[3]], [B[4], B[5]]]
        ppv = [B[6], B[7]]
        for i in range(3):
            P.op("pool", lambda e, i=i: e.memset(Vaug[i][:], 1.0), w=[("V", i)])
        vi = 0
        bi = 0
        for gi in range(NG):
            gq = gi % 2
            w_in = g.w["w_in"][l]
            load_w(P, wq[gq], w_in, KC, ("wq", gq), cols=(O_DQ + gi * 128, O_DQ + (gi + 1) * 128))
            load_w(P, wk[gq], w_in, KC, ("wk", gq), cols=(O_DK + gi * 128, O_DK + (gi + 1) * 128))
            load_w(P, wv[gq], w_in, KC, ("wv", gq), cols=(O_DV + gi * 128, O_DV + (gi + 1) * 128))
            P.op("dve", lambda e: e.memset(mx[:], 0.0), w=["mx"])
            for t in range(8):
                sl = slice(t * 512, (t + 1) * 512)
                for (W, wn, dst, dn, mc) in ((wq[gq], "wq", QT, "QT", 0), (wk[gq], "wk", KT, "KT", 2), (wv[gq], "wv", VT, "VT", -1)):
                    pjn[0] += 1
                    pq_, pqt = (B[0], "B0") if pjn[0] % 2 == 0 else (B[2], "B2")
                    for kc in range(KC):
                        _mm(P, pq_[:, :], W[:, kc, :], uT[:, kc, sl], kc == 0, kc == KC - 1, r=[((wn, gq), kc), ("uT", kc)], w=[pqt])
                    P.op("act", lambda e, dst=dst, sl=sl, pq_=pq_: e.activation(out=dst[:, sl], in_=pq_[:, :], func=AF.Copy), w=[pqt, dn])
                    if mc < 0:
                        continue
                    P.op("act", lambda e, pq_=pq_: e.activation(out=sqt[:], in_=pq_[:, :], func=AF.Square), w=[pqt, "sqt"])
                    for hh in range(2):
                        ps_, pst_ = (B[3], "B3") if hh == 0 else (B[4], "B4")
                        _mm(P, ps_[:, :], g.headsel[:, hh * 128:(hh + 1) * 128], sqt[:], True, True, r=["sqt"], w=[pst_])
                        P.op("dve", lambda e, ps_=ps_, hh=hh: e.reduce_max(out=tm[:, hh:hh + 1], in_=ps_[:, :], axis=AX.X), w=[pst_, ("tm", hh)])
                        P.op("dve", lambda e, c=mc + hh, hh=hh: e.tensor_tensor(out=mx[:, c:c + 1], in0=mx[:, c:c + 1], in1=tm[:, hh:hh + 1], op=ALU.max),
                             r=[("tm", hh)], w=["mx"])
            P.op("dve", lambda e: e.tensor_tensor(out=prod[:], in0=mx[:, 0:2], in1=mx[:, 2:4], op=ALU.mult), r=["mx"], w=["prod"])
            P.op("act", lambda e: e.activation(out=prod[:], in_=prod[:], func=AF.Sqrt), w=["prod"])
            P.op("dve", lambda e: e.tensor_scalar(out=negm[:], in0=prod[:], scalar1=-0.125, scalar2=None, op0=ALU.mult), r=["prod"], w=["negm"])
            blocks = []
            for pi, r_ in enumerate((1, 4, 16)):
                nblk = 32 // r_
                for p_ in range(r_):
                    for b in range(nblk):
                        blocks.append((pi, r_, p_, b))

            def mk(pi, r_, p_, b, vi_, bi_):
                def tok(bb):
                    s0 = p_ + r_ * 128 * bb
                    return slice(s0, s0 + r_ * 127 + 1, r_)
                cur = vi_ % 3
                prv = (vi_ - 1) % 3
                bq = bi_ % 2
                tb = tok(b)
                lo = 128 if b == 0 else 0

                def stageA():
                    P.op("pe", lambda e: e.transpose(out=pvt[:, 0:128], in_=VT[:, tb], identity=g.identb), r=["VT"], w=["pvt"])
                    P.op("act", lambda e: e.activation(out=Vaug[cur][:, :, 64:128],
                                                       in_=pvt[:, 0:128].rearrange("p (h d) -> p h d", h=2), func=AF.Copy),
                         w=["pvt", ("V", cur)])
                    for hh in range(2):
                        rows = slice(64 * hh, 64 * hh + 64)
                        pb = pst[bq][hh]
                        btok = "B%d" % (2 + 2 * bq + hh)
                        if b > 0:
                            _mm(P, pb[:, 0:128], KT[rows, tok(b - 1)], QT[rows, tb], True, True, r=["KT", "QT"], w=[btok])
                        _mm(P, pb[:, 128:256], KT[rows, tb], QT[rows, tb], True, True, r=["KT", "QT"], w=[btok])
                        cc = -8.0 * SLOPES[gi * 2 + hh] * r_
                        P.op("dve", lambda e, pb=pb, hh=hh, cc=cc: e.scalar_tensor_tensor(
                            out=sbt[bq][:, hh, lo:256], in0=g.D2[:, lo:256], scalar=cc, in1=pb[:, lo:256], op0=ALU.mult, op1=ALU.add),
                            w=[btok, ("sbt", bq, hh)])
                        P.op("act", lambda e, hh=hh: e.activation(out=PT[bq][:, hh, lo:256], in_=sbt[bq][:, hh, lo:256],
                                                                 func=AF.Exp, scale=0.125, bias=negm[:, hh:hh + 1]),
                             r=[("sbt", bq, hh), "negm"], w=[("PT", bq, hh)])

                def stageB():
                    pp = ppv[bq]
                    ptok = "B%d" % (6 + bq)
                    for hh in range(2):
                        if b > 0:
                            _mm(P, pp[:, hh * 128:(hh + 1) * 128], Vaug[prv][:, hh, :], PT[bq][:, hh, 0:128], True, False,
                                r=[("V", prv), ("PT", bq, hh)], w=[ptok])
                        _mm(P, pp[:, hh * 128:(hh + 1) * 128], Vaug[cur][:, hh, :], PT[bq][:, hh, 128:256], b == 0, True,
                            r=[("V", cur), ("PT", bq, hh)], w=[ptok])
                    ppv3 = pp[:, 0:256].rearrange("p (h q) -> p h q", h=2)
                    if pi == 0:
                        P.op("dve", lambda e: e.tensor_copy(out=acc[:, :, tb], in_=ppv3), w=[ptok, "acc"])
                    else:
                        P.op("dve", lambda e: e.tensor_tensor(out=acc[:, :, tb], in0=ppv3, in1=acc[:, :, tb], op=ALU.add),
                             w=[ptok, "acc"])
                return stageA, stageB

            prevB = None
            for blk in blocks:
                sA, sB = mk(*blk, vi, bi)
                vi += 1
                bi += 1
                sA()
                if prevB is not None:
                    prevB()
                prevB = sB
            prevB()
            for hh in range(2):
                P.op("dve", lambda e, hh=hh: e.reciprocal(out=rl[0:64, :], in_=acc[0:64, hh, :]), r=["acc"], w=["rl"])
                P.op("act", lambda e: e.activation(out=rl2[64:128, :], in_=rl[0:64, :], func=AF.Copy), r=["rl"], w=["rl2"])
                P.op("pool", lambda e, hh=hh: e.tensor_tensor(out=ob[64:128, :], in0=acc[64:128, hh, :], in1=rl2[64:128, :], op=ALU.mult),
                     r=["acc", "rl2"], w=["ob"])
                r0 = (2 * gi + hh) * 64
                P.op("sp", lambda e, r0=r0: e.dma_start(out=g.obT[r0:r0 + 64, :], in_=ob[64:128, :]), r=["ob"], dma=True)
        P.emit()


def phase_dn(g, l):
    st, P, sb, ps = new_phase(g)
    import os
    HG = 4
    NPASS = int(os.environ.get("DN_NPASS", 2))
    NT = int(os.environ.get("DN_NT", 8))
    T = 512
    with st:
        wqkv = sb("wqkv", [128, KC, 3 * HG * 128], BF16)
        wz = sb("wz", [128, KC, HG * 128], BF16)
        wba = sb("wba", [128, KC, 16], BF16)
        ut = [sb("ut%d" % i, [128, KC, T], BF16) for i in range(2)]
        betaT = sb("betaT", [8, T], F32)
        g1 = sb("g1", [8, T], F32)
        g2 = sb("g2", [8, T], F32)
        g3 = sb("g3", [8, T], F32)
        gcT = sb("gcT", [8, T], F32)
        tk = sb("tk", [128, 4, 16], F32)
        egc = sb("egc", [128, 4, 8], F32)
        negc = sb("negc", [128, 4, 8], F32)
        bege = sb("bege", [128, 4, 8], F32)
        glb = sb("glb", [128, 4, 8], F32)
        dl = sb("dl", [128, 4, 8], F32)
        edl = sb("edl", [128, 4, 8], F32)
        egl = [sb("egl%d" % i, [128, 4, HG], F32) for i in range(2)]
        halo = sb("halo", [128, 3 * HG, 3], F32)
        xpre = [sb("xpre%d" % i, [128, T + 3], F32) for i in range(2)]
        yb = [sb("yb%d" % i, [128, T], F32) for i in range(2)]
        sbf = [sb("sbf%d" % i, [128, T], F32) for i in range(2)]
        sqh = sb("sqh", [128, T], BF16)
        ctmp = sb("ctmp", [128, T], F32)
        srn = sb("srn", [128, T], F32)
        rinvn = sb("rinvn", [128, T], F32)
        qT = [sb("qT%d" % i, [128, T], BF16) for i in range(2)]
        kT = [sb("kT%d" % i, [128, T], BF16) for i in range(2)]
        vT = [sb("vT%d" % i, [128, T], BF16) for i in range(2)]
        egcb = sb("egcb", [128, T], F32)
        tE = sb("tE", [128, 4, 128], F32)
        E4 = sb("E4", [128, 4, 128], F32)
        BBs = sb("BBs", [128, 4, 128], F32)
        EBs = sb("EBs", [128, 4, 128], F32)
        import os as _os
        DT_T = F32R if _os.environ.get("USE_F32R") else F32
        A = [sb("A%d" % i, [128, 4, 128], DT_T) for i in range(2)]
        Bm = [sb("Bm%d" % i, [128, 4, 128], DT_T) for i in range(2)]
        R = sb("R", [128, 4, 128], DT_T)
        kbg = sb("kbg", [128, 4, 128], DT_T)
        vb = sb("vb", [128, 4, 128], DT_T)
        identr = sb("identr", [128, 128], DT_T)
        P.op("act", lambda e: e.activation(out=identr[:], in_=g.identf, func=AF.Copy), w=["identr"])
        wT4 = [sb("wT4%d" % i, [128, HG, T], BF16) for i in range(2)]
        u4 = [sb("u4%d" % i, [128, HG, 4, 128], F32) for i in range(2)]
        qgT = [sb("qgT%d" % i, [128, HG, T], BF16) for i in range(2)]
        qkT4 = [sb("qkT4%d" % i, [128, HG, T], BF16) for i in range(2)]
        kdec = [sb("kdec%d" % i, [128, HG, 4, 128], BF16) for i in range(2)]
        Sf = sb("Sf", [128, HG, 128], F32)
        Sb = sb("Sb", [128, HG, 128], BF16)
        vnew = [sb("vnew%d" % i, [128, HG, 128], BF16) for i in range(2)]
        oraw = sb("oraw", [128, HG, T], F32)
        sqo = sb("sqo", [128, T], BF16)
        sro = sb("sro", [128, T], F32)
        rinvo = sb("rinvo", [128, T], F32)
        sz = sb("sz", [128, T], F32)
        on = sb("on", [128, T], F32)
        oab = [sb("oab%d" % i, [128, T], BF16) for i in range(2)]
        PA = ps("PA")
        PB = ps("PB")
        PC = [ps("PC%d" % i) for i in range(2)]
        PTt = ps("PTt", (128, 1024), BF16)
        PSW = ps("PSW")
        PSO = ps("PSO")
        PSD = ps("PSD")
        pcn = [0]

        def pc():
            i = pcn[0] % 2
            pcn[0] += 1
            return PC[i], ("PC", i)

        def c4(ap):
            return ap.rearrange("p (c i) -> p c i", c=4)

        def bc4(ap2):
            return ap2.unsqueeze(1).to_broadcast([128, 4, 128])

        def colbc(ap_c):
            return ap_c.unsqueeze(2).to_broadcast([128, 4, 128])

        w_in = g.w["w_in"][l]
        udv = g.uT.rearrange("(kc p) t -> p kc t", p=128)
        nAcol = g.nA[:, l:l + 1]
        dtbcol = g.dtb_sb[:, l:l + 1]
        dnncol = g.dnn[:, l:l + 1]

        def gates(t):
            q = t % 2
            P.op("sp", lambda e: e.dma_start(out=ut[q][:], in_=udv[:, :, t * T:(t + 1) * T]), w=[("ut", q)], dma=True)
            for kc in range(KC):
                _mm(P, PA[0:8, :], wba[:, kc, 0:8], ut[q][:, kc, :], kc == 0, kc == KC - 1, r=["wba", ("ut", q)], w=["PA"])
            P.op("act", lambda e: e.activation(out=betaT[:], in_=PA[0:8, :], func=AF.Sigmoid), w=["PA", "betaT"])
            for kc in range(KC):
                _mm(P, PA[0:8, :], wba[:, kc, 8:16], ut[q][:, kc, :], kc == 0, kc == KC - 1, r=["wba", ("ut", q)], w=["PA"])
            P.op("dve", lambda e: e.tensor_scalar(out=g1[:], in0=PA[0:8, :], scalar1=dtbcol, scalar2=None, op0=ALU.add), w=["PA", "g1"])
            P.op("dve", lambda e: e.tensor_scalar(out=g2[:], in0=g1[:], scalar1=-1.0, scalar2=None, op0=ALU.mult), r=["g1"], w=["g2"])
            P.op("dve", lambda e: e.tensor_tensor(out=g2[:], in0=g2[:], in1=g1[:], op=ALU.max), r=["g1"], w=["g2"])
            P.op("act", lambda e: e.activation(out=g2[:], in_=g2[:], func=AF.Exp, scale=-1.0), w=["g2"])
            P.op("act", lambda e: e.activation(out=g2[:], in_=g2[:], func=AF.Ln, bias=1.0), w=["g2"])
            P.op("dve", lambda e: e.tensor_scalar(out=g1[:], in0=g1[:], scalar1=0.0, scalar2=None, op0=ALU.max), w=["g1"])
            P.op("dve", lambda e: e.tensor_tensor(out=g1[:], in0=g1[:], in1=g2[:], op=ALU.add), r=["g2"], w=["g1"])
            P.op("dve", lambda e: e.tensor_scalar(out=g3[:], in0=g1[:], scalar1=nAcol, scalar2=None, op0=ALU.mult), r=["g1"], w=["g3"])
            P.op("dve", lambda e: e.tensor_tensor_scan(out=gcT[:], data0=g.resetm, data1=g3[:], initial=0.0, op0=ALU.mult, op1=ALU.add),
                 r=["g3"], w=["gcT"])
            for c in range(4):
                cs = slice(c * 128, (c + 1) * 128)
                _mm(P, PB[:, c * 16:c * 16 + 8], gcT[0:8, cs], g.identf[0:8, 0:8], True, True, r=["gcT"], w=["PB"])
                _mm(P, PB[:, c * 16 + 8:c * 16 + 16], betaT[0:8, cs], g.identf[0:8, 0:8], True, True, r=["betaT"], w=["PB"])
            tkf = tk[:].rearrange("p c k -> p (c k)")
            P.op("act", lambda e: e.activation(out=tkf, in_=PB[:, 0:64], func=AF.Copy), w=["PB", "tk"])
            P.op("act", lambda e: e.activation(out=egc[:], in_=tk[:, :, 0:8], func=AF.Exp), r=["tk"], w=["egc"])
            P.op("dve", lambda e: e.tensor_scalar(out=negc[:], in0=tk[:, :, 0:8], scalar1=-1.0, scalar2=None, op0=ALU.mult), r=["tk"], w=["negc"])
            P.op("dve", lambda e: e.tensor_tensor(out=bege[:], in0=tk[:, :, 8:16], in1=egc[:], op=ALU.mult), r=["tk", "egc"], w=["bege"])

        def pre(t, hp, hl):
            h = hp * HG + hl
            q = t % 2
            ws = hl % 2
            for part in range(3):
                fcl = part * HG + hl
                fcg = part * 8 + h
                xi = part % 2
                xp = xpre[xi]
                for kc in range(KC):
                    _mm(P, PA[:, :], wqkv[:, kc, fcl * 128:(fcl + 1) * 128], ut[q][:, kc, :], kc == 0, kc == KC - 1,
                        r=[("wqkv", kc), ("ut", q)], w=["PA"])
                P.op("pool", lambda e, xp=xp, fcl=fcl: e.tensor_copy(out=xp[:, 0:3], in_=halo[:, fcl, :]), r=[("halo", fcl)], w=[("xpre", xi)])
                P.op("act", lambda e, xp=xp: e.activation(out=xp[:, 3:T + 3], in_=PA[:, :], func=AF.Copy), w=["PA", ("xpre", xi)])
                P.op("pool", lambda e, xp=xp, fcl=fcl: e.tensor_copy(out=halo[:, fcl, :], in_=xp[:, T:T + 3]), r=[("xpre", xi)], w=[("halo", fcl)])
                y = yb[xi]
                cwb = l * 96 + fcg * 4
                P.op("dve", lambda e, xp=xp, y=y, cwb=cwb: e.tensor_scalar(out=y[:], in0=xp[:, 0:T], scalar1=g.convw[:, cwb:cwb + 1], scalar2=None,
                                                                       op0=ALU.mult), r=[("xpre", xi)], w=[("y", xi)])
                for j in range(1, 4):
                    P.op("dve", lambda e, xp=xp, y=y, cwb=cwb, j=j: e.scalar_tensor_tensor(out=y[:], in0=xp[:, j:j + T],
                                                                                      scalar=g.convw[:, cwb + j:cwb + j + 1],
                                                                                      in1=y[:], op0=ALU.mult, op1=ALU.add),
                         r=[("xpre", xi)], w=[("y", xi)])
                if part == 2:
                    P.op("act", lambda e, y=y: e.activation(out=vT[ws][:], in_=y[:], func=AF.Silu), r=[("y", xi)], w=[("vT", ws)])
                else:
                    s_ = sbf[xi]
                    dst = qT[ws] if part == 0 else kT[ws]
                    dn_ = ("qT", ws) if part == 0 else ("kT", ws)
                    P.op("act", lambda e, y=y, s_=s_: e.activation(out=s_[:], in_=y[:], func=AF.Silu), r=[("y", xi)], w=[("sbf", xi)])
                    P.op("act", lambda e, s_=s_: e.activation(out=sqh[:], in_=s_[:], func=AF.Square), r=[("sbf", xi)], w=["sqh"])
                    _mm(P, PB[:, :], g.onesb, sqh[:], True, True, r=["sqh"], w=["PB"])
                    scl = 128.0 if part == 0 else 1.0
                    P.op("act", lambda e, scl=scl: e.activation(out=srn[:], in_=PB[:, :], func=AF.Ln, scale=scl, bias=EPS * scl), w=["PB", "srn"])
                    P.op("act", lambda e: e.activation(out=rinvn[:], in_=srn[:], func=AF.Exp, scale=-0.5), r=["srn"], w=["rinvn"])
                    P.op("dve", lambda e, s_=s_, dst=dst: e.tensor_tensor(out=dst[:], in0=s_[:], in1=rinvn[:], op=ALU.mult),
                         r=[("sbf", xi), "rinvn"], w=[dn_])
            selh = g.sel[:, h * 128:(h + 1) * 128]
            _mm(P, PB[:, :], selh, gcT[:], True, True, r=["gcT"], w=["PB"])
            P.op("act", lambda e: e.activation(out=glb[:, :, h], in_=PB[:, 127:512:128], func=AF.Copy), w=["PB", ("glb", h)])
            P.op("act", lambda e: e.activation(out=egcb[:], in_=PB[:, :], func=AF.Exp), w=["PB", "egcb"])
            P.op("dve", lambda e: e.tensor_tensor(out=tE[:], in0=c4(PB[:, :]), in1=bc4(g.negmask), op=ALU.add), w=["PB", "tE"])
            P.op("pool", lambda e: e.tensor_tensor(out=qgT[q][:, hl, :], in0=qT[ws][:], in1=egcb[:], op=ALU.mult),
                 r=[("qT", ws), "egcb"], w=[("qg", q, hl)])
            P.op("dve", lambda e: e.tensor_tensor(out=dl[:, :, h], in0=glb[:, :, h], in1=tk[:, :, h], op=ALU.subtract),
                 r=[("glb", h), "tk"], w=[("dl", h)])
            P.op("act", lambda e: e.activation(out=edl[:, :, h], in_=dl[:, :, h], func=AF.Exp), r=[("dl", h)], w=[("edl", h)])
            P.op("act", lambda e: e.activation(out=egl[q][:, :, hl], in_=glb[:, :, h], func=AF.Exp), r=[("glb", h)], w=[("egl", q, hl)])
            for c in range(4):
                P.op("act", lambda e, c=c: e.activation(out=E4[:, c, :], in_=tE[:, c, :], func=AF.Exp, bias=negc[:, c, h:h + 1]),
                     r=["tE", "negc"], w=["E4"])
            _mm(P, PB[:, :], selh, betaT[:], True, True, r=["betaT"], w=["PB"])
            P.op("dve", lambda e: e.tensor_tensor(out=BBs[:], in0=c4(PB[:, :]), in1=bc4(g.strict01), op=ALU.mult), w=["PB", "BBs"])
            P.op("pool", lambda e: e.tensor_tensor(out=EBs[:], in0=E4[:], in1=BBs[:], op=ALU.mult), r=["E4", "BBs"], w=["EBs"])
            for c in range(4):
                cs = slice(c * 128, (c + 1) * 128)
                P.op("pe", lambda e, cs=cs: e.transpose(out=PTt[:, cs], in_=kT[ws][:, cs], identity=g.identb), r=[("kT", ws)], w=["PT"])
            for c in range(4):
                cs = slice(c * 128, (c + 1) * 128)
                P.op("pe", lambda e, cs=cs, c=c: e.transpose(out=PTt[:, 512 + c * 128:512 + (c + 1) * 128], in_=vT[ws][:, cs], identity=g.identb),
                     r=[("vT", ws)], w=["PT"])
            P.op("dve", lambda e: e.tensor_tensor(out=kbg[:], in0=c4(PTt[:, 0:512]), in1=colbc(bege[:, :, h]), op=ALU.mult),
                 r=["bege"], w=["PT", "kbg"])
            P.op("dve", lambda e: e.tensor_tensor(out=kdec[q][:, hl, :, :], in0=c4(PTt[:, 0:512]), in1=colbc(edl[:, :, h]), op=ALU.mult),
                 r=[("edl", h)], w=["PT", ("kdec", q, hl)])
            P.op("dve", lambda e: e.tensor_tensor(out=vb[:], in0=c4(PTt[:, 512:1024]), in1=colbc(tk[:, :, 8 + h]), op=ALU.mult),
                 r=["tk"], w=["PT", "vb"])
            pkk, tkk = pc()
            for c in range(4):
                cs = slice(c * 128, (c + 1) * 128)
                _mm(P, pkk[:, cs], kT[ws][:, cs], kT[ws][:, cs], True, True, r=[("kT", ws)], w=[tkk])
            pqk, tqk = pc()
            for c in range(4):
                cs = slice(c * 128, (c + 1) * 128)
                _mm(P, pqk[:, cs], kT[ws][:, cs], qT[ws][:, cs], True, True, r=[("kT", ws), ("qT", ws)], w=[tqk])
            P.op("dve", lambda e: e.tensor_tensor(out=A[0][:], in0=c4(pkk[:, :]), in1=EBs[:], op=ALU.mult), r=["EBs"], w=[tkk, ("A", 0)])
            P.op("dve", lambda e: e.tensor_tensor(out=c4(qkT4[q][:, hl, :]), in0=c4(pqk[:, :]), in1=E4[:], op=ALU.mult),
                 r=["E4"], w=[tqk, ("qk", q, hl)])
            pt0, tt0 = pc()
            for c in range(4):
                cs = slice(c * 128, (c + 1) * 128)
                _mmr(P, pt0[:, cs], A[0][:, c, :], identr[:], True, True, r=[("A", 0), "identr"], w=[tt0])
            P.op("act", lambda e, pt0=pt0: e.activation(out=Bm[0][:], in_=c4(pt0[:, :]), func=AF.Copy), w=[tt0, ("B", 0)])
            P.op("dve", lambda e: e.scalar_tensor_tensor(out=R[:], in0=A[0][:], scalar=-1.0, in1=bc4(g.identf), op0=ALU.mult, op1=ALU.add),
                 r=[("A", 0)], w=["R"])
            for k in range(1, 7):
                ap_, bp_ = (k - 1) % 2, (k - 1) % 2
                an_, bn_ = k % 2, k % 2
                if k <= 5:
                    px, tx = pc()
                    _fill(P, g, px, tx)
                    for c in range(4):
                        cs = slice(c * 128, (c + 1) * 128)
                        _mmr(P, px[:, cs], Bm[bp_][:, c, :], A[ap_][:, c, :], True, True, r=[("B", bp_), ("A", ap_)], w=[tx])
                if k <= 5:
                    P.op("act", lambda e, px=px, an_=an_: e.activation(out=A[an_][:], in_=c4(px[:, :]), func=AF.Copy), w=[tx, ("A", an_)])
                    py, ty = pc()
                    _fill(P, g, py, ty)
                    for c in range(4):
                        cs = slice(c * 128, (c + 1) * 128)
                        P.op("pe", lambda e, py=py, cs=cs, c=c, an_=an_: e.transpose(out=py[:, cs], in_=A[an_][:, c, :], identity=g.identf),
                             r=[("A", an_)], w=[ty])
                else:
                    py, ty = pc()
                    _fill(P, g, py, ty)
                    for c in range(4):
                        cs = slice(c * 128, (c + 1) * 128)
                        _mmr(P, py[:, cs], A[ap_][:, c, :], Bm[bp_][:, c, :], True, True, r=[("B", bp_), ("A", ap_)], w=[ty])
                P.op("act", lambda e, py=py, bn_=bn_: e.activation(out=Bm[bn_][:], in_=c4(py[:, :]), func=AF.Copy), w=[ty, ("B", bn_)])
                pz, tz = pc()
                _fill(P, g, pz, tz)
                for c in range(4):
                    cs = slice(c * 128, (c + 1) * 128)
                    _mmr(P, pz[:, cs], Bm[bn_][:, c, :], R[:, c, :], True, True, r=[("B", bn_), "R"], w=[tz])
                P.op("dve", lambda e, pz=pz: e.tensor_tensor(out=R[:], in0=c4(pz[:, :]), in1=R[:], op=ALU.add), w=[tz, "R"])
            pw, tw = pc()
            _fill(P, g, pw, tw)
            for c in range(4):
                cs = slice(c * 128, (c + 1) * 128)
                _mmr(P, pw[:, cs], kbg[:, c, :], R[:, c, :], True, True, r=["kbg", "R"], w=[tw])
            P.op("act", lambda e, pw=pw: e.activation(out=wT4[q][:, hl, :], in_=pw[:, :], func=AF.Copy), w=[tw, ("wT", q, hl)])
            pu, tu = pc()
            _fill(P, g, pu, tu)
            for c in range(4):
                cs = slice(c * 128, (c + 1) * 128)
                _mmr(P, pu[:, cs], R[:, c, :], vb[:, c, :], True, True, r=["vb", "R"], w=[tu])
            P.op("act", lambda e, pu=pu: e.activation(out=u4[q][:, hl, :, :], in_=c4(pu[:, :]), func=AF.Copy), w=[tu, ("u4", q, hl)])

        def scan_step(t, hp, c):
            q = t % 2
            vq = c % 2
            cs = slice(c * 128, (c + 1) * 128)
            for hl in range(HG):
                _mm(P, PSW[:, hl * 128:(hl + 1) * 128], wT4[q][:, hl, cs], Sb[:, hl, :], True, True, r=[("wT", q, hl), "Sb"], w=["PSW"])
            P.op("dve", lambda e: e.tensor_tensor(out=vnew[vq][:], in0=u4[q][:, :, c, :], in1=PSW[:, :].rearrange("p (h e) -> p h e", h=HG),
                                                  op=ALU.subtract),
                 r=[("u4", q, hl) for hl in range(HG)], w=["PSW", ("vnew", vq)])
            for hl in range(HG):
                _mm(P, PSO[:, hl * 128:(hl + 1) * 128], Sb[:, hl, :], qgT[q][:, hl, cs], True, False, r=[("qg", q, hl), "Sb"], w=["PSO"])
                _mm(P, PSO[:, hl * 128:(hl + 1) * 128], vnew[vq][:, hl, :], qkT4[q][:, hl, cs], False, True,
                    r=[("qk", q, hl), ("vnew", vq)], w=["PSO"])
            for hl in range(HG):
                _mm(P, PSD[:, hl * 128:(hl + 1) * 128], kdec[q][:, hl, c, :], vnew[vq][:, hl, :], True, True,
                    r=[("kdec", q, hl), ("vnew", vq)], w=["PSD"])
            for hl in range(HG):
                P.op("dve", lambda e, hl=hl: e.scalar_tensor_tensor(out=Sf[:, hl, :], in0=Sf[:, hl, :], scalar=egl[q][:, c, hl:hl + 1],
                                                                   in1=PSD[:, hl * 128:(hl + 1) * 128], op0=ALU.mult, op1=ALU.add),
                     r=[("egl", q, hl)], w=["PSD", "Sf"])
            P.op("pool", lambda e: e.tensor_copy(out=Sb[:], in_=Sf[:]), r=["Sf"], w=["Sb"])
            P.op("act", lambda e: e.activation(out=oraw[:, :, cs], in_=PSO[:, :].rearrange("p (h i) -> p h i", h=HG), func=AF.Copy),
                 w=["PSO", "oraw"])

        def post(t, hp, hl):
            h = hp * HG + hl
            q = t % 2
            P.op("act", lambda e: e.activation(out=sqo[:], in_=oraw[:, hl, :], func=AF.Square), r=["oraw"], w=["sqo"])
            _mm(P, PB[:, :], g.onesb, sqo[:], True, True, r=["sqo"], w=["PB"])
            P.op("act", lambda e: e.activation(out=sro[:], in_=PB[:, :], func=AF.Ln, scale=1.0 / 128.0, bias=EPS), w=["PB", "sro"])
            P.op("act", lambda e: e.activation(out=rinvo[:], in_=sro[:], func=AF.Exp, scale=-0.5), r=["sro"], w=["rinvo"])
            for kc in range(KC):
                _mm(P, PA[:, :], wz[:, kc, hl * 128:(hl + 1) * 128], ut[q][:, kc, :], kc == 0, kc == KC - 1, r=[("wz", kc), ("ut", q)], w=["PA"])
            P.op("act", lambda e: e.activation(out=sz[:], in_=PA[:, :], func=AF.Silu), w=["PA", "sz"])
            P.op("dve", lambda e: e.tensor_tensor(out=on[:], in0=oraw[:, hl, :], in1=rinvo[:], op=ALU.mult), r=["oraw", "rinvo"], w=["on"])
            ob_ = oab[hl % 2]
            P.op("dve", lambda e: e.scalar_tensor_tensor(out=ob_[:], in0=on[:], scalar=dnncol, in1=sz[:], op0=ALU.mult, op1=ALU.mult),
                 r=["on", "sz"], w=[("oab", hl % 2)])
            P.op("sp", lambda e: e.dma_start(out=g.oaT[h * 128:(h + 1) * 128, t * T:(t + 1) * T], in_=ob_[:]), r=[("oab", hl % 2)], dma=True)

        load_w(P, wba, w_in, KC, "wba_", cols=(O_B, O_B + 16))
        P.op("pool", lambda e: e.memset(g1[:], 0.0), r=[("wba_", kc) for kc in range(KC)], w=["wba"])
        for hp in range(NPASS):
            v3 = w_in.rearrange("(kc p) n -> p kc n", p=128)
            for part in range(3):
                for kc in range(KC):
                    c0 = part * 1024 + hp * HG * 128
                    P.op("pool", lambda e, part=part, kc=kc, c0=c0: e.dma_start(out=wqkv[:, kc, part * HG * 128:(part + 1) * HG * 128],
                                                                              in_=v3[:, kc, c0:c0 + HG * 128]), w=[("wqkv", kc)], dma=True)
            load_w(P, wz, w_in, KC, "wz", cols=(O_Z + hp * HG * 128, O_Z + (hp + 1) * HG * 128))
            P.op("pool", lambda e: e.memset(halo[:], 0.0), w=[("halo", i) for i in range(3 * HG)])
            P.op("pool", lambda e: e.memset(Sf[:], 0.0), w=["Sf"])
            P.op("pool", lambda e: e.memset(Sb[:], 0.0), w=["Sb"])
            gates(0)
            for hl in range(HG):
                pre(0, hp, hl)
            for t in range(NT):
                if t + 1 < NT:
                    gates(t + 1)
                for c in range(4):
                    scan_step(t, hp, c)
                    if t + 1 < NT:
                        pre(t + 1, hp, c)
                for hl in range(HG):
                    post(t, hp, hl)
        P.emit()


def host_consts():
    cf = np.zeros((128, 2176), np.float32)
    j = np.arange(128)[:, None]
    i = np.arange(128)[None, :]
    cf[:, 0:128] = np.eye(128, dtype=np.float32)
    cf[:, 128:256] = np.where(j <= i, 0.0, NEG)
    cf[:, 256:384] = (j < i).astype(np.float32)
    k = j
    q = i
    prev = np.where(q <= k, 128.0 + q - k, BIGD)
    cur = np.where(q >= k, (q - k) * 1.0, BIGD)
    cf[:, 384:512] = prev
    cf[:, 512:640] = cur
    for h in range(8):
        cf[h, 640 + h * 128: 640 + (h + 1) * 128] = 1.0
    rm = np.ones((512,), np.float32)
    rm[0::128] = 0.0
    cf[0:8, 1664:2176] = rm[None, :]
    cb = np.zeros((128, 512), np.float32)
    cb[:, 0:128] = np.eye(128, dtype=np.float32)
    cb[:, 128:256] = 1.0
    cb[0:64, 256:384] = 1.0
    cb[64:128, 384:512] = 1.0
    return cf, cb


def make_in_maps(inputs, ncores=8):
    f = lambda a: np.ascontiguousarray(np.asarray(a, dtype=np.float32))
    cf, cb = host_consts()
    shared = {}
    shared["ada_w"] = f(inputs["ada_w"])
    shared["ada_bT"] = f(np.asarray(inputs["ada_b"]).reshape(DEPTH, 72, 128).transpose(2, 0, 1).reshape(128, DEPTH * 72))
    ln = np.stack([np.asarray(inputs["ln_ffn1"]), np.asarray(inputs["ln_mix"]), np.asarray(inputs["ln_ffn2"])], axis=1)
    lnT = ln.reshape(DEPTH, 3, 8, 128).transpose(3, 0, 1, 2).reshape(128, DEPTH * 24)
    fn = np.asarray(inputs["final_norm"]).reshape(8, 128).T
    shared["lnT"] = f(np.concatenate([lnT, fn], axis=1))
    cw = np.asarray(inputs["conv_w"])
    shared["conv_wT"] = f(cw.reshape(DEPTH, 4, 24, 128).transpose(3, 0, 2, 1).reshape(128, DEPTH * 96))
    shared["dnnT"] = f(np.asarray(inputs["dn_norm"]).T)
    shared["alog"] = f(np.asarray(inputs["a_log"]).T)
    shared["dtb"] = f(np.asarray(inputs["dt_bias"]).T)
    for nm in ("ffn1_wg", "ffn1_wu", "ffn1_wd", "ffn2_wg", "ffn2_wu", "ffn2_wd", "w_in", "w_a", "w_b", "w_o"):
        shared[nm] = f(inputs[nm])
    shared["cf"] = cf
    shared["cb"] = cb
    x = np.asarray(inputs["x"], dtype=np.float32)
    c = np.asarray(inputs["c"], dtype=np.float32)
    maps = []
    for b in range(ncores):
        m = dict(shared)
        m["xT"] = np.ascontiguousarray(x[b].T)
        m["c_col"] = np.ascontiguousarray(c[b].reshape(8, 128).T)
        maps.append(m)
    return maps


def kernel(**inputs):
    nc = bass.Bass("TRN2", target_bir_lowering=False)
    build(nc)
    maps = make_in_maps(inputs, 8)
    res = run_bass_kernel_spmd(nc, maps, core_ids=list(range(8)))
    out = np.stack([np.ascontiguousarray(r["outT"].T) for r in res.results], axis=0)
    return out.astype(np.float32)
```
